# Optimizing a Trainium2 kernel written in Bass

```python
import math
import jax, jax.numpy as jnp
from jax import lax
import numpy as np

D_MODEL = 2048
BATCH = 4
SEQ = 2048
DEPTH = 1
DEC_BATCH = 8
DEC_SEQ = 8
PAST_LEN = 16384
PAGE_SIZE = 128

D_MIX = D_MODEL
D_A = D_MIX // 2
D_B = D_MIX - D_A
HEAD_DIM_A = 128
N_HEADS_A = D_A // HEAD_DIM_A
HEAD_DIM_B = 128
N_HEADS_B = D_B // HEAD_DIM_B
D_IN = 4 * D_A + 3 * D_B
SPLITS = [D_A, 2 * D_A, 3 * D_A, 4 * D_A, 4 * D_A + D_B, 4 * D_A + 2 * D_B]
D_FF = 5632
GLA_CHUNK = 32
Q_BLOCK = 128
N_MOD = 9
SB_BIAS_INIT = -9.0
EPS = 1e-6

kernel_name = "hymba_hgrn2_stickbreak_macaron_adaln_step"


def rmsnorm(x, g):
    xf = x.astype(jnp.float32)
    y = xf * lax.rsqrt(jnp.mean(xf * xf, axis=-1, keepdims=True) + EPS)
    return (y * g.astype(jnp.float32)).astype(x.dtype)


def head_rmsnorm(o, g):
    of = o.astype(jnp.float32)
    y = of * lax.rsqrt(jnp.mean(of * of, axis=-1, keepdims=True) + EPS)
    return y.reshape(o.shape[0], o.shape[1], -1) * g.astype(jnp.float32)


def modulate(h, shift, scale):
    return h * (1.0 + scale[:, None, :]) + shift[:, None, :]


def swiglu(h, w_gate, w_up, w_down):
    return (jax.nn.silu(h @ w_gate) * (h @ w_up)) @ w_down


def forget_lower_bounds(lb_logits):
    logits = jnp.concatenate([lb_logits.astype(jnp.float32), jnp.zeros((1, lb_logits.shape[1]), jnp.float32)], axis=0)
    return jnp.cumsum(jax.nn.softmax(logits, axis=0), axis=0)[:lb_logits.shape[0]]


def gated_linear_recurrence(q, k, v, logf, s0, chunk):
    B, T, H, DK = q.shape
    DV = v.shape[-1]
    n = T // chunk

    def to_chunks(a):
        return a.reshape(B, n, chunk, H, a.shape[-1]).transpose(1, 0, 2, 3, 4)

    causal = jnp.tril(jnp.ones((chunk, chunk), dtype=bool))

    def step(s, xs):
        qi, ki, vi, fi = xs
        b = jnp.cumsum(fi, axis=1)
        b_last = b[:, -1:]
        q_dec = qi * jnp.exp(b)
        k_inv = ki * jnp.exp(-b)
        k_end = ki * jnp.exp(b_last - b)
        a = jnp.einsum('bthk,bshk->bhts', q_dec, k_inv)
        a = jnp.where(causal, a, 0.0)
        o = jnp.einsum('bhts,bshv->bthv', a, vi) + jnp.einsum('bthk,bhkv->bthv', q_dec, s)
        s = jnp.exp(b_last[:, 0])[..., None] * s + jnp.einsum('bshk,bshv->bhkv', k_end, vi)
        return s, o

    s_fin, o = lax.scan(step, s0, (to_chunks(q), to_chunks(k), to_chunks(v), to_chunks(logf)))
    return o.transpose(1, 0, 2, 3, 4).reshape(B, T, H, DV), s_fin


def stick_breaking_block(q, k_segs, v_segs, q_pos, k_pos, bias):
    qf = q.astype(jnp.float32) * (HEAD_DIM_B ** -0.5)
    z = jnp.concatenate([jnp.einsum('bqhd,bkhd->bhqk', qf, kk.astype(jnp.float32)) for kk in k_segs], axis=-1)
    z = z + bias.astype(jnp.float32)[None, :, None, None]
    visible = k_pos[None, :] < q_pos[:, None]
    log1mb = jnp.where(visible, jax.nn.log_sigmoid(-z), 0.0)
    rev = lax.cumsum(log1mb, axis=3, reverse=True)
    a = jnp.exp(jnp.where(visible, z + rev, -jnp.inf))
    out = 0.0
    off = 0
    for vv in v_segs:
        L = vv.shape[1]
        out = out + jnp.einsum('bhqk,bkhd->bqhd', a[..., off:off + L], vv.astype(jnp.float32))
        off += L
    return out


def stick_breaking(q, k, v, k_past, v_past, bias):
    Tq = q.shape[1]
    past = 0 if k_past is None else k_past.shape[1]
    pos = jnp.arange(past + Tq, dtype=jnp.int32)
    outs = []
    for start in range(0, Tq, Q_BLOCK):
        end = min(start + Q_BLOCK, Tq)
        k_segs = ([] if k_past is None else [k_past]) + [k[:, :end]]
        v_segs = ([] if v_past is None else [v_past]) + [v[:, :end]]
        outs.append(stick_breaking_block(q[:, start:end], k_segs, v_segs,
                                         pos[past + start:past + end], pos[:past + end], bias))
    return jnp.concatenate(outs, axis=1)


def token_mixer(h, lb, s0, k_past, v_past, chunk, w_in, g_out_a, g_out_b, b_sb, w_out):
    B, T, _ = h.shape
    proj = h @ w_in
    q_a, f_a, i_a, g_a, q_b, k_b, v_b = jnp.split(proj, SPLITS, axis=-1)

    def heads_a(t):
        return t.reshape(B, T, N_HEADS_A, HEAD_DIM_A)

    def heads_b(t):
        return t.reshape(B, T, N_HEADS_B, HEAD_DIM_B)

    f = lb + (1.0 - lb) * jax.nn.sigmoid(f_a.astype(jnp.float32))
    o_a, s_fin = gated_linear_recurrence(
        heads_a(q_a.astype(jnp.float32)) * (HEAD_DIM_A ** -0.5),
        heads_a(1.0 - f), heads_a(i_a.astype(jnp.float32)), heads_a(jnp.log(f)),
        s0.astype(jnp.float32), chunk)
    o_a = head_rmsnorm(o_a, g_out_a).astype(h.dtype) * jax.nn.silu(g_a)

    kh, vh = heads_b(k_b), heads_b(v_b)
    o_b = head_rmsnorm(stick_breaking(heads_b(q_b), kh, vh, k_past, v_past, b_sb), g_out_b).astype(h.dtype)

    out = jnp.concatenate([o_a, o_b], axis=-1) @ w_out
    return out, s_fin.astype(s0.dtype), kh, vh


def decoder_layer(x, c, lb, s0, k_past, v_past, chunk,
                  norm_ffn1, norm_mix, norm_ffn2, w_mod, b_mod,
                  w_ffn1_gate, w_ffn1_up, w_ffn1_down,
                  w_in, g_out_a, g_out_b, b_sb, w_out,
                  w_ffn2_gate, w_ffn2_up, w_ffn2_down):
    mod = jax.nn.silu(c) @ w_mod + b_mod
    sh1, sc1, ga1, sh2, sc2, ga2, sh3, sc3, ga3 = jnp.split(mod, N_MOD, axis=-1)
    h = modulate(rmsnorm(x, norm_ffn1), sh1, sc1)
    x = x + 0.5 * ga1[:, None, :] * swiglu(h, w_ffn1_gate, w_ffn1_up, w_ffn1_down)
    h = modulate(rmsnorm(x, norm_mix), sh2, sc2)
    m, s_fin, k_new, v_new = token_mixer(h, lb, s0, k_past, v_past, chunk, w_in, g_out_a, g_out_b, b_sb, w_out)
    x = x + ga2[:, None, :] * m
    h = modulate(rmsnorm(x, norm_ffn2), sh3, sc3)
    x = x + 0.5 * ga3[:, None, :] * swiglu(h, w_ffn2_gate, w_ffn2_up, w_ffn2_down)
    return x, s_fin, k_new, v_new


def final_norm(x, c, norm_final, w_final_mod, b_final_mod):
    shift, scale = jnp.split(jax.nn.silu(c) @ w_final_mod + b_final_mod, 2, axis=-1)
    return modulate(rmsnorm(x, norm_final), shift, scale)


def setup_inputs(seed: int = 0) -> dict:
    key = jax.random.key(seed)
    it = iter(jax.random.split(key, 40))

    def nrm(shape, std):
        return std * jax.random.normal(next(it), shape, dtype=jnp.float32)

    n_pages = PAST_LEN // PAGE_SIZE
    n_used = DEC_BATCH * n_pages
    n_phys = n_used + max(1, n_used // 4)
    page_table = jax.random.permutation(next(it), n_phys)[:n_used].reshape(DEC_BATCH, n_pages).astype(jnp.int32)
    sd = D_MODEL ** -0.5
    return {
        "x_prompt": nrm((BATCH, SEQ, D_MODEL), 1.0),
        "x_sample": nrm((DEC_BATCH, DEC_SEQ, D_MODEL), 1.0),
        "cache_k": nrm((DEPTH, n_phys, PAGE_SIZE, N_HEADS_B, HEAD_DIM_B), 1.0),
        "cache_v": nrm((DEPTH, n_phys, PAGE_SIZE, N_HEADS_B, HEAD_DIM_B), 1.0),
        "state_hgrn": nrm((DEPTH, DEC_BATCH, N_HEADS_A, HEAD_DIM_A, HEAD_DIM_A), 0.4),
        "page_table": page_table,
        "c_prompt": nrm((BATCH, D_MODEL), 1.0),
        "c_sample": nrm((DEC_BATCH, D_MODEL), 1.0),
        "lb_logits": nrm((DEPTH, D_A), 0.1),
        "norm_ffn1": 1.0 + nrm((DEPTH, D_MODEL), 0.02),
        "norm_mix": 1.0 + nrm((DEPTH, D_MODEL), 0.02),
        "norm_ffn2": 1.0 + nrm((DEPTH, D_MODEL), 0.02),
        "w_mod": nrm((DEPTH, D_MODEL, N_MOD * D_MODEL), 0.5 * sd),
        "b_mod": nrm((DEPTH, N_MOD * D_MODEL), 0.02),
        "w_ffn1_gate": nrm((DEPTH, D_MODEL, D_FF), sd),
        "w_ffn1_up": nrm((DEPTH, D_MODEL, D_FF), sd),
        "w_ffn1_down": nrm((DEPTH, D_FF, D_MODEL), D_FF ** -0.5),
        "w_in": nrm((DEPTH, D_MODEL, D_IN), sd),
        "g_out_a": 1.0 + nrm((DEPTH, D_A), 0.02),
        "g_out_b": 1.0 + nrm((DEPTH, D_B), 0.02),
        "b_sb": SB_BIAS_INIT + nrm((DEPTH, N_HEADS_B), 0.1),
        "w_out": nrm((DEPTH, D_MIX, D_MODEL), D_MIX ** -0.5),
        "w_ffn2_gate": nrm((DEPTH, D_MODEL, D_FF), sd),
        "w_ffn2_up": nrm((DEPTH, D_MODEL, D_FF), sd),
        "w_ffn2_down": nrm((DEPTH, D_FF, D_MODEL), D_FF ** -0.5),
        "norm_final": 1.0 + nrm((D_MODEL,), 0.02),
        "w_final_mod": nrm((D_MODEL, 2 * D_MODEL), 0.5 * sd),
        "b_final_mod": nrm((2 * D_MODEL,), 0.02),
    }


def reference(x_prompt, x_sample, cache_k, cache_v, state_hgrn, page_table, c_prompt, c_sample,
              lb_logits, norm_ffn1, norm_mix, norm_ffn2, w_mod, b_mod,
              w_ffn1_gate, w_ffn1_up, w_ffn1_down, w_in, g_out_a, g_out_b, b_sb, w_out,
              w_ffn2_gate, w_ffn2_up, w_ffn2_down, norm_final, w_final_mod, b_final_mod):
    lbs = forget_lower_bounds(lb_logits)
    xp, xs = x_prompt, x_sample
    n_dec = x_sample.shape[0]
    s0_prompt = jnp.zeros((x_prompt.shape[0], N_HEADS_A, HEAD_DIM_A, HEAD_DIM_A), x_prompt.dtype)
    chunk_prompt = min(GLA_CHUNK, x_prompt.shape[1])
    chunk_sample = x_sample.shape[1]
    kp_l, vp_l, sp_l, ks_l, vs_l, ss_l = [], [], [], [], [], []
    for l in range(DEPTH):
        lw = (norm_ffn1[l], norm_mix[l], norm_ffn2[l], w_mod[l], b_mod[l],
              w_ffn1_gate[l], w_ffn1_up[l], w_ffn1_down[l],
              w_in[l], g_out_a[l], g_out_b[l], b_sb[l], w_out[l],
              w_ffn2_gate[l], w_ffn2_up[l], w_ffn2_down[l])
        xp, sp, kp, vp = decoder_layer(xp, c_prompt, lbs[l], s0_prompt, None, None, chunk_prompt, *lw)
        k_past = jnp.take(cache_k[l], page_table, axis=0).reshape(n_dec, -1, N_HEADS_B, HEAD_DIM_B)
        v_past = jnp.take(cache_v[l], page_table, axis=0).reshape(n_dec, -1, N_HEADS_B, HEAD_DIM_B)
        xs, ss, ks, vs = decoder_layer(xs, c_sample, lbs[l], state_hgrn[l], k_past, v_past, chunk_sample, *lw)
        kp_l.append(kp); vp_l.append(vp); sp_l.append(sp)
        ks_l.append(ks); vs_l.append(vs); ss_l.append(ss)
    y_prompt = final_norm(xp, c_prompt, norm_final, w_final_mod, b_final_mod)
    y_sample = final_norm(xs, c_sample, norm_final, w_final_mod, b_final_mod)
    k_prompt = jnp.stack(kp_l)
    v_prompt = jnp.stack(vp_l)
    k_sample = jnp.stack(ks_l)
    v_sample = jnp.stack(vs_l)
    s_prompt = jnp.stack(sp_l)
    s_sample = jnp.stack(ss_l)
    return (y_prompt, y_sample, k_prompt, v_prompt, k_sample, v_sample, s_prompt, s_sample)
```

```python
import os
from contextlib import ExitStack
import numpy as np
import concourse.bass as bass
import concourse.mybir as mybir
from concourse.bass_utils import run_bass_kernel_spmd

F32 = mybir.dt.float32
BF16 = mybir.dt.bfloat16
I32 = mybir.dt.int32
AF = mybir.ActivationFunctionType
ALU = mybir.AluOpType

D = 2048
DFF = 5632
NFF = DFF // 128
DIN = 7168
T = 1024
NTB = T // 128
GC = 4
EPS = 1e-6
NCORES = 8
NPAGES = 128
NPHYS = 1280

R_CP, R_CS, R_N1, R_N2, R_N3, R_NF, R_LB, R_GA, R_GB, R_BM = 0, 16, 32, 48, 64, 80, 96, 104, 112, 128


class Sched:
    ENG = ("pe", "act", "dve", "pool", "sp")

    def __init__(self, nc, es, n_dma_slots=6):
        self.nc = nc
        self.ops = {e: [] for e in self.ENG}
        self.cnt = {e: 0 for e in self.ENG}
        self.lastw = {}
        self.readers = {}
        self.sems = {}
        for e in ("pe", "act", "dve", "pool"):
            self.sems[e] = es.enter_context(nc.semaphore("sem_" + e))
        self.slots = {}
        self.slot_cnt = {}
        self.slot_rr = {}
        for q in ("sp", "pool"):
            self.slots[q] = []
            for j in range(n_dma_slots):
                nm = "dq_%s_%d" % (q, j)
                self.sems[nm] = es.enter_context(nc.semaphore(nm))
                self.slots[q].append(nm)
                self.slot_cnt[nm] = 0
            self.slot_rr[q] = 0
        self.nops = 0

    def _deps(self, r, w):
        deps = set()
        for k in r:
            if k in self.lastw:
                deps.add(self.lastw[k])
        for k in w:
            if k in self.lastw:
                deps.add(self.lastw[k])
            for t in self.readers.get(k, ()):
                deps.add(t)
        return deps

    def _commit(self, tok, r, w):
        for k in w:
            self.lastw[k] = tok
            self.readers[k] = []
        for k in r:
            if k not in w:
                self.readers.setdefault(k, []).append(tok)

    def op(self, eng, fn, r=(), w=()):
        deps = self._deps(r, w)
        if eng == "pe":
            deps = {d for d in deps if d[0] != "pe"}
        self.cnt[eng] += 1
        tok = (eng, self.cnt[eng])
        self.ops[eng].append((fn, deps, tok, 1))
        self._commit(tok, r, w)
        self.nops += 1

    def dma(self, q, fn, r=(), w=()):
        deps = self._deps(r, w)
        j = self.slot_rr[q]
        self.slot_rr[q] = (j + 1) % len(self.slots[q])
        nm = self.slots[q][j]
        if self.slot_cnt[nm] > 0:
            deps.add((nm, 16 * self.slot_cnt[nm]))
        self.slot_cnt[nm] += 1
        tok = (nm, 16 * self.slot_cnt[nm])
        self.ops[q].append((fn, deps, tok, 16))
        self._commit(tok, r, w)
        self.nops += 1

    def barrier(self):
        toks = [(e, self.cnt[e]) for e in ("pe", "act", "dve", "pool") if self.cnt[e] > 0] + self.all_dma_tokens()
        for e in self.ENG:
            self.ops[e].append((None, set(toks), None, 0))
        self.lastw = {}
        self.readers = {}

    def final_waits(self, eng, toks):
        self.ops[eng].append((None, set(toks), None, 0))

    def all_dma_tokens(self):
        return [(nm, 16 * c) for nm, c in self.slot_cnt.items() if c > 0]

    def emit(self, block):
        sems = self.sems

        def mk(engname):
            def body(e):
                seen = {}
                for fn, deps, tok, inc in self.ops[engname]:
                    for (s, v) in sorted(deps):
                        if seen.get(s, 0) < v:
                            e.wait_ge(sems[s], v)
                            seen[s] = v
                    if fn is None:
                        continue
                    ins = fn(e)
                    ins.then_inc(sems[tok[0]], inc)
            return body

        block.tensor(mk("pe"))
        block.scalar(mk("act"))
        block.vector(mk("dve"))
        block.gpsimd(mk("pool"))
        block.sync(mk("sp"))


def build_program(stage_limit=99):
    nc = bass.Bass("TRN2", target_bir_lowering=False)
    dt = nc.dram_tensor
    x_all = dt("x_all", [2 * T, D], F32, kind="ExternalInput").ap()
    x_smp = dt("x_smp", [8, D], F32, kind="ExternalInput").ap()
    vecs = dt("vecs", [384, 128], F32, kind="ExternalInput").ap()
    w_mod = dt("w_mod", [D, 9 * D], F32, kind="ExternalInput").ap()
    w_fin = dt("w_fin", [D, 2 * D], F32, kind="ExternalInput").ap()
    w1g = dt("w1g", [D, DFF], F32, kind="ExternalInput").ap()
    w1u = dt("w1u", [D, DFF], F32, kind="ExternalInput").ap()
    w1d = dt("w1d", [DFF, D], F32, kind="ExternalInput").ap()
    w2g = dt("w2g", [D, DFF], F32, kind="ExternalInput").ap()
    w2u = dt("w2u", [D, DFF], F32, kind="ExternalInput").ap()
    w2d = dt("w2d", [DFF, D], F32, kind="ExternalInput").ap()
    w_in = dt("w_in", [D, DIN], F32, kind="ExternalInput").ap()
    w_out = dt("w_out", [D, D], F32, kind="ExternalInput").ap()
    y_p = dt("y_p", [T, D], F32, kind="ExternalOutput").ap()
    y_s = dt("y_s", [8, D], F32, kind="ExternalOutput").ap()
    k_p = dt("k_p", [T, 1024], F32, kind="ExternalOutput").ap()
    v_p = dt("v_p", [T, 1024], F32, kind="ExternalOutput").ap()
    k_s = dt("k_s", [8, 1024], F32, kind="ExternalOutput").ap()
    v_s = dt("v_s", [8, 1024], F32, kind="ExternalOutput").ap()
    s_p = dt("s_p", [8, 128, 128], F32, kind="ExternalOutput").ap()
    s_s = dt("s_s", [8, 128, 128], F32, kind="ExternalOutput").ap()
    st_in = dt("st_in", [8, 128, 128], F32, kind="ExternalInput").ap()
    cst = dt("cst", [128, 2304], F32, kind="ExternalInput").ap()
    b_sb = dt("b_sb", [1, 8], F32, kind="ExternalInput").ap()
    cache_k = dt("cache_k", [NPHYS * 128, 1024], F32, kind="ExternalInput").ap()
    cache_v = dt("cache_v", [NPHYS * 128, 1024], F32, kind="ExternalInput").ap()
    ptab = dt("ptab", [1, NPAGES], I32, kind="ExternalInput").ap()
    b64 = dt("b64", [64, 1], F32, kind="ExternalInput").ap()
    flag = dt("flag", [1, 1], F32, kind="ExternalInput").ap()
    cst2 = dt("cst2", [64, 1032], F32, kind="ExternalInput").ap()
    xsp = dt("xsp", [T, D], F32, kind="Internal").ap()
    kTp_d = dt("kTp_d", [8, 128, T], BF16, kind="Internal").ap()
    Vp_d = dt("Vp_d", [8, 128, T], BF16, kind="Internal").ap()

    es = ExitStack()
    with es:
        def sb(name, shape, dtype):
            return es.enter_context(nc.sbuf_tensor(name, shape, dtype))

        S = Sched(nc, es)
        U = sb("U", [128, NTB * D], F32)
        x_tok = U[:].rearrange("p (t d) -> p t d", t=NTB)
        xS = sb("xS", [128, D], F32)
        hT = sb("hT", [128, 16, T + 8], BF16)
        aT = sb("aT", [128, GC, T + 8], BF16)
        wgb = [sb("wgb%d" % i, [128, 16, 256], BF16) for i in range(2)]
        wub = [sb("wub%d" % i, [128, 16, 256], BF16) for i in range(2)]
        wdb = [sb("wdb%d" % i, [128, GC, 512], BF16) for i in range(2)]
        gP = sb("gP", [128, D], F32)
        gS = sb("gS", [128, D], F32)
        xn = sb("xn", [128, D], F32)
        sgt = [sb("sgt%d" % i, [128, 512], F32) for i in range(1)]
        dtmp = [sb("dtmp%d" % i, [128, 512], F32) for i in range(2)]
        vecs_sb = sb("vecs_sb", [128, 3, 128], F32)
        cols = sb("cols", [128, 384], F32)
        scT = sb("scT", [128, 16, 2], BF16)
        scT32 = sb("scT32", [128, 16, 2], F32)
        modc = sb("modc", [128, 176, 2], F32)
        effs = sb("effs", [128, 4, 16, 2], F32)
        ident = sb("ident", [128, 128], F32)
        ones_f = sb("ones_f", [128, 128], F32)
        diag = [sb("diag%d" % i, [128, 128], F32) for i in range(2)]
        ssq = sb("ssq", [128, 16], F32)
        rstd = sb("rstd", [128, 16], F32)
        iota_i = sb("iota_i", [128, 128], I32)
        iota_f = sb("iota_f", [128, 128], F32)
        pid_f = sb("pid_f", [128, 1], F32)
        Sst = sb("Sst", [128, 8, 128], F32)
        Sss = sb("Sss", [128, 8, 128], F32)
        lbc = sb("lbc", [128, 8], F32)
        oml = sb("oml", [128, 8], F32)
        kend_tok = sb("kend_tok", [128, 128], BF16)
        ntri = sb("ntri", [128, 128], BF16)
        nones = sb("nones", [128, 128], BF16)
        ones_b = sb("ones_b", [128, 128], BF16)
        mle = sb("mle", [128, 128], BF16)
        bsbc = sb("bsbc", [128, 8], F32)
        bsbp = sb("bsbp", [128, 8], F32)
        flagc = sb("flagc", [128, 1], F32)
        nbig = sb("nbig", [128, 1], F32)
        va_tok = sb("va_tok", [128, 128], BF16)
        ps = es.enter_context(nc.psum_tensor("ps", [128, 8, 512], F32))

        S.op("pool", lambda e: e.iota(iota_i[:], [[1, 128]], base=0, channel_multiplier=0), w=["iota_i"])
        S.op("pool", lambda e: e.tensor_copy(iota_f[:], iota_i[:]), r=["iota_i"], w=["iota_f"])
        S.op("pool", lambda e: e.iota(iota_i[:, 0:1], [[1, 1]], base=0, channel_multiplier=1), r=["iota_f"], w=["iota_i"])
        S.op("pool", lambda e: e.tensor_copy(pid_f[:], iota_i[:, 0:1]), r=["iota_i"], w=["pid_f"])
        S.op("dve", lambda e: e.tensor_scalar(ident[:], iota_f[:], pid_f[:, 0:1], None, ALU.is_equal),
             r=["iota_f", "pid_f"], w=["ident"])
        S.op("dve", lambda e: e.memset(ones_f[:], 1.0), w=["ones_f"])
        S.op("dve", lambda e: e.memset(ones_b[:], 1.0), w=["ones_b"])
        S.op("dve", lambda e: e.memset(nones[:], -1.0), w=["nones"])
        S.dma("pool", lambda e: e.dma_start(out=ntri[:], in_=cst[:, 0:128]), w=["ntri"])
        S.dma("pool", lambda e: e.dma_start(out=mle[:], in_=cst[:, 128:256]), w=["mle"])
        S.dma("sp", lambda e: e.dma_start(out=bsbc[:], in_=b_sb.partition_broadcast(128)), w=["bsbc"])
        S.dma("sp", lambda e: e.dma_start(out=flagc[:], in_=flag.partition_broadcast(128)), w=["flagc"])
        S.op("dve", lambda e: e.tensor_scalar(nbig[:], flagc[:], -1.0, 1.0e4, ALU.add, ALU.mult), r=["flagc"], w=["nbig"])
        S.op("dve", lambda e: e.tensor_scalar(bsbp[:], bsbc[:], nbig[:, 0:1], None, ALU.add), r=["nbig", "bsbc"], w=["bsbp"])

        S.dma("sp", lambda e: e.dma_start(out=vecs_sb[:], in_=vecs.rearrange("(g r) c -> r g c", g=3)), w=["vecs_sb"])
        for g in range(3):
            S.op("pe", lambda e, g=g: e.transpose(ps[:, 7, g * 128:(g + 1) * 128], vecs_sb[:, g, :], ident[:]),
                 r=["vecs_sb", "ident"], w=[("ps", 7)])
        S.op("act", lambda e: e.activation(cols[:], ps[:, 7, 0:384], AF.Copy), r=[("ps", 7)], w=["cols"])
        for r_ in range(2):
            S.op("act", lambda e, r_=r_: e.activation(scT[:, :, r_], cols[:, 16 * r_:16 * r_ + 16], AF.Silu),
                 r=["cols"], w=["scT"])
            S.op("act", lambda e, r_=r_: e.activation(scT32[:, :, r_], cols[:, 16 * r_:16 * r_ + 16], AF.Silu),
                 r=["cols"], w=["scT32"])

        SC_OFF = [16, 64, 112, 160]
        SH_OFF = [0, 48, 96, 144]
        GA_OFF = [32, 80, 128]
        G_ROW = [R_N1, R_N2, R_N3, R_NF]
        for n in range(24):
            src_ = w_mod[:, n * 256:(n + 1) * 256]
            bufs4 = [wgb[0], wgb[1], wub[0], wub[1]]
            keys4 = [("wgb", 0), ("wgb", 1), ("wub", 0), ("wub", 1)]
            buf, bk = bufs4[n % 4], keys4[n % 4]
            S.dma("pool", lambda e, buf=buf, src_=src_: e.dma_start(
                out=buf[:], in_=src_.rearrange("(k p) c -> p k c", p=128)), w=[bk])
            for cc in range(2):
                j = n * 2 + cc
                for k in range(16):
                    S.op("pe", lambda e, buf=buf, k=k, cc=cc, j=j: e.matmul(
                        ps[:, 6, 2 * j:2 * j + 2], buf[:, k, cc * 128:(cc + 1) * 128], scT[:, k, :],
                        start=(k == 0), stop=(k == 15)), r=[bk, "scT"], w=[("ps", 6)])
        S.op("dve", lambda e: e.tensor_tensor(
            modc[:, 0:48, :], ps[:, 6, 0:96].rearrange("p (j r) -> p j r", r=2),
            cols[:, R_BM:R_BM + 48].unsqueeze(2).to_broadcast([128, 48, 2]), ALU.add),
            r=[("ps", 6), "cols"], w=["modc"])

        def make_effs(n):
            for r_ in range(2):
                S.op("dve", lambda e, n=n, r_=r_: e.scalar_tensor_tensor(
                    effs[:, n, :, r_], modc[:, SC_OFF[n]:SC_OFF[n] + 16, r_], 1.0,
                    cols[:, G_ROW[n]:G_ROW[n] + 16], ALU.add, ALU.mult), r=["modc", "cols"], w=["effs"])
        make_effs(0)

        bg = []
        stg = xS[:].bitcast(BF16).rearrange("p (k c) -> p k c", k=16)

        def mod_group_task(n):
            def task():
                src_ = w_mod[:, n * 256:(n + 1) * 256] if n < 72 else w_fin[:, (n - 72) * 256:(n - 71) * 256]
                S.dma("pool", lambda e: e.dma_start(out=stg, in_=src_.rearrange("(k p) c -> p k c", p=128)), w=[("x", NTB)])
                for cc in range(2):
                    j = n * 2 + cc
                    for k in range(16):
                        S.op("pe", lambda e, k=k, cc=cc, j=j: e.matmul(
                            ps[:, 6, 2 * j:2 * j + 2], stg[:, k, cc * 128:(cc + 1) * 128], scT[:, k, :],
                            start=(k == 0), stop=(k == 15)), r=[("x", NTB), "scT"], w=[("ps", 6)])
            return task
        bg_all = [mod_group_task(n) for n in range(24, 88)]
        bg.extend(bg_all[:16])

        def finish_mod(part):
            while bg:
                bg.pop(0)()
            if part == 0:
                S.op("dve", lambda e: e.tensor_tensor(
                    modc[:, 48:80, :], ps[:, 6, 96:160].rearrange("p (j r) -> p j r", r=2),
                    cols[:, R_BM + 48:R_BM + 80].unsqueeze(2).to_broadcast([128, 32, 2]), ALU.add),
                    r=[("ps", 6), "cols"], w=["modc"])
                make_effs(1)
                bg.extend(bg_all[16:])
            else:
                S.op("dve", lambda e: e.tensor_tensor(
                    modc[:, 80:176, :], ps[:, 6, 160:352].rearrange("p (j r) -> p j r", r=2),
                    cols[:, R_BM + 80:R_BM + 176].unsqueeze(2).to_broadcast([128, 96, 2]), ALU.add),
                    r=[("ps", 6), "cols"], w=["modc"])
                make_effs(2)
                make_effs(3)

        def make_gate(m, mult):
            for r_, gt, gk in ((0, gP, "gP"), (1, gS, "gS")):
                for k in range(16):
                    dg = diag[k % 2]
                    S.op("dve", lambda e, dg=dg, k=k, r_=r_: e.tensor_scalar(
                        dg[:], ident[:], modc[:, GA_OFF[m] + k, r_:r_ + 1], mult, ALU.mult, ALU.mult),
                        r=["ident", "modc"], w=[("diag", k % 2)])
                    S.op("pe", lambda e, dg=dg, k=k: e.matmul(
                        ps[:, 4 + k // 4, (k % 4) * 128:(k % 4 + 1) * 128], ones_f[:], dg[:], start=True, stop=True),
                        r=[("diag", k % 2), "ones_f"], w=[("ps", 4 + k // 4)])
                for b in range(4):
                    S.op("act", lambda e, b=b, gt=gt: e.activation(gt[:, b * 512:(b + 1) * 512], ps[:, 4 + b, :], AF.Copy),
                         r=[("ps", 4 + b)], w=[gk])

        def tile_cols(with_sample):
            t = [(0, 512), (512, 512)]
            if with_sample:
                t.append((1024, 8))
            return t

        def tb_list(with_sample):
            l = [(tb, 128) for tb in range(NTB)]
            if with_sample:
                l.append((NTB, 8))
            return l

        def xrow(tb, n):
            return x_tok[:, tb, :] if tb < NTB else xS[0:n, :]

        def norm_to_hT(nidx, with_sample):
            for tb, n in tb_list(with_sample):
                xin = xrow(tb, n)
                r_ = 0 if tb < NTB else 1
                S.op("act", lambda e, xin=xin, n=n, tb=tb: e.activation(
                    xn[0:n, :], xin, AF.Square, accum_out=ssq[0:n, tb:tb + 1]), r=[("x", tb)], w=["xn", ("ssq", tb)])
                S.op("act", lambda e, n=n, tb=tb: e.activation(
                    rstd[0:n, tb:tb + 1], ssq[0:n, tb:tb + 1], AF.Sqrt, bias=EPS, scale=1.0 / D),
                    r=[("ssq", tb)], w=[("rstd", tb)])
                S.op("dve", lambda e, n=n, tb=tb: e.reciprocal(rstd[0:n, tb:tb + 1], rstd[0:n, tb:tb + 1]),
                     r=[("rstd", tb)], w=[("rstd", tb)])
                S.op("dve", lambda e, xin=xin, n=n, tb=tb: e.tensor_scalar(
                    xn[0:n, :], xin, rstd[0:n, tb:tb + 1], None, ALU.mult), r=[("x", tb), ("rstd", tb)], w=["xn"])
                for kq in range(4):
                    bank = kq % 2
                    for kk in range(4):
                        k = kq * 4 + kk
                        S.op("pe", lambda e, k=k, kk=kk, n=n, bank=bank: e.transpose(
                            ps[:, bank, kk * 128:kk * 128 + n], xn[0:n, k * 128:(k + 1) * 128], ident[0:n, 0:n]),
                            r=["xn", "ident"], w=[("ps", bank)])
                    for kk in range(4):
                        k = kq * 4 + kk
                        S.op("act", lambda e, k=k, kk=kk, n=n, tb=tb, bank=bank, r_=r_: e.activation(
                            hT[:, k, tb * 128:tb * 128 + n], ps[:, bank, kk * 128:kk * 128 + n], AF.Identity,
                            bias=modc[:, SH_OFF[nidx] + k, r_:r_ + 1], scale=effs[:, nidx, k, r_:r_ + 1]),
                            r=[("ps", bank), "modc", "effs"], w=[("hT", tb)])

        def ffn(wg, wu, wd, with_sample):
            tiles = tile_cols(with_sample)
            tbs = tb_list(with_sample)
            pair_i = 0
            dq = 0
            for g0 in range(0, NFF, GC):
                nch = min(GC, NFF - g0)
                for cp in range(0, nch, 2):
                    c0 = g0 + cp
                    bi = pair_i % 2
                    pair_i += 1
                    S.dma("pool", lambda e, bi=bi, c0=c0: e.dma_start(
                        out=wgb[bi][:], in_=wg[:, c0 * 128:c0 * 128 + 256].rearrange("(k p) c -> p k c", p=128)),
                        w=[("wgb", bi)])
                    S.dma("pool", lambda e, bi=bi, c0=c0: e.dma_start(
                        out=wub[bi][:], in_=wu[:, c0 * 128:c0 * 128 + 256].rearrange("(k p) c -> p k c", p=128)),
                        w=[("wub", bi)])
                    for cc in range(2):
                        ci = cp + cc
                        for ti, (t0, tn) in enumerate(tiles):
                            hkeys = [("hT", tb) for tb in range(t0 // 128, t0 // 128 + max(1, tn // 128))]
                            if bg:
                                bg.pop(0)()
                            bg_, bu = (0, 1) if (ci * 3 + ti) % 2 == 0 else (2, 3)
                            for k in range(16):
                                S.op("pe", lambda e, k=k, cc=cc, t0=t0, tn=tn, bg_=bg_, bi=bi: e.matmul(
                                    ps[:, bg_, 0:tn], wgb[bi][:, k, cc * 128:(cc + 1) * 128], hT[:, k, t0:t0 + tn],
                                    start=(k == 0), stop=(k == 15)), r=[("wgb", bi)] + hkeys, w=[("ps", bg_)])
                            for k in range(16):
                                S.op("pe", lambda e, k=k, cc=cc, t0=t0, tn=tn, bu=bu, bi=bi: e.matmul(
                                    ps[:, bu, 0:tn], wub[bi][:, k, cc * 128:(cc + 1) * 128], hT[:, k, t0:t0 + tn],
                                    start=(k == 0), stop=(k == 15)), r=[("wub", bi)] + hkeys, w=[("ps", bu)])
                            sg = sgt[0]
                            sgk = ("sgt", 0)
                            S.op("act", lambda e, sg=sg, tn=tn, bg_=bg_: e.activation(sg[:, 0:tn], ps[:, bg_, 0:tn], AF.Silu),
                                 r=[("ps", bg_)], w=[sgk])
                            S.op("dve", lambda e, sg=sg, tn=tn, bu=bu, ci=ci, t0=t0: e.tensor_tensor(
                                aT[:, ci, t0:t0 + tn], sg[:, 0:tn], ps[:, bu, 0:tn], ALU.mult),
                                r=[sgk, ("ps", bu)], w=[("aT", ci, ti)])
                for cg in range(4):
                    wi = dq % 2
                    dq += 1
                    S.dma("pool", lambda e, wi=wi, g0=g0, nch=nch, cg=cg: e.dma_start(
                        out=wdb[wi][:, 0:nch, :],
                        in_=wd[g0 * 128:(g0 + nch) * 128, cg * 512:(cg + 1) * 512].rearrange("(k p) c -> p k c", p=128)),
                        w=[("wdb", wi)])
                    for bi_, (tb, n) in enumerate(tbs):
                        bank = 4 + (cg * len(tbs) + bi_) % 2
                        ti = min(tb // 4, 2)
                        for ci in range(nch):
                            S.op("pe", lambda e, ci=ci, tb=tb, n=n, bank=bank, wi=wi: e.matmul(
                                ps[0:n, bank, :], aT[:, ci, tb * 128:tb * 128 + n], wdb[wi][:, ci, :],
                                start=(ci == 0), stop=(ci == nch - 1)),
                                r=[("aT", ci, ti), ("wdb", wi)], w=[("ps", bank)])
                        tmp = dtmp[bank - 4]
                        gt, gk = (gP, "gP") if tb < NTB else (gS, "gS")
                        S.op("dve", lambda e, tmp=tmp, n=n, bank=bank, gt=gt, cg=cg: e.tensor_tensor(
                            tmp[0:n, :], ps[0:n, bank, :], gt[0:n, cg * 512:(cg + 1) * 512], ALU.mult),
                            r=[("ps", bank), gk], w=[("dtmp", bank)])
                        xo = xrow(tb, n)[:, cg * 512:(cg + 1) * 512]
                        S.op("dve", lambda e, tmp=tmp, n=n, xo=xo: e.tensor_tensor(xo, xo, tmp[0:n, :], ALU.add),
                             r=[("dtmp", bank), ("x", tb)], w=[("x", tb)])

        S.op("act", lambda e: e.activation(lbc[:], cols[:, R_LB:R_LB + 8], AF.Sigmoid), r=["cols"], w=["lbc"])
        S.op("dve", lambda e: e.tensor_scalar(oml[:], lbc[:], -1.0, 1.0, ALU.mult, ALU.add), r=["lbc"], w=["oml"])
        S.op("dve", lambda e: e.memset(Sst[:], 0.0), w=["Sst"])
        S.dma("sp", lambda e: e.dma_start(out=Sss[:], in_=st_in.rearrange("h k v -> k h v")), w=["Sss"])

        def kv_proj(half, with_sample):
            tbs = tb_list(with_sample)
            bufs4 = [wgb[0], wgb[1], wub[0], wub[1]]
            keys4 = [("wgb", 0), ("wgb", 1), ("wub", 0), ("wub", 1)]
            for cgi in range(8):
                c0 = 5120 + cgi * 256
                buf, bk = bufs4[cgi % 4], keys4[cgi % 4]
                S.dma("pool", lambda e, buf=buf, c0=c0: e.dma_start(
                    out=buf[:], in_=w_in[:, c0:c0 + 256].rearrange("(k p) c -> p k c", p=128)), w=[bk])
                for bi_, (tb, n) in enumerate(tbs):
                    bank = 4 + (cgi * len(tbs) + bi_) % 2
                    for k in range(16):
                        S.op("pe", lambda e, k=k, tb=tb, n=n, bank=bank, buf=buf: e.matmul(
                            ps[0:n, bank, 0:256], hT[:, k, tb * 128:tb * 128 + n], buf[:, k, :],
                            start=(k == 0), stop=(k == 15)), r=[("hT", tb), bk], w=[("ps", bank)])
                    tmp = dtmp[bank - 4]
                    S.op("act", lambda e, tmp=tmp, n=n, bank=bank: e.activation(tmp[0:n, 0:256], ps[0:n, bank, 0:256], AF.Copy),
                         r=[("ps", bank)], w=[("dtmp", bank)])
                    isk = cgi < 4
                    cc0 = (cgi % 4) * 256
                    if tb < NTB:
                        dst = (k_p if isk else v_p)[tb * 128:(tb + 1) * 128, cc0:cc0 + 256]
                    else:
                        dst = (k_s if isk else v_s)[:, cc0:cc0 + 256]
                    S.dma("sp", lambda e, dst=dst, tmp=tmp, n=n: e.dma_start(out=dst, in_=tmp[0:n, 0:256]), r=[("dtmp", bank)])

        HS = 128.0 ** -0.5
        oT = U[:, 0:8256].bitcast(BF16).rearrange("p (k t) -> p k t", k=16)
        _uo = [8256]

        def ualloc_f(n):
            a_ = _uo[0]
            _uo[0] += n
            assert _uo[0] <= NTB * D
            return U[:, a_:a_ + n]

        def ualloc_b(ncols):
            n = (ncols + 1) // 2
            a_ = _uo[0]
            _uo[0] += n
            assert _uo[0] <= NTB * D
            return U[:, a_:a_ + n].bitcast(BF16)[:, 0:ncols]

        fbT, bbT, b2T, ebT, nbT, sgT_ = [ualloc_f(512) for _ in range(6)]
        qd, kiv = ualloc_b(512), ualloc_b(512)
        ke = ualloc_f(128)
        aTm = ualloc_b(128)
        osq = ualloc_b(512)
        orstd = ualloc_f(512)
        onb = ualloc_f(512)
        Sbf = ualloc_b(128)
        qT = ualloc_b(1032)
        kTo = ualloc_b(1032)
        kTp = ualloc_b(1024)
        Vo = ualloc_b(1024).rearrange("p (t d) -> p t d", t=8)
        Vp = ualloc_b(1024).rearrange("p (t d) -> p t d", t=8)
        e_s = [dtmp[0], dtmp[1]]
        lfl = sgt[0][:].bitcast(BF16)
        l_s = [lfl[:, 0:512], lfl[:, 512:1024]]
        aTf = aT[:].rearrange("p g t -> p (g t)")
        R32 = [aTf[:, 0:1024].bitcast(F32), aTf[:, 1024:2048].bitcast(F32)]
        Rbf = [aTf[:, 2048:2560], aTf[:, 2560:3072]]
        ATs = [aTf[:, 3072:3584], aTf[:, 3584:4096]]
        w0f = wdb[0][:].rearrange("p g t -> p (g t)")
        qpad = w0f[:, 0:512].rearrange("p (h m) -> p h m", h=8)
        kTn = w0f[:, 512:576].rearrange("p (h m) -> p h m", h=8)
        Vn = w0f[:, 576:1600]
        Mj = wdb[1]
        W7 = [wgb[0][:], wgb[1][:], wub[0][:], wub[1][:],
              gP[:].bitcast(BF16).rearrange("p (k c) -> p k c", k=16),
              gS[:].bitcast(BF16).rearrange("p (k c) -> p k c", k=16),
              xn[:].bitcast(BF16).rearrange("p (k c) -> p k c", k=16)]
        W7K = [("W7", i) for i in range(7)]
        TYPE_BASE = [0, 1024, 2048, 3072, 4096, 5120, 6144]

        def proj_fm(dst_bank, wi, cc, t0, tn):
            hkeys = [("hT", tb) for tb in range(t0 // 128, t0 // 128 + max(1, tn // 128))]
            for k in range(16):
                S.op("pe", lambda e, k=k: e.matmul(
                    ps[:, dst_bank, 0:tn], W7[wi][:, k, cc * 128:(cc + 1) * 128], hT[:, k, t0:t0 + tn],
                    start=(k == 0), stop=(k == 15)), r=[W7K[wi]] + hkeys, w=[("ps", dst_bank)])

        def proj_tm(dst_bank, c0, pkey, wi, cc, tb, n):
            for k in range(16):
                S.op("pe", lambda e, k=k: e.matmul(
                    ps[0:n, dst_bank, c0:c0 + 128], hT[:, k, tb * 128:tb * 128 + n], W7[wi][:, k, cc * 128:(cc + 1) * 128],
                    start=(k == 0), stop=(k == 15)), r=[W7K[wi], ("hT", tb)], w=[pkey])

        def head_norm_out(o_ap, okey, ss_ap, sskey, C, dst, gcol, gate, bufs):
            (osq_, osqk), (orstd_, orstdk), (onb_, onbk) = bufs
            S.op("act", lambda e: e.activation(osq_[:, 0:C], o_ap, AF.Square), r=[okey], w=[osqk])
            S.op("pe", lambda e: e.matmul(ss_ap, ones_b[:], osq_[:, 0:C], start=True, stop=True), r=["ones_b", osqk], w=[sskey])
            S.op("act", lambda e: e.activation(orstd_[:, 0:C], ss_ap, AF.Sqrt, bias=EPS, scale=1.0 / 128), r=[sskey], w=[orstdk])
            S.op("dve", lambda e: e.reciprocal(orstd_[:, 0:C], orstd_[:, 0:C]), r=[orstdk], w=[orstdk])
            S.op("dve", lambda e: e.tensor_tensor(onb_[:, 0:C], o_ap, orstd_[:, 0:C], ALU.mult), r=[okey, orstdk], w=[onbk])
            if gate is None:
                S.op("dve", lambda e: e.tensor_scalar(dst, onb_[:, 0:C], gcol, None, ALU.mult), r=[onbk, "cols"], w=["oT"])
            else:
                S.op("dve", lambda e: e.scalar_tensor_tensor(dst, onb_[:, 0:C], gcol, gate, ALU.mult, ALU.mult),
                     r=[onbk, "cols", "sgT"], w=["oT"])

        HN_BUFS = ((osq, "osq"), (orstd, "orstd"), (onb, "onb"))

        def hgrn_head(ws, h, cc, state_only=False, after_proj=None):
            tiles = tile_cols(ws)
            for ti, (t0, tn) in enumerate(tiles):
                C = min(128, tn)
                fb, bb, b2, eb, nb, sg = fbT[:, 0:tn], bbT[:, 0:tn], b2T[:, 0:tn], ebT[:, 0:tn], nbT[:, 0:tn], sgT_[:, 0:tn]
                proj_fm(4, 1, cc, t0, tn)
                S.op("act", lambda e, fb=fb, tn=tn: e.activation(fb, ps[:, 4, 0:tn], AF.Sigmoid), r=[("ps", 4)], w=["fbT"])
                S.op("dve", lambda e, fb=fb: e.tensor_scalar(fb, fb, oml[:, h:h + 1], lbc[:, h:h + 1], ALU.mult, ALU.add),
                     r=["fbT", "oml", "lbc"], w=["fbT"])
                S.op("act", lambda e, fb=fb, bb=bb: e.activation(bb, fb, AF.Ln), r=["fbT"], w=["bbT"])
                for c in range(max(1, tn // 128)):
                    S.op("dve", lambda e, c=c, C=C, bb=bb, b2=b2: e.tensor_tensor_scan(
                        b2[:, c * C:(c + 1) * C], ones_f[:, 0:C], bb[:, c * C:(c + 1) * C], 0.0, ALU.mult, ALU.add),
                        r=["bbT", "ones_f"], w=["b2T"])
                S.op("act", lambda e, b2=b2, eb=eb: e.activation(eb, b2, AF.Exp), r=["b2T"], w=["ebT"])
                S.op("act", lambda e, b2=b2, nb=nb: e.activation(nb, b2, AF.Exp, scale=-1.0), r=["b2T"], w=["nbT"])
                S.op("dve", lambda e, fb=fb: e.tensor_scalar(fb, fb, -1.0, 1.0, ALU.mult, ALU.add), r=["fbT"], w=["fbT"])
                S.op("dve", lambda e, fb=fb, nb=nb: e.tensor_tensor(nb, fb, nb, ALU.mult), r=["fbT", "nbT"], w=["nbT"])
                if not state_only:
                    S.op("dve", lambda e, nb=nb, tn=tn: e.tensor_copy(kiv[:, 0:tn], nb), r=["nbT"], w=["kiv"])
                    proj_fm(4, 0, cc, t0, tn)
                    S.op("dve", lambda e, eb=eb, tn=tn: e.scalar_tensor_tensor(
                        qd[:, 0:tn], ps[:, 4, 0:tn], HS, eb, ALU.mult, ALU.mult), r=[("ps", 4), "ebT"], w=["qd"])
                    proj_fm(4, 3, cc, t0, tn)
                    S.op("act", lambda e, sg=sg, tn=tn: e.activation(sg, ps[:, 4, 0:tn], AF.Silu), r=[("ps", 4)], w=["sgT"])
                if ti == len(tiles) - 1 and after_proj is not None:
                    pass
                yield
                for c in range(max(1, tn // 128)):
                    tb = t0 // 128 + c
                    St = Sst if tb < NTB else Sss
                    sk = "Sst" if tb < NTB else "Sss"
                    lo, hi = c * C, (c + 1) * C
                    last = hi - 1
                    S.op("dve", lambda e, nb=nb, eb=eb, lo=lo, hi=hi, last=last, C=C: e.tensor_scalar(
                        ke[:, 0:C], nb[:, lo:hi], eb[:, last:last + 1], None, ALU.mult), r=["nbT", "ebT"], w=["ke"])
                    S.op("pe", lambda e, C=C: e.transpose(ps[0:C, 5, 128:256], ke[:, 0:C], ident[:]),
                         r=["ke", "ident"], w=[("ps", 5)])
                    S.op("act", lambda e, C=C: e.activation(kend_tok[0:C, :], ps[0:C, 5, 128:256], AF.Copy),
                         r=[("ps", 5)], w=["kend_tok"])
                    proj_tm(5, 0, ("ps", 5), 2, cc, tb, C)
                    S.op("act", lambda e, C=C: e.activation(va_tok[0:C, :], ps[0:C, 5, 0:128], AF.Copy),
                         r=[("ps", 5)], w=["va_tok"])
                    if not state_only:
                        S.op("pe", lambda e, lo=lo, hi=hi, C=C: e.matmul(ps[0:C, 5, 256:256 + C], kiv[:, lo:hi], qd[:, lo:hi], start=True, stop=True),
                             r=["kiv", "qd"], w=[("ps", 5)])
                        S.op("dve", lambda e, C=C: e.tensor_tensor(aTm[0:C, 0:C], ps[0:C, 5, 256:256 + C], mle[0:C, 0:C], ALU.mult),
                             r=[("ps", 5), "mle"], w=["aTm"])
                        S.op("act", lambda e, St=St: e.activation(Sbf[:, :], St[:, h, :], AF.Copy), r=[sk], w=["Sbf"])
                        S.op("pe", lambda e, C=C: e.matmul(ps[:, 6, 0:C], va_tok[0:C, :], aTm[0:C, 0:C], start=True, stop=False),
                             r=["va_tok", "aTm"], w=[("ps", 6)])
                        S.op("pe", lambda e, lo=lo, hi=hi, C=C: e.matmul(ps[:, 6, 0:C], Sbf[:, :], qd[:, lo:hi], start=False, stop=True),
                             r=["Sbf", "qd"], w=[("ps", 6)])
                    S.op("pe", lambda e, C=C: e.matmul(ps[:, 5, 384:512], kend_tok[0:C, :], va_tok[0:C, :], start=True, stop=True),
                         r=["kend_tok", "va_tok"], w=[("ps", 5)])
                    S.op("dve", lambda e, St=St, eb=eb, last=last: e.scalar_tensor_tensor(
                        St[:, h, :], St[:, h, :], eb[:, last:last + 1], ps[:, 5, 384:512], ALU.mult, ALU.add),
                        r=[sk, "ebT", ("ps", 5)], w=[sk])
                    if not state_only:
                        head_norm_out(ps[:, 6, 0:C], ("ps", 6), ps[:, 6, 128:128 + C], ("ps", 6), C,
                                      oT[:, h, t0 + lo:t0 + hi], cols[:, R_GA + h:R_GA + h + 1], sg[:, lo:hi], HN_BUFS)
                    if ti == len(tiles) - 1 and c == max(1, tn // 128) - 1 and after_proj is not None:
                        after_proj()
                    yield

        def sb_head(half, ws, h, cc, after_proj=None):
            for ti, (t0, tn) in enumerate(tile_cols(ws)):
                if half == 1:
                    proj_fm(0, 4, cc, t0, tn)
                    S.op("act", lambda e, t0=t0, tn=tn: e.activation(qT[:, t0:t0 + tn], ps[:, 0, 0:tn], AF.Copy, scale=HS),
                         r=[("ps", 0)], w=["qT"])
                proj_fm(1, 5, cc, t0, tn)
                S.op("act", lambda e, t0=t0, tn=tn: e.activation(kTo[:, t0:t0 + tn], ps[:, 1, 0:tn], AF.Copy),
                     r=[("ps", 1)], w=["kTo"])
                yield
            for tb, n in tb_list(ws):
                proj_tm(7, 0, ("ps", 7), 6, cc, tb, n)
                dst = Vo[:, tb, :] if tb < NTB else Vn[0:n, h * 128:(h + 1) * 128]
                S.op("act", lambda e, n=n, dst=dst: e.activation(dst, ps[0:n, 7, 0:128], AF.Copy), r=[("ps", 7)],
                     w=["Vo" if tb < NTB else "Vn"])
            if after_proj is not None:
                after_proj()
            yield
            if ws:
                S.op("dve", lambda e: e.tensor_copy(qpad[:, h, 8 * h:8 * h + 8], qT[:, T:T + 8]), r=["qT"], w=["qpad"])
                S.op("dve", lambda e: e.tensor_copy(kTn[:, h, :], kTo[:, T:T + 8]), r=["kTo"], w=["kTn"])
            if half == 0:
                S.dma("sp", lambda e: e.dma_start(out=kTp_d[h], in_=kTo[:, 0:T]), r=["kTo"], w=[("kTp_d", h)])
                S.dma("sp", lambda e: e.dma_start(out=Vp_d[h], in_=Vo.rearrange("p t d -> p (t d)")), r=["Vo"], w=[("Vp_d", h)])
                return
            else:
                S.dma("sp", lambda e: e.dma_start(out=kTp[:, :], in_=kTp_d[h]), r=[("kTp_d", h)], w=["kTp"])
                S.dma("sp", lambda e: e.dma_start(out=Vp.rearrange("p t d -> p (t d)"), in_=Vp_d[h]), r=[("Vp_d", h)], w=["Vp"])
            lists = []
            for r_ in range(2):
                l_ = [("p", i) for i in range(8)] if half == 1 else []
                l_ += [("o", ob) for ob in range(4 * r_ + 4)]
                lists.append(l_)
            for step in range(max(len(l_) for l_ in lists)):
                for s in range(2):
                    if step >= len(lists[s]):
                        continue
                    kind, bi_ = lists[s][-(1 + step)]
                    nsteps = len(lists[s])
                    if kind == "p":
                        kblk, vblk, kk, vk, j = kTp[:, bi_ * 128:(bi_ + 1) * 128], Vp[:, bi_, :], "kTp", "Vp", -1
                        bias = bsbp[:, h:h + 1]
                    else:
                        kblk, vblk, kk, vk, j = kTo[:, bi_ * 128:(bi_ + 1) * 128], Vo[:, bi_, :], "kTo", "Vo", bi_ - 4 * s
                        bias = bsbc[:, h:h + 1]
                    q = qT[:, 512 * s:512 * s + 512]
                    S.op("pe", lambda e, s=s, kblk=kblk, q=q: e.matmul(ps[:, s, :], kblk, q, start=True, stop=False),
                         r=[kk, "qT"], w=[("ps", s)])
                    S.op("act", lambda e, s=s, bias=bias: e.activation(e_s[s][:, :], ps[:, s, :], AF.Exp, bias=bias), r=[("ps", s), "bsbc", "bsbp"], w=[("e", s)])
                    S.op("act", lambda e, s=s: e.activation(l_s[s], e_s[s][:, :], AF.Ln, bias=1.0), r=[("e", s)], w=[("l", s)])
                    if j >= 0:
                        S.op("dve", lambda e, s=s, j=j: e.tensor_tensor(l_s[s], l_s[s], Mj[:, j, :], ALU.mult),
                             r=[("l", s), "Mj"], w=[("l", s)])
                    S.op("pe", lambda e, s=s, step=step: e.matmul(ps[:, s, :], ntri[:], l_s[s], start=False, stop=(step == 0)),
                         r=["ntri", ("l", s)], w=[("ps", s)])
                    if step > 0:
                        S.op("pe", lambda e, s=s: e.matmul(ps[:, s, :], nones[:], Rbf[s], start=False, stop=True),
                             r=["nones", ("Rbf", s)], w=[("ps", s)])
                    S.op("act", lambda e, s=s, bias=bias: e.activation(ATs[s], ps[:, s, :], AF.Exp, bias=bias), r=[("ps", s), "bsbc", "bsbp"], w=[("AT", s)])
                    if j >= 0:
                        S.op("dve", lambda e, s=s, j=j: e.tensor_tensor(ATs[s], ATs[s], Mj[:, j, :], ALU.mult),
                             r=[("AT", s), "Mj"], w=[("AT", s)])
                    S.op("pe", lambda e, s=s, vblk=vblk, step=step, nsteps=nsteps: e.matmul(
                        ps[:, 2 + s, :], vblk, ATs[s], start=(step == 0), stop=(step == nsteps - 1)),
                        r=[vk, ("AT", s)], w=[("ps", 2 + s)])
                    if step < nsteps - 1:
                        if step == 0:
                            S.op("dve", lambda e, s=s: e.tensor_copy(R32[s], l_s[s]), r=[("l", s)], w=[("R32", s)])
                        else:
                            S.op("dve", lambda e, s=s: e.tensor_tensor(R32[s], R32[s], l_s[s], ALU.add), r=[("l", s), ("R32", s)], w=[("R32", s)])
                        S.op("act", lambda e, s=s: e.activation(Rbf[s], R32[s], AF.Copy), r=[("R32", s)], w=[("Rbf", s)])
                yield
            for s in range(2):
                bufs = ((l_s[s], ("l", s)), (e_s[s], ("e", s)), (R32[s], ("R32", s)))
                head_norm_out(ps[:, 2 + s, :], ("ps", 2 + s), ps[:, 7, :], ("ps", 7), 512,
                              oT[:, 8 + h, 512 * s:512 * s + 512], cols[:, R_GB + h:R_GB + h + 1], None, bufs)
                yield

        def run_all(*gens):
            gens = [g for g in gens if g is not None]
            while gens:
                for g in list(gens):
                    try:
                        next(g)
                    except StopIteration:
                        gens.remove(g)

        def sample_attention():
            base = 8256
            Kpg = [U[:, base + i * 1024:base + (i + 1) * 1024] for i in range(2)]
            Vpg = [U[:, base + 2048 + i * 2048:base + 2048 + (i + 1) * 2048].bitcast(BF16).rearrange("p (s c) -> p s c", s=4)
                   for i in range(2)]
            KT = wgb[0][:].rearrange("p k c -> p (k c)").rearrange("p (h s) -> p h s", h=8)
            f0 = wub[0][:].rearrange("p k c -> p (k c)").bitcast(F32)
            f1 = wub[1][:].rearrange("p k c -> p (k c)").bitcast(F32)
            eB, lB, lbB, cB = f0[0:64, 0:512], f0[0:64, 512:1024], f0[0:64, 1024:1536], f0[0:64, 1536:2048]
            AB, ones5 = f1[0:64, 0:512], f1[0:64, 512:1024]
            AT4 = f1[:, 1024:1152].bitcast(BF16).rearrange("p (s m) -> p s m", s=4)
            ATn = f1[:, 1152:1184].bitcast(BF16)
            sm = f1[0:64, 1184:1200]
            mnew = f1[0:64, 1200:1208]
            o64 = f1[0:64, 1280:1408]
            tmpB = gP[0:64, 0:1024]
            mbig = gP[0:64, 1024:2048]
            gSi = gS[:].bitcast(I32)
            pt_i, idx_i = gSi[:, 0:128], gSi[:, 128:256]
            pt_f = gS[:, 256:384]

            S.dma("sp", lambda e: e.dma_start(out=pt_i, in_=ptab.partition_broadcast(128)), w=["pt_i"])
            S.dma("sp", lambda e: e.dma_start(out=sm[:, 2:3], in_=b64), w=["sm_b"])
            S.dma("sp", lambda e: e.dma_start(out=mnew, in_=cst2[:, 0:8]), w=["mnew"])
            S.dma("sp", lambda e: e.dma_start(out=mbig, in_=cst2[:, 8:1032]), w=["mbig"])
            S.op("dve", lambda e: e.tensor_copy(pt_f, pt_i), r=["pt_i"], w=["pt_f"])
            S.op("dve", lambda e: e.tensor_scalar(pt_f, pt_f, 128.0, pid_f[:, 0:1], ALU.mult, ALU.add), r=["pt_f", "pid_f"], w=["pt_f"])
            S.op("dve", lambda e: e.tensor_copy(idx_i, pt_f), r=["pt_f"], w=["idx_i"])
            S.op("dve", lambda e: e.memset(ones5, 1.0), w=["ones5"])
            bcol = sm[:, 2:3]

            def ew_chain(zbank, n, mask, first):
                S.op("act", lambda e: e.activation(eB[:, 0:n], ps[0:64, zbank, 0:n], AF.Exp, bias=bcol), r=[("ps", zbank), "sm_b"], w=["eB"])
                S.op("act", lambda e: e.activation(lB[:, 0:n], eB[:, 0:n], AF.Ln, bias=1.0), r=["eB"], w=["lB"])
                if mask is not None:
                    S.op("dve", lambda e: e.tensor_tensor(lB[:, 0:n], lB[:, 0:n], mask, ALU.mult), r=["lB", "mnew"], w=["lB"])
                S.op("dve", lambda e: e.scalar_tensor_tensor(lbB[:, 0:n], ps[0:64, zbank, 0:n], bcol, lB[:, 0:n], ALU.add, ALU.subtract),
                     r=[("ps", zbank), "sm_b", "lB"], w=["lbB"])
                S.op("dve", lambda e: e.tensor_tensor_scan(cB[:, 0:n], ones5[:, 0:n], lB[:, 0:n], 0.0, ALU.mult, ALU.add),
                     r=["lB", "ones5"], w=["cB"])
                if first:
                    S.op("dve", lambda e: e.tensor_copy(sm[:, 0:1], cB[:, n - 1:n]), r=["cB"], w=["carry"])
                else:
                    S.op("dve", lambda e: e.tensor_tensor(sm[:, 0:1], sm[:, 0:1], cB[:, n - 1:n], ALU.add), r=["cB", "carry"], w=["carry"])
                S.op("dve", lambda e: e.tensor_scalar(sm[:, 1:2], sm[:, 0:1], -1.0, None, ALU.mult), r=["carry"], w=["negc"])
                S.op("dve", lambda e: e.tensor_tensor(lbB[:, 0:n], lbB[:, 0:n], cB[:, 0:n], ALU.add), r=["lbB", "cB"], w=["lbB"])
                S.op("act", lambda e: e.activation(AB[:, 0:n], lbB[:, 0:n], AF.Exp, bias=sm[:, 1:2]), r=["lbB", "negc"], w=["AB"])
                if mask is not None:
                    S.op("dve", lambda e: e.tensor_tensor(AB[:, 0:n], AB[:, 0:n], mask, ALU.mult), r=["AB", "mnew"], w=["AB"])

            for h in range(8):
                S.op("pe", lambda e, h=h: e.matmul(ps[0:64, 0, 0:8], qpad[:, h, :], kTn[:, h, :], start=(h == 0), stop=(h == 7)),
                     r=["qpad", "kTn"], w=[("ps", 0)])
            ew_chain(0, 8, mnew, True)
            S.op("pe", lambda e: e.transpose(ps[0:8, 1, 0:64], AB[:, 0:8], ident[0:64, 0:64]), r=["AB", "ident"], w=[("ps", 1)])
            S.op("act", lambda e: e.activation(ATn[0:8, :], ps[0:8, 1, 0:64], AF.Copy), r=[("ps", 1)], w=["ATn"])
            for hf in range(2):
                S.op("pe", lambda e, hf=hf: e.matmul(ps[0:64, 6 + hf, :], ATn[0:8, :], Vn[0:8, hf * 512:(hf + 1) * 512], start=True, stop=False),
                     r=["ATn", "Vn"], w=[("ps", 6 + hf)])
            NCH = NPAGES // 4
            KTs = [KT, wgb[1][:].rearrange("p k c -> p (k c)").rearrange("p (h s) -> p h s", h=8)]
            ZB = [0, 4]

            def prep(ci):
                ch = NCH - 1 - ci
                vb = Vpg[ci % 2]
                vk = ("Vpg", ci % 2)
                KTc = KTs[ci % 2]
                for sl in range(4):
                    pg = ch * 4 + sl
                    kb_ = Kpg[(ci * 4 + sl) % 2]
                    kk_ = ("Kpg", (ci * 4 + sl) % 2)
                    S.dma("pool", lambda e, kb_=kb_, pg=pg: e.indirect_dma_start(
                        out=kb_, out_offset=None, in_=cache_k[:, :],
                        in_offset=bass.IndirectOffsetOnAxis(ap=idx_i[:, pg:pg + 1], axis=0)), r=["idx_i"], w=[kk_])
                    S.dma("pool", lambda e, vb=vb, sl=sl, pg=pg: e.indirect_dma_start(
                        out=vb[:, sl, :], out_offset=None, in_=cache_v[:, :],
                        in_offset=bass.IndirectOffsetOnAxis(ap=idx_i[:, pg:pg + 1], axis=0)), r=["idx_i"], w=[(vk, sl)])
                    for hq in range(2):
                        bank = 2 + hq
                        for hh in range(4):
                            h = hq * 4 + hh
                            S.op("pe", lambda e, kb_=kb_, h=h, hh=hh, bank=bank: e.transpose(
                                ps[:, bank, hh * 128:(hh + 1) * 128], kb_[:, h * 128:(h + 1) * 128], ident[:]),
                                r=[kk_, "ident"], w=[("ps", bank)])
                        dst = KTc[:, hq * 4:(hq + 1) * 4, sl * 128:(sl + 1) * 128]
                        srcp = ps[:, bank, :].rearrange("p (h s) -> p h s", h=4)
                        if hq == 0:
                            S.op("act", lambda e, dst=dst, srcp=srcp: e.activation(dst, srcp, AF.Copy), r=[("ps", bank)], w=[("KT", ci % 2, sl)])
                        else:
                            S.op("dve", lambda e, dst=dst, srcp=srcp: e.tensor_copy(dst, srcp), r=[("ps", bank)], w=[("KT", ci % 2, sl)])
                zb = ZB[ci % 2]
                for h in range(8):
                    S.op("pe", lambda e, h=h, zb=zb, KTc=KTc: e.matmul(ps[0:64, zb, :], qpad[:, h, :], KTc[:, h, :], start=(h == 0), stop=(h == 7)),
                         r=["qpad"] + [("KT", ci % 2, sl) for sl in range(4)], w=[("ps", zb)])

            prep(0)
            for ci in range(NCH):
                if ci + 1 < NCH:
                    prep(ci + 1)
                vb = Vpg[ci % 2]
                vk = ("Vpg", ci % 2)
                ew_chain(ZB[ci % 2], 512, None, False)
                for sl in range(4):
                    S.op("pe", lambda e, sl=sl: e.transpose(ps[:, 1, sl * 64:(sl + 1) * 64], AB[:, sl * 128:(sl + 1) * 128], ident[0:64, 0:64]),
                         r=["AB", "ident"], w=[("ps", 1)])
                S.op("act", lambda e: e.activation(AT4[:, :, :], ps[:, 1, 0:256].rearrange("p (s m) -> p s m", s=4), AF.Copy),
                     r=[("ps", 1)], w=["AT4"])
                for sl in range(4):
                    for hf in range(2):
                        S.op("pe", lambda e, sl=sl, hf=hf, vb=vb, ci=ci: e.matmul(
                            ps[0:64, 6 + hf, :], AT4[:, sl, :], vb[:, sl, hf * 512:(hf + 1) * 512],
                            start=False, stop=(ci == NCH - 1 and sl == 3)), r=["AT4", (vk, sl)], w=[("ps", 6 + hf)])
            for hf in range(2):
                S.op("dve", lambda e, hf=hf: e.tensor_tensor(tmpB[:, hf * 512:(hf + 1) * 512], ps[0:64, 6 + hf, :],
                                                             mbig[:, hf * 512:(hf + 1) * 512], ALU.mult),
                     r=[("ps", 6 + hf), "mbig"], w=["tmpB"])
            S.op("dve", lambda e: e.tensor_reduce(o64, tmpB.rearrange("p (h d) -> p d h", h=8), mybir.AxisListType.X, ALU.add),
                 r=["tmpB"], w=["o64"])
            S.op("act", lambda e: e.activation(tmpB[:, 0:128], o64, AF.Square, accum_out=sm[:, 3:4]), r=["o64"], w=["tmpB", "ssq64"])
            S.op("act", lambda e: e.activation(sm[:, 4:5], sm[:, 3:4], AF.Sqrt, bias=EPS, scale=1.0 / 128), r=["ssq64"], w=["rstd64"])
            S.op("dve", lambda e: e.reciprocal(sm[:, 4:5], sm[:, 4:5]), r=["rstd64"], w=["rstd64"])
            S.op("dve", lambda e: e.tensor_scalar(o64, o64, sm[:, 4:5], None, ALU.mult), r=["o64", "rstd64"], w=["o64"])
            S.op("pe", lambda e: e.transpose(ps[:, 1, 0:64], o64, ident[0:64, 0:64]), r=["o64", "ident"], w=[("ps", 1)])
            for h in range(8):
                S.op("dve", lambda e, h=h: e.tensor_scalar(oT[:, 8 + h, T:T + 8], ps[:, 1, 8 * h:8 * h + 8],
                                                           cols[:, R_GB + h:R_GB + h + 1], None, ALU.mult),
                     r=[("ps", 1), "cols"], w=["oT"])

        def load_w7(types, hp):
            for ty in types:
                c0 = TYPE_BASE[ty] + hp * 256
                S.dma("pool", lambda e, ty=ty, c0=c0: e.dma_start(
                    out=W7[ty], in_=w_in[:, c0:c0 + 256].rearrange("(k p) c -> p k c", p=128)), w=[W7K[ty]])

        def mixer(half, ws):
            if half == 0:
                for hp in range(4):
                    load_w7((1, 2, 5, 6), hp)
                    for cc in range(2):
                        h = hp * 2 + cc
                        for _ in hgrn_head(False, h, cc, state_only=True):
                            if bg:
                                bg.pop(0)()
                        for _ in sb_head(0, False, h, cc):
                            if bg:
                                bg.pop(0)()
                S.op("dve", lambda e: e.tensor_scalar(Sst[:].rearrange("p h v -> p (h v)"), Sst[:].rearrange("p h v -> p (h v)"),
                                                      flagc[:, 0:1], None, ALU.mult), r=["Sst", "flagc"], w=["Sst"])
                return
            S.dma("pool", lambda e: e.dma_start(out=Mj[:], in_=cst[:, 256:2304].rearrange("p (j t) -> p j t", j=4)), w=["Mj"])
            if ws:
                S.op("dve", lambda e: e.memset(w0f[:, 0:576], 0.0), w=["qpad", "kTn"])
            load_w7((0, 1, 2, 3), 0)
            load_w7((4, 5, 6), 0)
            run_all(hgrn_head(ws, 0, 0))
            for h in range(8):
                nh = h + 1
                ap_h = (lambda hp=(nh + 1) // 2: load_w7((0, 1, 2, 3), hp)) if (nh % 2 == 1 and nh < 7) else None
                ap_s = (lambda hp=(h + 1) // 2: load_w7((4, 5, 6), hp)) if (h % 2 == 1 and h < 7) else None
                gh = hgrn_head(ws, nh, nh % 2, after_proj=ap_h) if nh < 8 else None
                gs = sb_head(half, ws, h, h % 2, after_proj=ap_s)
                run_all(gs, gh)
            if ws:
                S.barrier()
                sample_attention()

        def make_bcast(dst, dkey, colfn, mult):
            for k in range(16):
                dg = diag[k % 2]
                S.op("dve", lambda e, dg=dg, k=k: e.tensor_scalar(dg[:], ident[:], colfn(k), mult, ALU.mult, ALU.mult),
                     r=["ident", "modc", "effs"], w=[("diag", k % 2)])
                S.op("pe", lambda e, dg=dg, k=k: e.matmul(
                    ps[:, 4 + k // 4, (k % 4) * 128:(k % 4 + 1) * 128], ones_f[:], dg[:], start=True, stop=True),
                    r=[("diag", k % 2), "ones_f"], w=[("ps", 4 + k // 4)])
            for b_ in range(4):
                S.op("act", lambda e, b_=b_: e.activation(dst[:, b_ * 512:(b_ + 1) * 512], ps[:, 4 + b_, :], AF.Copy),
                     r=[("ps", 4 + b_)], w=[dkey])

        def wout_residual(ws):
            tbs = tb_list(ws)
            for cg in range(4):
                pair = (wgb, "wgb") if cg % 2 == 0 else (wub, "wub")
                for hf in range(2):
                    S.dma("pool", lambda e, pair=pair, hf=hf, cg=cg: e.dma_start(
                        out=pair[0][hf][:].rearrange("p k c -> p (k c)").rearrange("p (k c) -> p k c", k=8),
                        in_=w_out[hf * 1024:(hf + 1) * 1024, cg * 512:(cg + 1) * 512].rearrange("(k p) c -> p k c", p=128)),
                        w=[(pair[1], hf)])
                for bi_, (tb, n) in enumerate(tbs):
                    bank = 4 + (cg * len(tbs) + bi_) % 2
                    for k in range(16):
                        wv_ = pair[0][k // 8][:].rearrange("p k c -> p (k c)").rearrange("p (k c) -> p k c", k=8)
                        S.op("pe", lambda e, k=k, tb=tb, n=n, bank=bank, wv_=wv_: e.matmul(
                            ps[0:n, bank, :], hT[:, k, tb * 128:tb * 128 + n], wv_[:, k % 8, :],
                            start=(k == 0), stop=(k == 15)), r=[("hT", tb), (pair[1], k // 8)], w=[("ps", bank)])
                    tmp = dtmp[bank - 4]
                    gt, gk = (gP, "gP") if tb < NTB else (gS, "gS")
                    S.op("dve", lambda e, tmp=tmp, n=n, bank=bank, gt=gt, cg=cg: e.tensor_tensor(
                        tmp[0:n, :], ps[0:n, bank, :], gt[0:n, cg * 512:(cg + 1) * 512], ALU.mult),
                        r=[("ps", bank), gk], w=[("dtmp", bank)])
                    xo = xrow(tb, n)[:, cg * 512:(cg + 1) * 512]
                    S.op("dve", lambda e, tmp=tmp, n=n, xo=xo: e.tensor_tensor(xo, xo, tmp[0:n, :], ALU.add),
                         r=[("dtmp", bank), ("x", tb)], w=[("x", tb)])

        def final_out(half, ws):
            for r_ in ((0, 1) if ws else (0,)):
                make_bcast(gP, "gP", lambda k, r_=r_: effs[:, 3, k, r_:r_ + 1], 1.0)
                make_bcast(gS, "gS", lambda k, r_=r_: modc[:, SH_OFF[3] + k, r_:r_ + 1], 1.0)
                for tb, n in (tb_list(False) if r_ == 0 else [(NTB, 8)]):
                    xin = xrow(tb, n)
                    S.op("act", lambda e, xin=xin, n=n, tb=tb: e.activation(
                        xn[0:n, :], xin, AF.Square, accum_out=ssq[0:n, tb:tb + 1]), r=[("x", tb)], w=["xn", ("ssq", tb)])
                    S.op("act", lambda e, n=n, tb=tb: e.activation(
                        rstd[0:n, tb:tb + 1], ssq[0:n, tb:tb + 1], AF.Sqrt, bias=EPS, scale=1.0 / D),
                        r=[("ssq", tb)], w=[("rstd", tb)])
                    S.op("dve", lambda e, n=n, tb=tb: e.reciprocal(rstd[0:n, tb:tb + 1], rstd[0:n, tb:tb + 1]),
                         r=[("rstd", tb)], w=[("rstd", tb)])
                    S.op("dve", lambda e, xin=xin, n=n, tb=tb: e.scalar_tensor_tensor(
                        xn[0:n, :], xin, rstd[0:n, tb:tb + 1], gP[0:n, :], ALU.mult, ALU.mult),
                        r=[("x", tb), ("rstd", tb), "gP"], w=["xn"])
                    S.op("dve", lambda e, n=n: e.tensor_tensor(xn[0:n, :], xn[0:n, :], gS[0:n, :], ALU.add), r=["xn", "gS"], w=["xn"])
                    dst = y_p[tb * 128:(tb + 1) * 128, :] if tb < NTB else y_s
                    S.dma("sp", lambda e, dst=dst, n=n: e.dma_start(out=dst, in_=xn[0:n, :]), r=["xn"])

        for half in range(2):
            ws = (half == 1)
            for tb in range(NTB):
                S.dma("sp", lambda e, tb=tb, half=half: e.dma_start(
                    out=x_tok[:, tb, :], in_=x_all[half * T + tb * 128:half * T + (tb + 1) * 128, :]), w=[("x", tb)])
            if ws:
                S.dma("sp", lambda e: e.dma_start(out=xS[0:8, :], in_=x_smp), w=[("x", NTB)])
            make_gate(0, 0.5)
            norm_to_hT(0, ws)
            ffn(w1g, w1u, w1d, ws)
            if half == 0:
                finish_mod(0)
            norm_to_hT(1, ws)
            if half == 0:
                S.barrier()
                mixer(0, False)
                finish_mod(1)
                S.barrier()
                continue
            kv_proj(half, ws)
            for tb in range(NTB):
                S.dma("sp", lambda e, tb=tb: e.dma_start(out=xsp[tb * 128:(tb + 1) * 128, :], in_=x_tok[:, tb, :]), r=[("x", tb)], w=[("xsp", tb)])
            S.barrier()
            mixer(half, ws)
            S.barrier()
            for k in range(16):
                S.op("dve" if k % 2 == 0 else "act",
                     (lambda e, k=k: e.tensor_copy(hT[:, k, :], oT[:, k, :])) if k % 2 == 0 else
                     (lambda e, k=k: e.activation(hT[:, k, :], oT[:, k, :], AF.Copy)), r=["oT"], w=["hTall"])
            S.barrier()
            for tb in range(NTB):
                S.dma("sp", lambda e, tb=tb: e.dma_start(out=x_tok[:, tb, :], in_=xsp[tb * 128:(tb + 1) * 128, :]), w=[("x", tb)])
            make_gate(1, 1.0)
            wout_residual(ws)
            make_gate(2, 0.5)
            norm_to_hT(2, ws)
            ffn(w2g, w2u, w2d, ws)
            final_out(half, ws)
            S.barrier()
        S.dma("sp", lambda e: e.dma_start(out=s_p.rearrange("h k v -> k h v"), in_=Sst[:]), r=["Sst"])
        S.dma("sp", lambda e: e.dma_start(out=s_s.rearrange("h k v -> k h v"), in_=Sss[:]), r=["Sss"])
        S.final_waits("sp", S.all_dma_tokens())

        with nc.Block() as block:
            S.emit(block)
    return nc


def _prep_vecs(inp, b, i):
    v = np.zeros((384, 128), np.float32)
    v[R_CP:R_CP + 16] = inp["c_prompt"][b].reshape(16, 128)
    v[R_CS:R_CS + 16] = inp["c_sample"][i].reshape(16, 128)
    v[R_N1:R_N1 + 16] = inp["norm_ffn1"][0].reshape(16, 128)
    v[R_N2:R_N2 + 16] = inp["norm_mix"][0].reshape(16, 128)
    v[R_N3:R_N3 + 16] = inp["norm_ffn2"][0].reshape(16, 128)
    v[R_NF:R_NF + 16] = inp["norm_final"].reshape(16, 128)
    v[R_LB:R_LB + 8] = inp["lb_logits"][0].reshape(8, 128)
    v[R_GA:R_GA + 8] = inp["g_out_a"][0].reshape(8, 128)
    v[R_GB:R_GB + 8] = inp["g_out_b"][0].reshape(8, 128)
    v[R_BM:R_BM + 144] = inp["b_mod"][0].reshape(144, 128)
    v[R_BM + 144:R_BM + 176] = inp["b_final_mod"].reshape(32, 128)
    return v


def _consts():
    c = np.zeros((128, 2304), np.float32)
    p = np.arange(128)[:, None]
    f = np.arange(128)[None, :]
    c[:, 0:128] = -((p >= f).astype(np.float32))
    c[:, 128:256] = (p <= f)
    t = np.arange(512)[None, :]
    for j in range(4):
        c[:, 256 + j * 512:256 + (j + 1) * 512] = (t > 128 * j + p)
    return c


def _consts2():
    c = np.zeros((64, 1032), np.float32)
    p = np.arange(64)[:, None]
    c[:, 0:8] = (np.arange(8)[None, :] < (p % 8))
    c[:, 8:1032] = ((np.arange(1024)[None, :] // 128) == (p // 8))
    return c


_NC_CACHE = {}


def kernel(**inp):
    inp = {k: np.asarray(v) for k, v in inp.items()}
    lim = int(os.environ.get("MK_STAGE", "99"))
    if lim not in _NC_CACHE:
        _NC_CACHE[lim] = build_program(lim)
    nc = _NC_CACHE[lim]
    in_maps = []
    for i in range(NCORES):
        b = i // 2
        m = {
            "x_all": np.ascontiguousarray(np.concatenate([inp["x_prompt"][b, 0:T], inp["x_prompt"][b, (i % 2) * T:(i % 2 + 1) * T]], axis=0)),
            "flag": np.full((1, 1), float(i % 2), np.float32),
            "x_smp": np.ascontiguousarray(inp["x_sample"][i]),
            "vecs": _prep_vecs(inp, b, i),
            "w_mod": inp["w_mod"][0], "w_fin": inp["w_final_mod"],
            "w1g": inp["w_ffn1_gate"][0], "w1u": inp["w_ffn1_up"][0], "w1d": inp["w_ffn1_down"][0],
            "w2g": inp["w_ffn2_gate"][0], "w2u": inp["w_ffn2_up"][0], "w2d": inp["w_ffn2_down"][0],
            "w_in": inp["w_in"][0], "w_out": inp["w_out"][0],
            "st_in": np.ascontiguousarray(inp["state_hgrn"][0, i]),
            "cst": _consts(), "b_sb": np.ascontiguousarray(inp["b_sb"].reshape(1, 8)),
            "cache_k": inp["cache_k"].reshape(NPHYS * 128, 1024), "cache_v": inp["cache_v"].reshape(NPHYS * 128, 1024),
            "ptab": np.ascontiguousarray(inp["page_table"][i].reshape(1, NPAGES)).astype(np.int32),
            "b64": np.ascontiguousarray(np.repeat(inp["b_sb"].reshape(8), 8).reshape(64, 1)),
            "cst2": _consts2(),
        }
        in_maps.append(m)
    res = run_bass_kernel_spmd(nc, in_maps, core_ids=list(range(NCORES))).results
    f32 = np.float32
    y_prompt = np.stack([np.concatenate([res[2 * b]["y_p"], res[2 * b + 1]["y_p"]], axis=0) for b in range(4)]).astype(f32)
    y_sample = np.stack([res[i]["y_s"] for i in range(8)]).astype(f32)
    k_prompt = np.stack([np.concatenate([res[2 * b]["k_p"], res[2 * b + 1]["k_p"]], axis=0).reshape(2 * T, 8, 128)
                         for b in range(4)])[None].astype(f32)
    v_prompt = np.stack([np.concatenate([res[2 * b]["v_p"], res[2 * b + 1]["v_p"]], axis=0).reshape(2 * T, 8, 128)
                         for b in range(4)])[None].astype(f32)
    k_sample = np.stack([res[i]["k_s"].reshape(8, 8, 128) for i in range(8)])[None].astype(f32)
    v_sample = np.stack([res[i]["v_s"].reshape(8, 8, 128) for i in range(8)])[None].astype(f32)
    s_prompt = np.stack([res[2 * b + 1]["s_p"] for b in range(4)])[None].astype(f32)
    s_sample = np.stack([res[i]["s_s"] for i in range(8)])[None].astype(f32)
    return (y_prompt, y_sample, k_prompt, v_prompt, k_sample, v_sample, s_prompt, s_sample)
```

```python
import os
from contextlib import ExitStack
import numpy as np
import concourse.bass as bass
import concourse.mybir as mybir
from concourse.bass_utils import run_bass_kernel_spmd

F32 = mybir.dt.float32
BF16 = mybir.dt.bfloat16
I32 = mybir.dt.int32
AF = mybir.ActivationFunctionType
ALU = mybir.AluOpType

D = 2048
DFF = 5632
NFF = DFF // 128
DIN = 7168
T = 1024
NTB = T // 128
GC = 4
EPS = 1e-6
NCORES = 8
NPAGES = 128
NPHYS = 1280

R_CP, R_CS, R_N1, R_N2, R_N3, R_NF, R_LB, R_GA, R_GB, R_BM = 0, 16, 32, 48, 64, 80, 96, 104, 112, 128


class Sched:
    ENG = ("pe", "act", "dve", "pool", "sp")

    def __init__(self, nc, es, n_dma_slots=6):
        self.nc = nc
        self.ops = {e: [] for e in self.ENG}
        self.cnt = {e: 0 for e in self.ENG}
        self.lastw = {}
        self.readers = {}
        self.sems = {}
        for e in ("pe", "act", "dve", "pool"):
            self.sems[e] = es.enter_context(nc.semaphore("sem_" + e))
        self.slots = {}
        self.slot_cnt = {}
        self.slot_rr = {}
        for q in ("sp", "pool"):
            self.slots[q] = []
            for j in range(n_dma_slots):
                nm = "dq_%s_%d" % (q, j)
                self.sems[nm] = es.enter_context(nc.semaphore(nm))
                self.slots[q].append(nm)
                self.slot_cnt[nm] = 0
            self.slot_rr[q] = 0
        self.nops = 0

    def _deps(self, r, w):
        deps = set()
        for k in r:
            if k in self.lastw:
                deps.add(self.lastw[k])
        for k in w:
            if k in self.lastw:
                deps.add(self.lastw[k])
            for t in self.readers.get(k, ()):
                deps.add(t)
        return deps

    def _commit(self, tok, r, w):
        for k in w:
            self.lastw[k] = tok
            self.readers[k] = []
        for k in r:
            if k not in w:
                self.readers.setdefault(k, []).append(tok)

    def op(self, eng, fn, r=(), w=()):
        deps = self._deps(r, w)
        if eng == "pe":
            deps = {d for d in deps if d[0] != "pe"}
        self.cnt[eng] += 1
        tok = (eng, self.cnt[eng])
        self.ops[eng].append((fn, deps, tok, 1))
        self._commit(tok, r, w)
        self.nops += 1

    def dma(self, q, fn, r=(), w=()):
        deps = self._deps(r, w)
        j = self.slot_rr[q]
        self.slot_rr[q] = (j + 1) % len(self.slots[q])
        nm = self.slots[q][j]
        if self.slot_cnt[nm] > 0:
            deps.add((nm, 16 * self.slot_cnt[nm]))
        self.slot_cnt[nm] += 1
        tok = (nm, 16 * self.slot_cnt[nm])
        self.ops[q].append((fn, deps, tok, 16))
        self._commit(tok, r, w)
        self.nops += 1

    def barrier(self):
        toks = [(e, self.cnt[e]) for e in ("pe", "act", "dve", "pool") if self.cnt[e] > 0] + self.all_dma_tokens()
        for e in self.ENG:
            self.ops[e].append((None, set(toks), None, 0))
        self.lastw = {}
        self.readers = {}

    def final_waits(self, eng, toks):
        self.ops[eng].append((None, set(toks), None, 0))

    def all_dma_tokens(self):
        return [(nm, 16 * c) for nm, c in self.slot_cnt.items() if c > 0]

    def emit(self, block):
        sems = self.sems

        def mk(engname):
            def body(e):
                seen = {}
                for fn, deps, tok, inc in self.ops[engname]:
                    for (s, v) in sorted(deps):
                        if seen.get(s, 0) < v:
                            e.wait_ge(sems[s], v)
                            seen[s] = v
                    if fn is None:
                        continue
                    ins = fn(e)
                    ins.then_inc(sems[tok[0]], inc)
            return body

        block.tensor(mk("pe"))
        block.scalar(mk("act"))
        block.vector(mk("dve"))
        block.gpsimd(mk("pool"))
        block.sync(mk("sp"))


def build_program(stage_limit=99):
    nc = bass.Bass("TRN2", target_bir_lowering=False)
    dt = nc.dram_tensor
    x_all = dt("x_all", [2 * T, D], F32, kind="ExternalInput").ap()
    x_smp = dt("x_smp", [8, D], F32, kind="ExternalInput").ap()
    vecs = dt("vecs", [384, 128], F32, kind="ExternalInput").ap()
    w_mod = dt("w_mod", [D, 9 * D], F32, kind="ExternalInput").ap()
    w_fin = dt("w_fin", [D, 2 * D], F32, kind="ExternalInput").ap()
    w1g = dt("w1g", [D, DFF], F32, kind="ExternalInput").ap()
    w1u = dt("w1u", [D, DFF], F32, kind="ExternalInput").ap()
    w1d = dt("w1d", [DFF, D], F32, kind="ExternalInput").ap()
    w2g = dt("w2g", [D, DFF], F32, kind="ExternalInput").ap()
    w2u = dt("w2u", [D, DFF], F32, kind="ExternalInput").ap()
    w2d = dt("w2d", [DFF, D], F32, kind="ExternalInput").ap()
    w_in = dt("w_in", [D, DIN], F32, kind="ExternalInput").ap()
    w_out = dt("w_out", [D, D], F32, kind="ExternalInput").ap()
    y_p = dt("y_p", [T, D], F32, kind="ExternalOutput").ap()
    y_s = dt("y_s", [8, D], F32, kind="ExternalOutput").ap()
    k_p = dt("k_p", [T, 1024], F32, kind="ExternalOutput").ap()
    v_p = dt("v_p", [T, 1024], F32, kind="ExternalOutput").ap()
    k_s = dt("k_s", [8, 1024], F32, kind="ExternalOutput").ap()
    v_s = dt("v_s", [8, 1024], F32, kind="ExternalOutput").ap()
    s_p = dt("s_p", [8, 128, 128], F32, kind="ExternalOutput").ap()
    s_s = dt("s_s", [8, 128, 128], F32, kind="ExternalOutput").ap()
    st_in = dt("st_in", [8, 128, 128], F32, kind="ExternalInput").ap()
    cst = dt("cst", [128, 2304], F32, kind="ExternalInput").ap()
    b_sb = dt("b_sb", [1, 8], F32, kind="ExternalInput").ap()
    cache_k = dt("cache_k", [NPHYS * 128, 1024], F32, kind="ExternalInput").ap()
    cache_v = dt("cache_v", [NPHYS * 128, 1024], F32, kind="ExternalInput").ap()
    ptab = dt("ptab", [1, NPAGES], I32, kind="ExternalInput").ap()
    b64 = dt("b64", [64, 1], F32, kind="ExternalInput").ap()
    flag = dt("flag", [1, 1], F32, kind="ExternalInput").ap()
    cst2 = dt("cst2", [64, 1032], F32, kind="ExternalInput").ap()
    xsp = dt("xsp", [T, D], F32, kind="Internal").ap()
    kTp_d = dt("kTp_d", [8, 128, T], BF16, kind="Internal").ap()
    Vp_d = dt("Vp_d", [8, 128, T], BF16, kind="Internal").ap()

    es = ExitStack()
    with es:
        def sb(name, shape, dtype):
            return es.enter_context(nc.sbuf_tensor(name, shape, dtype))

        S = Sched(nc, es)
        U = sb("U", [128, NTB * D], F32)
        x_tok = U[:].rearrange("p (t d) -> p t d", t=NTB)
        xS = sb("xS", [128, D], F32)
        hT = sb("hT", [128, 16, T + 8], BF16)
        aT = sb("aT", [128, GC, T + 8], BF16)
        wgb = [sb("wgb%d" % i, [128, 16, 256], BF16) for i in range(2)]
        wub = [sb("wub%d" % i, [128, 16, 256], BF16) for i in range(2)]
        wdb = [sb("wdb%d" % i, [128, GC, 512], BF16) for i in range(2)]
        gP = sb("gP", [128, D], F32)
        gS = sb("gS", [128, D], F32)
        xn = sb("xn", [128, D], F32)
        sgt = [sb("sgt%d" % i, [128, 512], F32) for i in range(1)]
        dtmp = [sb("dtmp%d" % i, [128, 512], F32) for i in range(2)]
        vecs_sb = sb("vecs_sb", [128, 3, 128], F32)
        cols = sb("cols", [128, 384], F32)
        scT = sb("scT", [128, 16, 2], BF16)
        scT32 = sb("scT32", [128, 16, 2], F32)
        modc = sb("modc", [128, 176, 2], F32)
        effs = sb("effs", [128, 4, 16, 2], F32)
        ident = sb("ident", [128, 128], F32)
        ones_f = sb("ones_f", [128, 128], F32)
        diag = [sb("diag%d" % i, [128, 128], F32) for i in range(2)]
        ssq = sb("ssq", [128, 16], F32)
        rstd = sb("rstd", [128, 16], F32)
        iota_i = sb("iota_i", [128, 128], I32)
        iota_f = sb("iota_f", [128, 128], F32)
        pid_f = sb("pid_f", [128, 1], F32)
        Sst = sb("Sst", [128, 8, 128], F32)
        Sss = sb("Sss", [128, 8, 128], F32)
        lbc = sb("lbc", [128, 8], F32)
        oml = sb("oml", [128, 8], F32)
        kend_tok = sb("kend_tok", [128, 128], BF16)
        ntri = sb("ntri", [128, 128], BF16)
        nones = sb("nones", [128, 128], BF16)
        ones_b = sb("ones_b", [128, 128], BF16)
        mle = sb("mle", [128, 128], BF16)
        bsbc = sb("bsbc", [128, 8], F32)
        bsbp = sb("bsbp", [128, 8], F32)
        flagc = sb("flagc", [128, 1], F32)
        nbig = sb("nbig", [128, 1], F32)
        va_tok = sb("va_tok", [128, 128], BF16)
        kend_tokB = sb("kend_tokB", [128, 128], BF16)
        va_tokB = sb("va_tokB", [128, 128], BF16)
        ps = es.enter_context(nc.psum_tensor("ps", [128, 8, 512], F32))

        S.op("pool", lambda e: e.iota(iota_i[:], [[1, 128]], base=0, channel_multiplier=0), w=["iota_i"])
        S.op("pool", lambda e: e.tensor_copy(iota_f[:], iota_i[:]), r=["iota_i"], w=["iota_f"])
        S.op("pool", lambda e: e.iota(iota_i[:, 0:1], [[1, 1]], base=0, channel_multiplier=1), r=["iota_f"], w=["iota_i"])
        S.op("pool", lambda e: e.tensor_copy(pid_f[:], iota_i[:, 0:1]), r=["iota_i"], w=["pid_f"])
        S.op("dve", lambda e: e.tensor_scalar(ident[:], iota_f[:], pid_f[:, 0:1], None, ALU.is_equal),
             r=["iota_f", "pid_f"], w=["ident"])
        S.op("dve", lambda e: e.memset(ones_f[:], 1.0), w=["ones_f"])
        S.op("dve", lambda e: e.memset(ones_b[:], 1.0), w=["ones_b"])
        S.op("dve", lambda e: e.memset(nones[:], -1.0), w=["nones"])
        S.dma("pool", lambda e: e.dma_start(out=ntri[:], in_=cst[:, 0:128]), w=["ntri"])
        S.dma("pool", lambda e: e.dma_start(out=mle[:], in_=cst[:, 128:256]), w=["mle"])
        S.dma("sp", lambda e: e.dma_start(out=bsbc[:], in_=b_sb.partition_broadcast(128)), w=["bsbc"])
        S.dma("sp", lambda e: e.dma_start(out=flagc[:], in_=flag.partition_broadcast(128)), w=["flagc"])
        S.op("dve", lambda e: e.tensor_scalar(nbig[:], flagc[:], -1.0, 1.0e4, ALU.add, ALU.mult), r=["flagc"], w=["nbig"])
        S.op("dve", lambda e: e.tensor_scalar(bsbp[:], bsbc[:], nbig[:, 0:1], None, ALU.add), r=["nbig", "bsbc"], w=["bsbp"])

        S.dma("sp", lambda e: e.dma_start(out=vecs_sb[:], in_=vecs.rearrange("(g r) c -> r g c", g=3)), w=["vecs_sb"])
        for g in range(3):
            S.op("pe", lambda e, g=g: e.transpose(ps[:, 7, g * 128:(g + 1) * 128], vecs_sb[:, g, :], ident[:]),
                 r=["vecs_sb", "ident"], w=[("ps", 7)])
        S.op("act", lambda e: e.activation(cols[:], ps[:, 7, 0:384], AF.Copy), r=[("ps", 7)], w=["cols"])
        for r_ in range(2):
            S.op("act", lambda e, r_=r_: e.activation(scT[:, :, r_], cols[:, 16 * r_:16 * r_ + 16], AF.Silu),
                 r=["cols"], w=["scT"])
            S.op("act", lambda e, r_=r_: e.activation(scT32[:, :, r_], cols[:, 16 * r_:16 * r_ + 16], AF.Silu),
                 r=["cols"], w=["scT32"])

        SC_OFF = [16, 64, 112, 160]
        SH_OFF = [0, 48, 96, 144]
        GA_OFF = [32, 80, 128]
        G_ROW = [R_N1, R_N2, R_N3, R_NF]
        for n in range(24):
            src_ = w_mod[:, n * 256:(n + 1) * 256]
            bufs4 = [wgb[0], wgb[1], wub[0], wub[1]]
            keys4 = [("wgb", 0), ("wgb", 1), ("wub", 0), ("wub", 1)]
            buf, bk = bufs4[n % 4], keys4[n % 4]
            S.dma("pool", lambda e, buf=buf, src_=src_: e.dma_start(
                out=buf[:], in_=src_.rearrange("(k p) c -> p k c", p=128)), w=[bk])
            for cc in range(2):
                j = n * 2 + cc
                for k in range(16):
                    S.op("pe", lambda e, buf=buf, k=k, cc=cc, j=j: e.matmul(
                        ps[:, 6, 2 * j:2 * j + 2], buf[:, k, cc * 128:(cc + 1) * 128], scT[:, k, :],
                        start=(k == 0), stop=(k == 15)), r=[bk, "scT"], w=[("ps", 6)])
        S.op("dve", lambda e: e.tensor_tensor(
            modc[:, 0:48, :], ps[:, 6, 0:96].rearrange("p (j r) -> p j r", r=2),
            cols[:, R_BM:R_BM + 48].unsqueeze(2).to_broadcast([128, 48, 2]), ALU.add),
            r=[("ps", 6), "cols"], w=["modc"])

        def make_effs(n):
            for r_ in range(2):
                S.op("dve", lambda e, n=n, r_=r_: e.scalar_tensor_tensor(
                    effs[:, n, :, r_], modc[:, SC_OFF[n]:SC_OFF[n] + 16, r_], 1.0,
                    cols[:, G_ROW[n]:G_ROW[n] + 16], ALU.add, ALU.mult), r=["modc", "cols"], w=["effs"])
        make_effs(0)

        bg = []
        stg = xS[:].bitcast(BF16).rearrange("p (k c) -> p k c", k=16)

        def mod_group_task(n):
            def task():
                src_ = w_mod[:, n * 256:(n + 1) * 256] if n < 72 else w_fin[:, (n - 72) * 256:(n - 71) * 256]
                S.dma("pool", lambda e: e.dma_start(out=stg, in_=src_.rearrange("(k p) c -> p k c", p=128)), w=[("x", NTB)])
                for cc in range(2):
                    j = n * 2 + cc
                    for k in range(16):
                        S.op("pe", lambda e, k=k, cc=cc, j=j: e.matmul(
                            ps[:, 6, 2 * j:2 * j + 2], stg[:, k, cc * 128:(cc + 1) * 128], scT[:, k, :],
                            start=(k == 0), stop=(k == 15)), r=[("x", NTB), "scT"], w=[("ps", 6)])
            return task
        for n in range(24, 88):
            bg.append(mod_group_task(n))

        def finish_mod():
            while bg:
                bg.pop(0)()
            S.op("dve", lambda e: e.tensor_tensor(
                modc[:, 48:176, :], ps[:, 6, 96:352].rearrange("p (j r) -> p j r", r=2),
                cols[:, R_BM + 48:R_BM + 176].unsqueeze(2).to_broadcast([128, 128, 2]), ALU.add),
                r=[("ps", 6), "cols"], w=["modc"])
            for n in (1, 2, 3):
                make_effs(n)

        def make_gate(m, mult):
            for r_, gt, gk in ((0, gP, "gP"), (1, gS, "gS")):
                for k in range(16):
                    dg = diag[k % 2]
                    S.op("dve", lambda e, dg=dg, k=k, r_=r_: e.tensor_scalar(
                        dg[:], ident[:], modc[:, GA_OFF[m] + k, r_:r_ + 1], mult, ALU.mult, ALU.mult),
                        r=["ident", "modc"], w=[("diag", k % 2)])
                    S.op("pe", lambda e, dg=dg, k=k: e.matmul(
                        ps[:, 4 + k // 4, (k % 4) * 128:(k % 4 + 1) * 128], ones_f[:], dg[:], start=True, stop=True),
                        r=[("diag", k % 2), "ones_f"], w=[("ps", 4 + k // 4)])
                for b in range(4):
                    S.op("act", lambda e, b=b, gt=gt: e.activation(gt[:, b * 512:(b + 1) * 512], ps[:, 4 + b, :], AF.Copy),
                         r=[("ps", 4 + b)], w=[gk])

        def tile_cols(with_sample):
            t = [(0, 512), (512, 512)]
            if with_sample:
                t.append((1024, 8))
            return t

        def tb_list(with_sample):
            l = [(tb, 128) for tb in range(NTB)]
            if with_sample:
                l.append((NTB, 8))
            return l

        def xrow(tb, n):
            return x_tok[:, tb, :] if tb < NTB else xS[0:n, :]

        def norm_to_hT(nidx, with_sample):
            for tb, n in tb_list(with_sample):
                xin = xrow(tb, n)
                r_ = 0 if tb < NTB else 1
                S.op("act", lambda e, xin=xin, n=n, tb=tb: e.activation(
                    xn[0:n, :], xin, AF.Square, accum_out=ssq[0:n, tb:tb + 1]), r=[("x", tb)], w=["xn", ("ssq", tb)])
                S.op("act", lambda e, n=n, tb=tb: e.activation(
                    rstd[0:n, tb:tb + 1], ssq[0:n, tb:tb + 1], AF.Sqrt, bias=EPS, scale=1.0 / D),
                    r=[("ssq", tb)], w=[("rstd", tb)])
                S.op("dve", lambda e, n=n, tb=tb: e.reciprocal(rstd[0:n, tb:tb + 1], rstd[0:n, tb:tb + 1]),
                     r=[("rstd", tb)], w=[("rstd", tb)])
                S.op("dve", lambda e, xin=xin, n=n, tb=tb: e.tensor_scalar(
                    xn[0:n, :], xin, rstd[0:n, tb:tb + 1], None, ALU.mult), r=[("x", tb), ("rstd", tb)], w=["xn"])
                for kq in range(4):
                    bank = kq % 2
                    for kk in range(4):
                        k = kq * 4 + kk
                        S.op("pe", lambda e, k=k, kk=kk, n=n, bank=bank: e.transpose(
                            ps[:, bank, kk * 128:kk * 128 + n], xn[0:n, k * 128:(k + 1) * 128], ident[0:n, 0:n]),
                            r=["xn", "ident"], w=[("ps", bank)])
                    for kk in range(4):
                        k = kq * 4 + kk
                        S.op("act", lambda e, k=k, kk=kk, n=n, tb=tb, bank=bank, r_=r_: e.activation(
                            hT[:, k, tb * 128:tb * 128 + n], ps[:, bank, kk * 128:kk * 128 + n], AF.Identity,
                            bias=modc[:, SH_OFF[nidx] + k, r_:r_ + 1], scale=effs[:, nidx, k, r_:r_ + 1]),
                            r=[("ps", bank), "modc", "effs"], w=[("hT", tb)])

        def ffn(wg, wu, wd, with_sample):
            tiles = tile_cols(with_sample)
            tbs = tb_list(with_sample)
            pair_i = 0
            dq = 0
            for g0 in range(0, NFF, GC):
                nch = min(GC, NFF - g0)
                for cp in range(0, nch, 2):
                    c0 = g0 + cp
                    bi = pair_i % 2
                    pair_i += 1
                    S.dma("pool", lambda e, bi=bi, c0=c0: e.dma_start(
                        out=wgb[bi][:], in_=wg[:, c0 * 128:c0 * 128 + 256].rearrange("(k p) c -> p k c", p=128)),
                        w=[("wgb", bi)])
                    S.dma("pool", lambda e, bi=bi, c0=c0: e.dma_start(
                        out=wub[bi][:], in_=wu[:, c0 * 128:c0 * 128 + 256].rearrange("(k p) c -> p k c", p=128)),
                        w=[("wub", bi)])
                    for cc in range(2):
                        ci = cp + cc
                        for ti, (t0, tn) in enumerate(tiles):
                            hkeys = [("hT", tb) for tb in range(t0 // 128, t0 // 128 + max(1, tn // 128))]
                            if bg:
                                bg.pop(0)()
                            bg_, bu = (0, 1) if (ci * 3 + ti) % 2 == 0 else (2, 3)
                            for k in range(16):
                                S.op("pe", lambda e, k=k, cc=cc, t0=t0, tn=tn, bg_=bg_, bi=bi: e.matmul(
                                    ps[:, bg_, 0:tn], wgb[bi][:, k, cc * 128:(cc + 1) * 128], hT[:, k, t0:t0 + tn],
                                    start=(k == 0), stop=(k == 15)), r=[("wgb", bi)] + hkeys, w=[("ps", bg_)])
                            for k in range(16):
                                S.op("pe", lambda e, k=k, cc=cc, t0=t0, tn=tn, bu=bu, bi=bi: e.matmul(
                                    ps[:, bu, 0:tn], wub[bi][:, k, cc * 128:(cc + 1) * 128], hT[:, k, t0:t0 + tn],
                                    start=(k == 0), stop=(k == 15)), r=[("wub", bi)] + hkeys, w=[("ps", bu)])
                            sg = sgt[0]
                            sgk = ("sgt", 0)
                            S.op("act", lambda e, sg=sg, tn=tn, bg_=bg_: e.activation(sg[:, 0:tn], ps[:, bg_, 0:tn], AF.Silu),
                                 r=[("ps", bg_)], w=[sgk])
                            S.op("dve", lambda e, sg=sg, tn=tn, bu=bu, ci=ci, t0=t0: e.tensor_tensor(
                                aT[:, ci, t0:t0 + tn], sg[:, 0:tn], ps[:, bu, 0:tn], ALU.mult),
                                r=[sgk, ("ps", bu)], w=[("aT", ci, ti)])
                for cg in range(4):
                    wi = dq % 2
                    dq += 1
                    S.dma("pool", lambda e, wi=wi, g0=g0, nch=nch, cg=cg: e.dma_start(
                        out=wdb[wi][:, 0:nch, :],
                        in_=wd[g0 * 128:(g0 + nch) * 128, cg * 512:(cg + 1) * 512].rearrange("(k p) c -> p k c", p=128)),
                        w=[("wdb", wi)])
                    for bi_, (tb, n) in enumerate(tbs):
                        bank = 4 + (cg * len(tbs) + bi_) % 2
                        ti = min(tb // 4, 2)
                        for ci in range(nch):
                            S.op("pe", lambda e, ci=ci, tb=tb, n=n, bank=bank, wi=wi: e.matmul(
                                ps[0:n, bank, :], aT[:, ci, tb * 128:tb * 128 + n], wdb[wi][:, ci, :],
                                start=(ci == 0), stop=(ci == nch - 1)),
                                r=[("aT", ci, ti), ("wdb", wi)], w=[("ps", bank)])
                        tmp = dtmp[bank - 4]
                        gt, gk = (gP, "gP") if tb < NTB else (gS, "gS")
                        S.op("dve", lambda e, tmp=tmp, n=n, bank=bank, gt=gt, cg=cg: e.tensor_tensor(
                            tmp[0:n, :], ps[0:n, bank, :], gt[0:n, cg * 512:(cg + 1) * 512], ALU.mult),
                            r=[("ps", bank), gk], w=[("dtmp", bank)])
                        xo = xrow(tb, n)[:, cg * 512:(cg + 1) * 512]
                        S.op("dve", lambda e, tmp=tmp, n=n, xo=xo: e.tensor_tensor(xo, xo, tmp[0:n, :], ALU.add),
                             r=[("dtmp", bank), ("x", tb)], w=[("x", tb)])

        S.op("act", lambda e: e.activation(lbc[:], cols[:, R_LB:R_LB + 8], AF.Sigmoid), r=["cols"], w=["lbc"])
        S.op("dve", lambda e: e.tensor_scalar(oml[:], lbc[:], -1.0, 1.0, ALU.mult, ALU.add), r=["lbc"], w=["oml"])
        S.op("dve", lambda e: e.memset(Sst[:], 0.0), w=["Sst"])
        S.dma("sp", lambda e: e.dma_start(out=Sss[:], in_=st_in.rearrange("h k v -> k h v")), w=["Sss"])

        def kv_proj(half, with_sample):
            tbs = tb_list(with_sample)
            bufs4 = [wgb[0], wgb[1], wub[0], wub[1]]
            keys4 = [("wgb", 0), ("wgb", 1), ("wub", 0), ("wub", 1)]
            for cgi in range(8):
                c0 = 5120 + cgi * 256
                buf, bk = bufs4[cgi % 4], keys4[cgi % 4]
                S.dma("pool", lambda e, buf=buf, c0=c0: e.dma_start(
                    out=buf[:], in_=w_in[:, c0:c0 + 256].rearrange("(k p) c -> p k c", p=128)), w=[bk])
                for bi_, (tb, n) in enumerate(tbs):
                    bank = 4 + (cgi * len(tbs) + bi_) % 2
                    for k in range(16):
                        S.op("pe", lambda e, k=k, tb=tb, n=n, bank=bank, buf=buf: e.matmul(
                            ps[0:n, bank, 0:256], hT[:, k, tb * 128:tb * 128 + n], buf[:, k, :],
                            start=(k == 0), stop=(k == 15)), r=[("hT", tb), bk], w=[("ps", bank)])
                    tmp = dtmp[bank - 4]
                    S.op("act", lambda e, tmp=tmp, n=n, bank=bank: e.activation(tmp[0:n, 0:256], ps[0:n, bank, 0:256], AF.Copy),
                         r=[("ps", bank)], w=[("dtmp", bank)])
                    isk = cgi < 4
                    cc0 = (cgi % 4) * 256
                    if tb < NTB:
                        dst = (k_p if isk else v_p)[tb * 128:(tb + 1) * 128, cc0:cc0 + 256]
                    else:
                        dst = (k_s if isk else v_s)[:, cc0:cc0 + 256]
                    S.dma("sp", lambda e, dst=dst, tmp=tmp, n=n: e.dma_start(out=dst, in_=tmp[0:n, 0:256]), r=[("dtmp", bank)])

        HS = 128.0 ** -0.5
        oT = U[:, 0:8256].bitcast(BF16).rearrange("p (k t) -> p k t", k=16)
        _uo = [8256]

        def ualloc_f(n):
            a_ = _uo[0]
            _uo[0] += n
            assert _uo[0] <= NTB * D
            return U[:, a_:a_ + n]

        def ualloc_b(ncols):
            n = (ncols + 1) // 2
            a_ = _uo[0]
            _uo[0] += n
            assert _uo[0] <= NTB * D
            return U[:, a_:a_ + n].bitcast(BF16)[:, 0:ncols]

        fbT, bbT, b2T, ebT, nbT, sgT_ = [ualloc_f(512) for _ in range(6)]
        qd, kiv = ualloc_b(512), ualloc_b(512)
        ke = ualloc_f(128)
        aTm = ualloc_b(128)
        osq = ualloc_b(512)
        orstd = ualloc_f(512)
        onb = ualloc_f(512)
        Sbf = ualloc_b(128)
        keB = ualloc_f(128)
        aTmB = ualloc_b(128)
        qT = ualloc_b(1032)
        kTo = ualloc_b(1032)
        kTp = ualloc_b(1024)
        Vo = ualloc_b(1024).rearrange("p (t d) -> p t d", t=8)
        Vp = ualloc_b(1024).rearrange("p (t d) -> p t d", t=8)
        e_s = [dtmp[0], dtmp[1]]
        lfl = sgt[0][:].bitcast(BF16)
        l_s = [lfl[:, 0:512], lfl[:, 512:1024]]
        aTf = aT[:].rearrange("p g t -> p (g t)")
        R32 = [aTf[:, 0:1024].bitcast(F32), aTf[:, 1024:2048].bitcast(F32)]
        Rbf = [aTf[:, 2048:2560], aTf[:, 2560:3072]]
        ATs = [aTf[:, 3072:3584], aTf[:, 3584:4096]]
        w0f = wdb[0][:].rearrange("p g t -> p (g t)")
        qpad = w0f[:, 0:512].rearrange("p (h m) -> p h m", h=8)
        kTn = w0f[:, 512:576].rearrange("p (h m) -> p h m", h=8)
        Vn = w0f[:, 576:1600]
        Mj = wdb[1]
        W7 = [wgb[0][:], wgb[1][:], wub[0][:], wub[1][:],
              gP[:].bitcast(BF16).rearrange("p (k c) -> p k c", k=16),
              gS[:].bitcast(BF16).rearrange("p (k c) -> p k c", k=16),
              xn[:].bitcast(BF16).rearrange("p (k c) -> p k c", k=16)]
        W7K = [("W7", i) for i in range(7)]
        TYPE_BASE = [0, 1024, 2048, 3072, 4096, 5120, 6144]

        def proj_fm(dst_bank, wi, cc, t0, tn):
            hkeys = [("hT", tb) for tb in range(t0 // 128, t0 // 128 + max(1, tn // 128))]
            for k in range(16):
                S.op("pe", lambda e, k=k: e.matmul(
                    ps[:, dst_bank, 0:tn], W7[wi][:, k, cc * 128:(cc + 1) * 128], hT[:, k, t0:t0 + tn],
                    start=(k == 0), stop=(k == 15)), r=[W7K[wi]] + hkeys, w=[("ps", dst_bank)])

        def proj_tm(dst_bank, wi, cc, tb, n):
            for k in range(16):
                S.op("pe", lambda e, k=k: e.matmul(
                    ps[0:n, dst_bank, 0:128], hT[:, k, tb * 128:tb * 128 + n], W7[wi][:, k, cc * 128:(cc + 1) * 128],
                    start=(k == 0), stop=(k == 15)), r=[W7K[wi], ("hT", tb)], w=[("ps", dst_bank)])

        def head_norm_out(obank, C, dst, gcol, gate, ssbank=6):
            S.op("act", lambda e: e.activation(osq[:, 0:C], ps[:, obank, 0:C], AF.Square), r=[("ps", obank)], w=["osq"])
            S.op("pe", lambda e: e.matmul(ps[:, ssbank, 0:C], ones_b[:], osq[:, 0:C], start=True, stop=True),
                 r=["ones_b", "osq"], w=[("ps", ssbank)])
            S.op("act", lambda e: e.activation(orstd[:, 0:C], ps[:, ssbank, 0:C], AF.Sqrt, bias=EPS, scale=1.0 / 128),
                 r=[("ps", ssbank)], w=["orstd"])
            S.op("dve", lambda e: e.reciprocal(orstd[:, 0:C], orstd[:, 0:C]), r=["orstd"], w=["orstd"])
            S.op("dve", lambda e: e.tensor_tensor(onb[:, 0:C], ps[:, obank, 0:C], orstd[:, 0:C], ALU.mult),
                 r=[("ps", obank), "orstd"], w=["onb"])
            if gate is None:
                S.op("dve", lambda e: e.tensor_scalar(dst, onb[:, 0:C], gcol, None, ALU.mult), r=["onb", "cols"], w=["oT"])
            else:
                S.op("dve", lambda e: e.scalar_tensor_tensor(dst, onb[:, 0:C], gcol, gate, ALU.mult, ALU.mult),
                     r=["onb", "cols", "sgT"], w=["oT"])

        ke2, kend2, va2, aTm2 = [ke, keB], [kend_tok, kend_tokB], [va_tok, va_tokB], [aTm, aTmB]

        def hgrn_head(ws, h, cc, state_only=False):
            for ti, (t0, tn) in enumerate(tile_cols(ws)):
                C = min(128, tn)
                fb, bb, b2, eb, nb, sg = fbT[:, 0:tn], bbT[:, 0:tn], b2T[:, 0:tn], ebT[:, 0:tn], nbT[:, 0:tn], sgT_[:, 0:tn]
                proj_fm(0, 1, cc, t0, tn)
                S.op("act", lambda e, fb=fb, tn=tn: e.activation(fb, ps[:, 0, 0:tn], AF.Sigmoid), r=[("ps", 0)], w=["fbT"])
                S.op("dve", lambda e, fb=fb: e.tensor_scalar(fb, fb, oml[:, h:h + 1], lbc[:, h:h + 1], ALU.mult, ALU.add),
                     r=["fbT", "oml", "lbc"], w=["fbT"])
                S.op("act", lambda e, fb=fb, bb=bb: e.activation(bb, fb, AF.Ln), r=["fbT"], w=["bbT"])
                for c in range(max(1, tn // 128)):
                    S.op("dve", lambda e, c=c, C=C, bb=bb, b2=b2: e.tensor_tensor_scan(
                        b2[:, c * C:(c + 1) * C], ones_f[:, 0:C], bb[:, c * C:(c + 1) * C], 0.0, ALU.mult, ALU.add),
                        r=["bbT", "ones_f"], w=["b2T"])
                S.op("act", lambda e, b2=b2, eb=eb: e.activation(eb, b2, AF.Exp), r=["b2T"], w=["ebT"])
                S.op("act", lambda e, b2=b2, nb=nb: e.activation(nb, b2, AF.Exp, scale=-1.0), r=["b2T"], w=["nbT"])
                S.op("dve", lambda e, fb=fb: e.tensor_scalar(fb, fb, -1.0, 1.0, ALU.mult, ALU.add), r=["fbT"], w=["fbT"])
                S.op("dve", lambda e, fb=fb, nb=nb: e.tensor_tensor(nb, fb, nb, ALU.mult), r=["fbT", "nbT"], w=["nbT"])
                if not state_only:
                    S.op("dve", lambda e, nb=nb, tn=tn: e.tensor_copy(kiv[:, 0:tn], nb), r=["nbT"], w=["kiv"])
                    proj_fm(7, 0, cc, t0, tn)
                    S.op("dve", lambda e, eb=eb, tn=tn: e.scalar_tensor_tensor(
                        qd[:, 0:tn], ps[:, 7, 0:tn], HS, eb, ALU.mult, ALU.mult), r=[("ps", 7), "ebT"], w=["qd"])
                    proj_fm(0, 3, cc, t0, tn)
                    S.op("act", lambda e, sg=sg, tn=tn: e.activation(sg, ps[:, 0, 0:tn], AF.Silu), r=[("ps", 0)], w=["sgT"])
                nchk = max(1, tn // 128)

                def P1(c, C=C, nb=nb, eb=eb, t0=t0):
                    tb = t0 // 128 + c
                    lo, hi = c * C, (c + 1) * C
                    last = hi - 1
                    rb = c % 2
                    ke_, kt_, vt_, am_ = ke2[rb], kend2[rb], va2[rb], aTm2[rb]
                    kb_, vb_ = (2, 6)[rb], (3, 7)[rb]
                    S.op("dve", lambda e: e.tensor_scalar(
                        ke_[:, 0:C], nb[:, lo:hi], eb[:, last:last + 1], None, ALU.mult), r=["nbT", "ebT"], w=[("ke", rb)])
                    S.op("pe", lambda e: e.transpose(ps[0:C, kb_, 0:128], ke_[:, 0:C], ident[:]),
                         r=[("ke", rb), "ident"], w=[("ps", kb_)])
                    S.op("act", lambda e: e.activation(kt_[0:C, :], ps[0:C, kb_, 0:128], AF.Copy),
                         r=[("ps", kb_)], w=[("kend", rb)])
                    proj_tm(vb_, 2, cc, tb, C)
                    S.op("act", lambda e: e.activation(vt_[0:C, :], ps[0:C, vb_, 0:128], AF.Copy),
                         r=[("ps", vb_)], w=[("va", rb)])
                    if not state_only:
                        S.op("pe", lambda e: e.matmul(ps[0:C, 4, 0:C], kiv[:, lo:hi], qd[:, lo:hi], start=True, stop=True),
                             r=["kiv", "qd"], w=[("ps", 4)])
                        S.op("dve", lambda e: e.tensor_tensor(am_[0:C, 0:C], ps[0:C, 4, 0:C], mle[0:C, 0:C], ALU.mult),
                             r=[("ps", 4), "mle"], w=[("aTm", rb)])

                def P2(c, C=C, nb=nb, eb=eb, t0=t0, sg=sg):
                    tb = t0 // 128 + c
                    St = Sst if tb < NTB else Sss
                    sk = "Sst" if tb < NTB else "Sss"
                    lo, hi = c * C, (c + 1) * C
                    last = hi - 1
                    rb = c % 2
                    kt_, vt_, am_ = kend2[rb], va2[rb], aTm2[rb]
                    if not state_only:
                        S.op("act", lambda e: e.activation(Sbf[:, :], St[:, h, :], AF.Copy), r=[sk], w=["Sbf"])
                        S.op("pe", lambda e: e.matmul(ps[:, 5, 0:C], vt_[0:C, :], am_[0:C, 0:C], start=True, stop=False),
                             r=[("va", rb), ("aTm", rb)], w=[("ps", 5)])
                        S.op("pe", lambda e: e.matmul(ps[:, 5, 0:C], Sbf[:, :], qd[:, lo:hi], start=False, stop=True),
                             r=["Sbf", "qd"], w=[("ps", 5)])
                    S.op("pe", lambda e: e.matmul(ps[:, 1, 0:128], kt_[0:C, :], vt_[0:C, :], start=True, stop=True),
                         r=[("kend", rb), ("va", rb)], w=[("ps", 1)])
                    S.op("dve", lambda e: e.scalar_tensor_tensor(
                        St[:, h, :], St[:, h, :], eb[:, last:last + 1], ps[:, 1, 0:128], ALU.mult, ALU.add),
                        r=[sk, "ebT", ("ps", 1)], w=[sk])
                    if not state_only:
                        head_norm_out(5, C, oT[:, h, t0 + lo:t0 + hi], cols[:, R_GA + h:R_GA + h + 1], sg[:, lo:hi], ssbank=0)

                P1(0)
                for c in range(nchk):
                    if c + 1 < nchk:
                        P1(c + 1)
                    P2(c)

        def sb_head(half, ws, h, cc):
            for ti, (t0, tn) in enumerate(tile_cols(ws)):
                if half == 1:
                    proj_fm(4, 4, cc, t0, tn)
                    S.op("act", lambda e, t0=t0, tn=tn: e.activation(qT[:, t0:t0 + tn], ps[:, 4, 0:tn], AF.Copy, scale=HS),
                         r=[("ps", 4)], w=["qT"])
                proj_fm(5, 5, cc, t0, tn)
                S.op("act", lambda e, t0=t0, tn=tn: e.activation(kTo[:, t0:t0 + tn], ps[:, 5, 0:tn], AF.Copy),
                     r=[("ps", 5)], w=["kTo"])
            for tb, n in tb_list(ws):
                proj_tm(7, 6, cc, tb, n)
                dst = Vo[:, tb, :] if tb < NTB else Vn[0:n, h * 128:(h + 1) * 128]
                S.op("act", lambda e, n=n, dst=dst: e.activation(dst, ps[0:n, 7, 0:128], AF.Copy), r=[("ps", 7)],
                     w=["Vo" if tb < NTB else "Vn"])
            if ws:
                S.op("dve", lambda e: e.tensor_copy(qpad[:, h, 8 * h:8 * h + 8], qT[:, T:T + 8]), r=["qT"], w=["qpad"])
                S.op("dve", lambda e: e.tensor_copy(kTn[:, h, :], kTo[:, T:T + 8]), r=["kTo"], w=["kTn"])
            if half == 0:
                S.dma("sp", lambda e: e.dma_start(out=kTp_d[h], in_=kTo[:, 0:T]), r=["kTo"], w=[("kTp_d", h)])
                S.dma("sp", lambda e: e.dma_start(out=Vp_d[h], in_=Vo.rearrange("p t d -> p (t d)")), r=["Vo"], w=[("Vp_d", h)])
                return
            else:
                S.dma("sp", lambda e: e.dma_start(out=kTp[:, :], in_=kTp_d[h]), r=[("kTp_d", h)], w=["kTp"])
                S.dma("sp", lambda e: e.dma_start(out=Vp.rearrange("p t d -> p (t d)"), in_=Vp_d[h]), r=[("Vp_d", h)], w=["Vp"])
            lists = []
            for r_ in range(2):
                l_ = [("p", i) for i in range(8)] if half == 1 else []
                l_ += [("o", ob) for ob in range(4 * r_ + 4)]
                lists.append(l_)
            for step in range(max(len(l_) for l_ in lists)):
                for s in range(2):
                    if step >= len(lists[s]):
                        continue
                    kind, bi_ = lists[s][-(1 + step)]
                    nsteps = len(lists[s])
                    if kind == "p":
                        kblk, vblk, kk, vk, j = kTp[:, bi_ * 128:(bi_ + 1) * 128], Vp[:, bi_, :], "kTp", "Vp", -1
                        bias = bsbp[:, h:h + 1]
                    else:
                        kblk, vblk, kk, vk, j = kTo[:, bi_ * 128:(bi_ + 1) * 128], Vo[:, bi_, :], "kTo", "Vo", bi_ - 4 * s
                        bias = bsbc[:, h:h + 1]
                    q = qT[:, 512 * s:512 * s + 512]
                    S.op("pe", lambda e, s=s, kblk=kblk, q=q: e.matmul(ps[:, s, :], kblk, q, start=True, stop=False),
                         r=[kk, "qT"], w=[("ps", s)])
                    S.op("act", lambda e, s=s, bias=bias: e.activation(e_s[s][:, :], ps[:, s, :], AF.Exp, bias=bias), r=[("ps", s), "bsbc", "bsbp"], w=[("e", s)])
                    S.op("act", lambda e, s=s: e.activation(l_s[s], e_s[s][:, :], AF.Ln, bias=1.0), r=[("e", s)], w=[("l", s)])
                    if j >= 0:
                        S.op("dve", lambda e, s=s, j=j: e.tensor_tensor(l_s[s], l_s[s], Mj[:, j, :], ALU.mult),
                             r=[("l", s), "Mj"], w=[("l", s)])
                    S.op("pe", lambda e, s=s, step=step: e.matmul(ps[:, s, :], ntri[:], l_s[s], start=False, stop=(step == 0)),
                         r=["ntri", ("l", s)], w=[("ps", s)])
                    if step > 0:
                        S.op("pe", lambda e, s=s: e.matmul(ps[:, s, :], nones[:], Rbf[s], start=False, stop=True),
                             r=["nones", ("Rbf", s)], w=[("ps", s)])
                    S.op("act", lambda e, s=s, bias=bias: e.activation(ATs[s], ps[:, s, :], AF.Exp, bias=bias), r=[("ps", s), "bsbc", "bsbp"], w=[("AT", s)])
                    if j >= 0:
                        S.op("dve", lambda e, s=s, j=j: e.tensor_tensor(ATs[s], ATs[s], Mj[:, j, :], ALU.mult),
                             r=[("AT", s), "Mj"], w=[("AT", s)])
                    S.op("pe", lambda e, s=s, vblk=vblk, step=step, nsteps=nsteps: e.matmul(
                        ps[:, 2 + s, :], vblk, ATs[s], start=(step == 0), stop=(step == nsteps - 1)),
                        r=[vk, ("AT", s)], w=[("ps", 2 + s)])
                    if step < nsteps - 1:
                        if step == 0:
                            S.op("dve", lambda e, s=s: e.tensor_copy(R32[s], l_s[s]), r=[("l", s)], w=[("R32", s)])
                        else:
                            S.op("dve", lambda e, s=s: e.tensor_tensor(R32[s], R32[s], l_s[s], ALU.add), r=[("l", s), ("R32", s)], w=[("R32", s)])
                        S.op("act", lambda e, s=s: e.activation(Rbf[s], R32[s], AF.Copy), r=[("R32", s)], w=[("Rbf", s)])
            for s in range(2):
                head_norm_out(2 + s, 512, oT[:, 8 + h, 512 * s:512 * s + 512], cols[:, R_GB + h:R_GB + h + 1], None)

        def sample_attention():
            base = 8256
            Kpg = [U[:, base + i * 1024:base + (i + 1) * 1024] for i in range(2)] + [U[:, base + 6144:base + 7168]]
            Vpg = [U[:, base + 2048 + i * 2048:base + 2048 + (i + 1) * 2048].bitcast(BF16).rearrange("p (s c) -> p s c", s=4)
                   for i in range(2)]
            KT = wgb[0][:].rearrange("p k c -> p (k c)").rearrange("p (h s) -> p h s", h=8)
            f0 = wub[0][:].rearrange("p k c -> p (k c)").bitcast(F32)
            f1 = wub[1][:].rearrange("p k c -> p (k c)").bitcast(F32)
            eB, lB, lbB, cB = f0[0:64, 0:512], f0[0:64, 512:1024], f0[0:64, 1024:1536], f0[0:64, 1536:2048]
            AB, ones5 = f1[0:64, 0:512], f1[0:64, 512:1024]
            AT4 = f1[:, 1024:1152].bitcast(BF16).rearrange("p (s m) -> p s m", s=4)
            ATn = f1[:, 1152:1184].bitcast(BF16)
            sm = f1[0:64, 1184:1200]
            mnew = f1[0:64, 1200:1208]
            o64 = f1[0:64, 1280:1408]
            tmpB = gP[0:64, 0:1024]
            mbig = gP[0:64, 1024:2048]
            gSi = gS[:].bitcast(I32)
            pt_i, idx_i = gSi[:, 0:128], gSi[:, 128:256]
            pt_f = gS[:, 256:384]

            S.dma("sp", lambda e: e.dma_start(out=pt_i, in_=ptab.partition_broadcast(128)), w=["pt_i"])
            S.dma("sp", lambda e: e.dma_start(out=sm[:, 2:3], in_=b64), w=["sm_b"])
            S.dma("sp", lambda e: e.dma_start(out=mnew, in_=cst2[:, 0:8]), w=["mnew"])
            S.dma("sp", lambda e: e.dma_start(out=mbig, in_=cst2[:, 8:1032]), w=["mbig"])
            S.op("dve", lambda e: e.tensor_copy(pt_f, pt_i), r=["pt_i"], w=["pt_f"])
            S.op("dve", lambda e: e.tensor_scalar(pt_f, pt_f, 128.0, pid_f[:, 0:1], ALU.mult, ALU.add), r=["pt_f", "pid_f"], w=["pt_f"])
            S.op("dve", lambda e: e.tensor_copy(idx_i, pt_f), r=["pt_f"], w=["idx_i"])
            S.op("dve", lambda e: e.memset(ones5, 1.0), w=["ones5"])
            bcol = sm[:, 2:3]

            def ew_chain(zbank, n, mask, first):
                S.op("act", lambda e: e.activation(eB[:, 0:n], ps[0:64, zbank, 0:n], AF.Exp, bias=bcol), r=[("ps", zbank), "sm_b"], w=["eB"])
                S.op("act", lambda e: e.activation(lB[:, 0:n], eB[:, 0:n], AF.Ln, bias=1.0), r=["eB"], w=["lB"])
                if mask is not None:
                    S.op("dve", lambda e: e.tensor_tensor(lB[:, 0:n], lB[:, 0:n], mask, ALU.mult), r=["lB", "mnew"], w=["lB"])
                S.op("dve", lambda e: e.scalar_tensor_tensor(lbB[:, 0:n], ps[0:64, zbank, 0:n], bcol, lB[:, 0:n], ALU.add, ALU.subtract),
                     r=[("ps", zbank), "sm_b", "lB"], w=["lbB"])
                S.op("dve", lambda e: e.tensor_tensor_scan(cB[:, 0:n], ones5[:, 0:n], lB[:, 0:n], 0.0, ALU.mult, ALU.add),
                     r=["lB", "ones5"], w=["cB"])
                if first:
                    S.op("dve", lambda e: e.tensor_copy(sm[:, 0:1], cB[:, n - 1:n]), r=["cB"], w=["carry"])
                else:
                    S.op("dve", lambda e: e.tensor_tensor(sm[:, 0:1], sm[:, 0:1], cB[:, n - 1:n], ALU.add), r=["cB", "carry"], w=["carry"])
                S.op("dve", lambda e: e.tensor_scalar(sm[:, 1:2], sm[:, 0:1], -1.0, None, ALU.mult), r=["carry"], w=["negc"])
                S.op("dve", lambda e: e.tensor_tensor(lbB[:, 0:n], lbB[:, 0:n], cB[:, 0:n], ALU.add), r=["lbB", "cB"], w=["lbB"])
                S.op("act", lambda e: e.activation(AB[:, 0:n], lbB[:, 0:n], AF.Exp, bias=sm[:, 1:2]), r=["lbB", "negc"], w=["AB"])
                if mask is not None:
                    S.op("dve", lambda e: e.tensor_tensor(AB[:, 0:n], AB[:, 0:n], mask, ALU.mult), r=["AB", "mnew"], w=["AB"])

            for h in range(8):
                S.op("pe", lambda e, h=h: e.matmul(ps[0:64, 0, 0:8], qpad[:, h, :], kTn[:, h, :], start=(h == 0), stop=(h == 7)),
                     r=["qpad", "kTn"], w=[("ps", 0)])
            ew_chain(0, 8, mnew, True)
            S.op("pe", lambda e: e.transpose(ps[0:8, 1, 0:64], AB[:, 0:8], ident[0:64, 0:64]), r=["AB", "ident"], w=[("ps", 1)])
            S.op("act", lambda e: e.activation(ATn[0:8, :], ps[0:8, 1, 0:64], AF.Copy), r=[("ps", 1)], w=["ATn"])
            for hf in range(2):
                S.op("pe", lambda e, hf=hf: e.matmul(ps[0:64, 6 + hf, :], ATn[0:8, :], Vn[0:8, hf * 512:(hf + 1) * 512], start=True, stop=False),
                     r=["ATn", "Vn"], w=[("ps", 6 + hf)])
            NCH = NPAGES // 4
            KTs = [KT, wgb[1][:].rearrange("p k c -> p (k c)").rearrange("p (h s) -> p h s", h=8)]
            ZB = [0, 4]

            def prep(ci):
                ch = NCH - 1 - ci
                vb = Vpg[ci % 2]
                vk = ("Vpg", ci % 2)
                KTc = KTs[ci % 2]
                for sl in range(4):
                    pg = ch * 4 + sl
                    kb_ = Kpg[(ci * 4 + sl) % 3]
                    kk_ = ("Kpg", (ci * 4 + sl) % 3)
                    S.dma("pool", lambda e, kb_=kb_, pg=pg: e.indirect_dma_start(
                        out=kb_, out_offset=None, in_=cache_k[:, :],
                        in_offset=bass.IndirectOffsetOnAxis(ap=idx_i[:, pg:pg + 1], axis=0)), r=["idx_i"], w=[kk_])
                    S.dma("pool", lambda e, vb=vb, sl=sl, pg=pg: e.indirect_dma_start(
                        out=vb[:, sl, :], out_offset=None, in_=cache_v[:, :],
                        in_offset=bass.IndirectOffsetOnAxis(ap=idx_i[:, pg:pg + 1], axis=0)), r=["idx_i"], w=[(vk, sl)])
                    for hq in range(2):
                        bank = 2 + hq
                        for hh in range(4):
                            h = hq * 4 + hh
                            S.op("pe", lambda e, kb_=kb_, h=h, hh=hh, bank=bank: e.transpose(
                                ps[:, bank, hh * 128:(hh + 1) * 128], kb_[:, h * 128:(h + 1) * 128], ident[:]),
                                r=[kk_, "ident"], w=[("ps", bank)])
                        dst = KTc[:, hq * 4:(hq + 1) * 4, sl * 128:(sl + 1) * 128]
                        srcp = ps[:, bank, :].rearrange("p (h s) -> p h s", h=4)
                        if hq == 0:
                            S.op("act", lambda e, dst=dst, srcp=srcp: e.activation(dst, srcp, AF.Copy), r=[("ps", bank)], w=[("KT", ci % 2, sl)])
                        else:
                            S.op("dve", lambda e, dst=dst, srcp=srcp: e.tensor_copy(dst, srcp), r=[("ps", bank)], w=[("KT", ci % 2, sl)])
                zb = ZB[ci % 2]
                for h in range(8):
                    S.op("pe", lambda e, h=h, zb=zb, KTc=KTc: e.matmul(ps[0:64, zb, :], qpad[:, h, :], KTc[:, h, :], start=(h == 0), stop=(h == 7)),
                         r=["qpad"] + [("KT", ci % 2, sl) for sl in range(4)], w=[("ps", zb)])

            prep(0)
            for ci in range(NCH):
                if ci + 1 < NCH:
                    prep(ci + 1)
                vb = Vpg[ci % 2]
                vk = ("Vpg", ci % 2)
                ew_chain(ZB[ci % 2], 512, None, False)
                for sl in range(4):
                    S.op("pe", lambda e, sl=sl: e.transpose(ps[:, 1, sl * 64:(sl + 1) * 64], AB[:, sl * 128:(sl + 1) * 128], ident[0:64, 0:64]),
                         r=["AB", "ident"], w=[("ps", 1)])
                S.op("act", lambda e: e.activation(AT4[:, :, :], ps[:, 1, 0:256].rearrange("p (s m) -> p s m", s=4), AF.Copy),
                     r=[("ps", 1)], w=["AT4"])
                for sl in range(4):
                    for hf in range(2):
                        S.op("pe", lambda e, sl=sl, hf=hf, vb=vb, ci=ci: e.matmul(
                            ps[0:64, 6 + hf, :], AT4[:, sl, :], vb[:, sl, hf * 512:(hf + 1) * 512],
                            start=False, stop=(ci == NCH - 1 and sl == 3)), r=["AT4", (vk, sl)], w=[("ps", 6 + hf)])
            for hf in range(2):
                S.op("dve", lambda e, hf=hf: e.tensor_tensor(tmpB[:, hf * 512:(hf + 1) * 512], ps[0:64, 6 + hf, :],
                                                             mbig[:, hf * 512:(hf + 1) * 512], ALU.mult),
                     r=[("ps", 6 + hf), "mbig"], w=["tmpB"])
            S.op("dve", lambda e: e.tensor_reduce(o64, tmpB.rearrange("p (h d) -> p d h", h=8), mybir.AxisListType.X, ALU.add),
                 r=["tmpB"], w=["o64"])
            S.op("act", lambda e: e.activation(tmpB[:, 0:128], o64, AF.Square, accum_out=sm[:, 3:4]), r=["o64"], w=["tmpB", "ssq64"])
            S.op("act", lambda e: e.activation(sm[:, 4:5], sm[:, 3:4], AF.Sqrt, bias=EPS, scale=1.0 / 128), r=["ssq64"], w=["rstd64"])
            S.op("dve", lambda e: e.reciprocal(sm[:, 4:5], sm[:, 4:5]), r=["rstd64"], w=["rstd64"])
            S.op("dve", lambda e: e.tensor_scalar(o64, o64, sm[:, 4:5], None, ALU.mult), r=["o64", "rstd64"], w=["o64"])
            S.op("pe", lambda e: e.transpose(ps[:, 1, 0:64], o64, ident[0:64, 0:64]), r=["o64", "ident"], w=[("ps", 1)])
            for h in range(8):
                S.op("dve", lambda e, h=h: e.tensor_scalar(oT[:, 8 + h, T:T + 8], ps[:, 1, 8 * h:8 * h + 8],
                                                           cols[:, R_GB + h:R_GB + h + 1], None, ALU.mult),
                     r=[("ps", 1), "cols"], w=["oT"])

        def mixer(half, ws):
            if half == 0:
                for hp in range(4):
                    for ty in (1, 2, 5, 6):
                        c0 = TYPE_BASE[ty] + hp * 256
                        S.dma("pool", lambda e, ty=ty, c0=c0: e.dma_start(
                            out=W7[ty], in_=w_in[:, c0:c0 + 256].rearrange("(k p) c -> p k c", p=128)), w=[W7K[ty]])
                    for cc in range(2):
                        h = hp * 2 + cc
                        hgrn_head(False, h, cc, state_only=True)
                        sb_head(0, False, h, cc)
                S.op("dve", lambda e: e.tensor_scalar(Sst[:].rearrange("p h v -> p (h v)"), Sst[:].rearrange("p h v -> p (h v)"),
                                                      flagc[:, 0:1], None, ALU.mult), r=["Sst", "flagc"], w=["Sst"])
                return
            S.dma("pool", lambda e: e.dma_start(out=Mj[:], in_=cst[:, 256:2304].rearrange("p (j t) -> p j t", j=4)), w=["Mj"])
            if ws:
                S.op("dve", lambda e: e.memset(w0f[:, 0:576], 0.0), w=["qpad", "kTn"])
            for hp in range(4):
                for ty in range(7):
                    c0 = TYPE_BASE[ty] + hp * 256
                    S.dma("pool", lambda e, ty=ty, c0=c0: e.dma_start(
                        out=W7[ty], in_=w_in[:, c0:c0 + 256].rearrange("(k p) c -> p k c", p=128)), w=[W7K[ty]])
                for cc in range(2):
                    h = hp * 2 + cc
                    hgrn_head(ws, h, cc)
                    sb_head(half, ws, h, cc)
            if ws:
                S.barrier()
                sample_attention()

        def make_bcast(dst, dkey, colfn, mult):
            for k in range(16):
                dg = diag[k % 2]
                S.op("dve", lambda e, dg=dg, k=k: e.tensor_scalar(dg[:], ident[:], colfn(k), mult, ALU.mult, ALU.mult),
                     r=["ident", "modc", "effs"], w=[("diag", k % 2)])
                S.op("pe", lambda e, dg=dg, k=k: e.matmul(
                    ps[:, 4 + k // 4, (k % 4) * 128:(k % 4 + 1) * 128], ones_f[:], dg[:], start=True, stop=True),
                    r=[("diag", k % 2), "ones_f"], w=[("ps", 4 + k // 4)])
            for b_ in range(4):
                S.op("act", lambda e, b_=b_: e.activation(dst[:, b_ * 512:(b_ + 1) * 512], ps[:, 4 + b_, :], AF.Copy),
                     r=[("ps", 4 + b_)], w=[dkey])

        def wout_residual(ws):
            tbs = tb_list(ws)
            for cg in range(4):
                pair = (wgb, "wgb") if cg % 2 == 0 else (wub, "wub")
                for hf in range(2):
                    S.dma("pool", lambda e, pair=pair, hf=hf, cg=cg: e.dma_start(
                        out=pair[0][hf][:].rearrange("p k c -> p (k c)").rearrange("p (k c) -> p k c", k=8),
                        in_=w_out[hf * 1024:(hf + 1) * 1024, cg * 512:(cg + 1) * 512].rearrange("(k p) c -> p k c", p=128)),
                        w=[(pair[1], hf)])
                for bi_, (tb, n) in enumerate(tbs):
                    bank = 4 + (cg * len(tbs) + bi_) % 2
                    for k in range(16):
                        wv_ = pair[0][k // 8][:].rearrange("p k c -> p (k c)").rearrange("p (k c) -> p k c", k=8)
                        S.op("pe", lambda e, k=k, tb=tb, n=n, bank=bank, wv_=wv_: e.matmul(
                            ps[0:n, bank, :], hT[:, k, tb * 128:tb * 128 + n], wv_[:, k % 8, :],
                            start=(k == 0), stop=(k == 15)), r=[("hT", tb), (pair[1], k // 8)], w=[("ps", bank)])
                    tmp = dtmp[bank - 4]
                    gt, gk = (gP, "gP") if tb < NTB else (gS, "gS")
                    S.op("dve", lambda e, tmp=tmp, n=n, bank=bank, gt=gt, cg=cg: e.tensor_tensor(
                        tmp[0:n, :], ps[0:n, bank, :], gt[0:n, cg * 512:(cg + 1) * 512], ALU.mult),
                        r=[("ps", bank), gk], w=[("dtmp", bank)])
                    xo = xrow(tb, n)[:, cg * 512:(cg + 1) * 512]
                    S.op("dve", lambda e, tmp=tmp, n=n, xo=xo: e.tensor_tensor(xo, xo, tmp[0:n, :], ALU.add),
                         r=[("dtmp", bank), ("x", tb)], w=[("x", tb)])

        def final_out(half, ws):
            for r_ in ((0, 1) if ws else (0,)):
                make_bcast(gP, "gP", lambda k, r_=r_: effs[:, 3, k, r_:r_ + 1], 1.0)
                make_bcast(gS, "gS", lambda k, r_=r_: modc[:, SH_OFF[3] + k, r_:r_ + 1], 1.0)
                for tb, n in (tb_list(False) if r_ == 0 else [(NTB, 8)]):
                    xin = xrow(tb, n)
                    S.op("act", lambda e, xin=xin, n=n, tb=tb: e.activation(
                        xn[0:n, :], xin, AF.Square, accum_out=ssq[0:n, tb:tb + 1]), r=[("x", tb)], w=["xn", ("ssq", tb)])
                    S.op("act", lambda e, n=n, tb=tb: e.activation(
                        rstd[0:n, tb:tb + 1], ssq[0:n, tb:tb + 1], AF.Sqrt, bias=EPS, scale=1.0 / D),
                        r=[("ssq", tb)], w=[("rstd", tb)])
                    S.op("dve", lambda e, n=n, tb=tb: e.reciprocal(rstd[0:n, tb:tb + 1], rstd[0:n, tb:tb + 1]),
                         r=[("rstd", tb)], w=[("rstd", tb)])
                    S.op("dve", lambda e, xin=xin, n=n, tb=tb: e.scalar_tensor_tensor(
                        xn[0:n, :], xin, rstd[0:n, tb:tb + 1], gP[0:n, :], ALU.mult, ALU.mult),
                        r=[("x", tb), ("rstd", tb), "gP"], w=["xn"])
                    S.op("dve", lambda e, n=n: e.tensor_tensor(xn[0:n, :], xn[0:n, :], gS[0:n, :], ALU.add), r=["xn", "gS"], w=["xn"])
                    dst = y_p[tb * 128:(tb + 1) * 128, :] if tb < NTB else y_s
                    S.dma("sp", lambda e, dst=dst, n=n: e.dma_start(out=dst, in_=xn[0:n, :]), r=["xn"])

        for half in range(2):
            ws = (half == 1)
            for tb in range(NTB):
                S.dma("sp", lambda e, tb=tb, half=half: e.dma_start(
                    out=x_tok[:, tb, :], in_=x_all[half * T + tb * 128:half * T + (tb + 1) * 128, :]), w=[("x", tb)])
            if ws:
                S.dma("sp", lambda e: e.dma_start(out=xS[0:8, :], in_=x_smp), w=[("x", NTB)])
            make_gate(0, 0.5)
            norm_to_hT(0, ws)
            ffn(w1g, w1u, w1d, ws)
            if half == 0:
                finish_mod()
            norm_to_hT(1, ws)
            if half == 0:
                S.barrier()
                mixer(0, False)
                S.barrier()
                continue
            kv_proj(half, ws)
            for tb in range(NTB):
                S.dma("sp", lambda e, tb=tb: e.dma_start(out=xsp[tb * 128:(tb + 1) * 128, :], in_=x_tok[:, tb, :]), r=[("x", tb)], w=[("xsp", tb)])
            S.barrier()
            mixer(half, ws)
            S.barrier()
            for k in range(16):
                S.op("dve" if k % 2 == 0 else "act",
                     (lambda e, k=k: e.tensor_copy(hT[:, k, :], oT[:, k, :])) if k % 2 == 0 else
                     (lambda e, k=k: e.activation(hT[:, k, :], oT[:, k, :], AF.Copy)), r=["oT"], w=["hTall"])
            S.barrier()
            for tb in range(NTB):
                S.dma("sp", lambda e, tb=tb: e.dma_start(out=x_tok[:, tb, :], in_=xsp[tb * 128:(tb + 1) * 128, :]), w=[("x", tb)])
            make_gate(1, 1.0)
            wout_residual(ws)
            make_gate(2, 0.5)
            norm_to_hT(2, ws)
            ffn(w2g, w2u, w2d, ws)
            final_out(half, ws)
            S.barrier()
        S.dma("sp", lambda e: e.dma_start(out=s_p.rearrange("h k v -> k h v"), in_=Sst[:]), r=["Sst"])
        S.dma("sp", lambda e: e.dma_start(out=s_s.rearrange("h k v -> k h v"), in_=Sss[:]), r=["Sss"])
        S.final_waits("sp", S.all_dma_tokens())

        with nc.Block() as block:
            S.emit(block)
    return nc


def _prep_vecs(inp, b, i):
    v = np.zeros((384, 128), np.float32)
    v[R_CP:R_CP + 16] = inp["c_prompt"][b].reshape(16, 128)
    v[R_CS:R_CS + 16] = inp["c_sample"][i].reshape(16, 128)
    v[R_N1:R_N1 + 16] = inp["norm_ffn1"][0].reshape(16, 128)
    v[R_N2:R_N2 + 16] = inp["norm_mix"][0].reshape(16, 128)
    v[R_N3:R_N3 + 16] = inp["norm_ffn2"][0].reshape(16, 128)
    v[R_NF:R_NF + 16] = inp["norm_final"].reshape(16, 128)
    v[R_LB:R_LB + 8] = inp["lb_logits"][0].reshape(8, 128)
    v[R_GA:R_GA + 8] = inp["g_out_a"][0].reshape(8, 128)
    v[R_GB:R_GB + 8] = inp["g_out_b"][0].reshape(8, 128)
    v[R_BM:R_BM + 144] = inp["b_mod"][0].reshape(144, 128)
    v[R_BM + 144:R_BM + 176] = inp["b_final_mod"].reshape(32, 128)
    return v


def _consts():
    c = np.zeros((128, 2304), np.float32)
    p = np.arange(128)[:, None]
    f = np.arange(128)[None, :]
    c[:, 0:128] = -((p >= f).astype(np.float32))
    c[:, 128:256] = (p <= f)
    t = np.arange(512)[None, :]
    for j in range(4):
        c[:, 256 + j * 512:256 + (j + 1) * 512] = (t > 128 * j + p)
    return c


def _consts2():
    c = np.zeros((64, 1032), np.float32)
    p = np.arange(64)[:, None]
    c[:, 0:8] = (np.arange(8)[None, :] < (p % 8))
    c[:, 8:1032] = ((np.arange(1024)[None, :] // 128) == (p // 8))
    return c


_NC_CACHE = {}


def kernel(**inp):
    inp = {k: np.asarray(v) for k, v in inp.items()}
    lim = int(os.environ.get("MK_STAGE", "99"))
    if lim not in _NC_CACHE:
        _NC_CACHE[lim] = build_program(lim)
    nc = _NC_CACHE[lim]
    in_maps = []
    for i in range(NCORES):
        b = i // 2
        m = {
            "x_all": np.ascontiguousarray(np.concatenate([inp["x_prompt"][b, 0:T], inp["x_prompt"][b, (i % 2) * T:(i % 2 + 1) * T]], axis=0)),
            "flag": np.full((1, 1), float(i % 2), np.float32),
            "x_smp": np.ascontiguousarray(inp["x_sample"][i]),
            "vecs": _prep_vecs(inp, b, i),
            "w_mod": inp["w_mod"][0], "w_fin": inp["w_final_mod"],
            "w1g": inp["w_ffn1_gate"][0], "w1u": inp["w_ffn1_up"][0], "w1d": inp["w_ffn1_down"][0],
            "w2g": inp["w_ffn2_gate"][0], "w2u": inp["w_ffn2_up"][0], "w2d": inp["w_ffn2_down"][0],
            "w_in": inp["w_in"][0], "w_out": inp["w_out"][0],
            "st_in": np.ascontiguousarray(inp["state_hgrn"][0, i]),
            "cst": _consts(), "b_sb": np.ascontiguousarray(inp["b_sb"].reshape(1, 8)),
            "cache_k": inp["cache_k"].reshape(NPHYS * 128, 1024), "cache_v": inp["cache_v"].reshape(NPHYS * 128, 1024),
            "ptab": np.ascontiguousarray(inp["page_table"][i].reshape(1, NPAGES)).astype(np.int32),
            "b64": np.ascontiguousarray(np.repeat(inp["b_sb"].reshape(8), 8).reshape(64, 1)),
            "cst2": _consts2(),
        }
        in_maps.append(m)
    res = run_bass_kernel_spmd(nc, in_maps, core_ids=list(range(NCORES))).results
    f32 = np.float32
    y_prompt = np.stack([np.concatenate([res[2 * b]["y_p"], res[2 * b + 1]["y_p"]], axis=0) for b in range(4)]).astype(f32)
    y_sample = np.stack([res[i]["y_s"] for i in range(8)]).astype(f32)
    k_prompt = np.stack([np.concatenate([res[2 * b]["k_p"], res[2 * b + 1]["k_p"]], axis=0).reshape(2 * T, 8, 128)
                         for b in range(4)])[None].astype(f32)
    v_prompt = np.stack([np.concatenate([res[2 * b]["v_p"], res[2 * b + 1]["v_p"]], axis=0).reshape(2 * T, 8, 128)
                         for b in range(4)])[None].astype(f32)
    k_sample = np.stack([res[i]["k_s"].reshape(8, 8, 128) for i in range(8)])[None].astype(f32)
    v_sample = np.stack([res[i]["v_s"].reshape(8, 8, 128) for i in range(8)])[None].astype(f32)
    s_prompt = np.stack([res[2 * b + 1]["s_p"] for b in range(4)])[None].astype(f32)
    s_sample = np.stack([res[i]["s_s"] for i in range(8)])[None].astype(f32)
    return (y_prompt, y_sample, k_prompt, v_prompt, k_sample, v_sample, s_prompt, s_sample)
```

```python
import os
from contextlib import ExitStack
import numpy as np
import concourse.bass as bass
import concourse.mybir as mybir
from concourse.bass_utils import run_bass_kernel_spmd

F32 = mybir.dt.float32
BF16 = mybir.dt.bfloat16
I32 = mybir.dt.int32
AF = mybir.ActivationFunctionType
ALU = mybir.AluOpType

D = 2048
DFF = 5632
NFF = DFF // 128
DIN = 7168
T = 1024
NTB = T // 128
GC = 4
EPS = 1e-6
NCORES = 8
NPAGES = 128
NPHYS = 1280

R_CP, R_CS, R_N1, R_N2, R_N3, R_NF, R_LB, R_GA, R_GB, R_BM = 0, 16, 32, 48, 64, 80, 96, 104, 112, 128


class Sched:
    ENG = ("pe", "act", "dve", "pool", "sp")

    def __init__(self, nc, es, n_dma_slots=6):
        self.nc = nc
        self.ops = {e: [] for e in self.ENG}
        self.cnt = {e: 0 for e in self.ENG}
        self.lastw = {}
        self.readers = {}
        self.sems = {}
        for e in ("pe", "act", "dve", "pool"):
            self.sems[e] = es.enter_context(nc.semaphore("sem_" + e))
        self.slots = {}
        self.slot_cnt = {}
        self.slot_rr = {}
        for q in ("sp", "pool"):
            self.slots[q] = []
            for j in range(n_dma_slots):
                nm = "dq_%s_%d" % (q, j)
                self.sems[nm] = es.enter_context(nc.semaphore(nm))
                self.slots[q].append(nm)
                self.slot_cnt[nm] = 0
            self.slot_rr[q] = 0
        self.nops = 0

    def _deps(self, r, w):
        deps = set()
        for k in r:
            if k in self.lastw:
                deps.add(self.lastw[k])
        for k in w:
            if k in self.lastw:
                deps.add(self.lastw[k])
            for t in self.readers.get(k, ()):
                deps.add(t)
        return deps

    def _commit(self, tok, r, w):
        for k in w:
            self.lastw[k] = tok
            self.readers[k] = []
        for k in r:
            if k not in w:
                self.readers.setdefault(k, []).append(tok)

    def op(self, eng, fn, r=(), w=()):
        deps = self._deps(r, w)
        if eng == "pe":
            deps = {d for d in deps if d[0] != "pe"}
        self.cnt[eng] += 1
        tok = (eng, self.cnt[eng])
        self.ops[eng].append((fn, deps, tok, 1))
        self._commit(tok, r, w)
        self.nops += 1

    def dma(self, q, fn, r=(), w=()):
        deps = self._deps(r, w)
        j = self.slot_rr[q]
        self.slot_rr[q] = (j + 1) % len(self.slots[q])
        nm = self.slots[q][j]
        if self.slot_cnt[nm] > 0:
            deps.add((nm, 16 * self.slot_cnt[nm]))
        self.slot_cnt[nm] += 1
        tok = (nm, 16 * self.slot_cnt[nm])
        self.ops[q].append((fn, deps, tok, 16))
        self._commit(tok, r, w)
        self.nops += 1

    def barrier(self):
        toks = [(e, self.cnt[e]) for e in ("pe", "act", "dve", "pool") if self.cnt[e] > 0] + self.all_dma_tokens()
        for e in self.ENG:
            self.ops[e].append((None, set(toks), None, 0))
        self.lastw = {}
        self.readers = {}

    def final_waits(self, eng, toks):
        self.ops[eng].append((None, set(toks), None, 0))

    def all_dma_tokens(self):
        return [(nm, 16 * c) for nm, c in self.slot_cnt.items() if c > 0]

    def emit(self, block):
        sems = self.sems

        def mk(engname):
            def body(e):
                seen = {}
                for fn, deps, tok, inc in self.ops[engname]:
                    for (s, v) in sorted(deps):
                        if seen.get(s, 0) < v:
                            e.wait_ge(sems[s], v)
                            seen[s] = v
                    if fn is None:
                        continue
                    ins = fn(e)
                    ins.then_inc(sems[tok[0]], inc)
            return body

        block.tensor(mk("pe"))
        block.scalar(mk("act"))
        block.vector(mk("dve"))
        block.gpsimd(mk("pool"))
        block.sync(mk("sp"))


def build_program(stage_limit=99):
    nc = bass.Bass("TRN2", target_bir_lowering=False)
    dt = nc.dram_tensor
    x_all = dt("x_all", [2 * T, D], F32, kind="ExternalInput").ap()
    x_smp = dt("x_smp", [8, D], F32, kind="ExternalInput").ap()
    vecs = dt("vecs", [384, 128], F32, kind="ExternalInput").ap()
    w_mod = dt("w_mod", [D, 9 * D], F32, kind="ExternalInput").ap()
    w_fin = dt("w_fin", [D, 2 * D], F32, kind="ExternalInput").ap()
    w1g = dt("w1g", [D, DFF], F32, kind="ExternalInput").ap()
    w1u = dt("w1u", [D, DFF], F32, kind="ExternalInput").ap()
    w1d = dt("w1d", [DFF, D], F32, kind="ExternalInput").ap()
    w2g = dt("w2g", [D, DFF], F32, kind="ExternalInput").ap()
    w2u = dt("w2u", [D, DFF], F32, kind="ExternalInput").ap()
    w2d = dt("w2d", [DFF, D], F32, kind="ExternalInput").ap()
    w_in = dt("w_in", [D, DIN], F32, kind="ExternalInput").ap()
    w_out = dt("w_out", [D, D], F32, kind="ExternalInput").ap()
    y_p = dt("y_p", [T, D], F32, kind="ExternalOutput").ap()
    y_s = dt("y_s", [8, D], F32, kind="ExternalOutput").ap()
    k_p = dt("k_p", [T, 1024], F32, kind="ExternalOutput").ap()
    v_p = dt("v_p", [T, 1024], F32, kind="ExternalOutput").ap()
    k_s = dt("k_s", [8, 1024], F32, kind="ExternalOutput").ap()
    v_s = dt("v_s", [8, 1024], F32, kind="ExternalOutput").ap()
    s_p = dt("s_p", [8, 128, 128], F32, kind="ExternalOutput").ap()
    s_s = dt("s_s", [8, 128, 128], F32, kind="ExternalOutput").ap()
    st_in = dt("st_in", [8, 128, 128], F32, kind="ExternalInput").ap()
    cst = dt("cst", [128, 2304], F32, kind="ExternalInput").ap()
    b_sb = dt("b_sb", [1, 8], F32, kind="ExternalInput").ap()
    cache_k = dt("cache_k", [NPHYS * 128, 1024], F32, kind="ExternalInput").ap()
    cache_v = dt("cache_v", [NPHYS * 128, 1024], F32, kind="ExternalInput").ap()
    ptab = dt("ptab", [1, NPAGES], I32, kind="ExternalInput").ap()
    b64 = dt("b64", [64, 1], F32, kind="ExternalInput").ap()
    flag = dt("flag", [1, 1], F32, kind="ExternalInput").ap()
    cst2 = dt("cst2", [64, 1032], F32, kind="ExternalInput").ap()
    xsp = dt("xsp", [T, D], F32, kind="Internal").ap()
    kTp_d = dt("kTp_d", [8, 128, T], BF16, kind="Internal").ap()
    Vp_d = dt("Vp_d", [8, 128, T], BF16, kind="Internal").ap()

    es = ExitStack()
    with es:
        def sb(name, shape, dtype):
            return es.enter_context(nc.sbuf_tensor(name, shape, dtype))

        S = Sched(nc, es)
        U = sb("U", [128, NTB * D], F32)
        x_tok = U[:].rearrange("p (t d) -> p t d", t=NTB)
        xS = sb("xS", [128, D], F32)
        hT = sb("hT", [128, 16, T + 8], BF16)
        aT = sb("aT", [128, GC, T + 8], BF16)
        wgb = [sb("wgb%d" % i, [128, 16, 256], BF16) for i in range(2)]
        wub = [sb("wub%d" % i, [128, 16, 256], BF16) for i in range(2)]
        wdb = [sb("wdb%d" % i, [128, GC, 512], BF16) for i in range(2)]
        gP = sb("gP", [128, D], F32)
        gS = sb("gS", [128, D], F32)
        xn = sb("xn", [128, D], F32)
        sgt = [sb("sgt%d" % i, [128, 512], F32) for i in range(1)]
        dtmp = [sb("dtmp%d" % i, [128, 512], F32) for i in range(2)]
        vecs_sb = sb("vecs_sb", [128, 3, 128], F32)
        cols = sb("cols", [128, 384], F32)
        scT = sb("scT", [128, 16, 2], BF16)
        scT32 = sb("scT32", [128, 16, 2], F32)
        modc = sb("modc", [128, 176, 2], F32)
        effs = sb("effs", [128, 4, 16, 2], F32)
        ident = sb("ident", [128, 128], F32)
        ones_f = sb("ones_f", [128, 128], F32)
        diag = [sb("diag%d" % i, [128, 128], F32) for i in range(2)]
        ssq = sb("ssq", [128, 16], F32)
        rstd = sb("rstd", [128, 16], F32)
        iota_i = sb("iota_i", [128, 128], I32)
        iota_f = sb("iota_f", [128, 128], F32)
        pid_f = sb("pid_f", [128, 1], F32)
        Sst = sb("Sst", [128, 8, 128], F32)
        Sss = sb("Sss", [128, 8, 128], F32)
        lbc = sb("lbc", [128, 8], F32)
        oml = sb("oml", [128, 8], F32)
        kend_tok = sb("kend_tok", [128, 128], BF16)
        ntri = sb("ntri", [128, 128], BF16)
        nones = sb("nones", [128, 128], BF16)
        ones_b = sb("ones_b", [128, 128], BF16)
        mle = sb("mle", [128, 128], BF16)
        bsbc = sb("bsbc", [128, 8], F32)
        bsbp = sb("bsbp", [128, 8], F32)
        flagc = sb("flagc", [128, 1], F32)
        nbig = sb("nbig", [128, 1], F32)
        va_tok = sb("va_tok", [128, 128], BF16)
        kend_tokB = sb("kend_tokB", [128, 128], BF16)
        va_tokB = sb("va_tokB", [128, 128], BF16)
        ps = es.enter_context(nc.psum_tensor("ps", [128, 8, 512], F32))

        S.op("pool", lambda e: e.iota(iota_i[:], [[1, 128]], base=0, channel_multiplier=0), w=["iota_i"])
        S.op("pool", lambda e: e.tensor_copy(iota_f[:], iota_i[:]), r=["iota_i"], w=["iota_f"])
        S.op("pool", lambda e: e.iota(iota_i[:, 0:1], [[1, 1]], base=0, channel_multiplier=1), r=["iota_f"], w=["iota_i"])
        S.op("pool", lambda e: e.tensor_copy(pid_f[:], iota_i[:, 0:1]), r=["iota_i"], w=["pid_f"])
        S.op("dve", lambda e: e.tensor_scalar(ident[:], iota_f[:], pid_f[:, 0:1], None, ALU.is_equal),
             r=["iota_f", "pid_f"], w=["ident"])
        S.op("dve", lambda e: e.memset(ones_f[:], 1.0), w=["ones_f"])
        S.op("dve", lambda e: e.memset(ones_b[:], 1.0), w=["ones_b"])
        S.op("dve", lambda e: e.memset(nones[:], -1.0), w=["nones"])
        S.dma("pool", lambda e: e.dma_start(out=ntri[:], in_=cst[:, 0:128]), w=["ntri"])
        S.dma("pool", lambda e: e.dma_start(out=mle[:], in_=cst[:, 128:256]), w=["mle"])
        S.dma("sp", lambda e: e.dma_start(out=bsbc[:], in_=b_sb.partition_broadcast(128)), w=["bsbc"])
        S.dma("sp", lambda e: e.dma_start(out=flagc[:], in_=flag.partition_broadcast(128)), w=["flagc"])
        S.op("dve", lambda e: e.tensor_scalar(nbig[:], flagc[:], -1.0, 1.0e4, ALU.add, ALU.mult), r=["flagc"], w=["nbig"])
        S.op("dve", lambda e: e.tensor_scalar(bsbp[:], bsbc[:], nbig[:, 0:1], None, ALU.add), r=["nbig", "bsbc"], w=["bsbp"])

        S.dma("sp", lambda e: e.dma_start(out=vecs_sb[:], in_=vecs.rearrange("(g r) c -> r g c", g=3)), w=["vecs_sb"])
        for g in range(3):
            S.op("pe", lambda e, g=g: e.transpose(ps[:, 7, g * 128:(g + 1) * 128], vecs_sb[:, g, :], ident[:]),
                 r=["vecs_sb", "ident"], w=[("ps", 7)])
        S.op("act", lambda e: e.activation(cols[:], ps[:, 7, 0:384], AF.Copy), r=[("ps", 7)], w=["cols"])
        for r_ in range(2):
            S.op("act", lambda e, r_=r_: e.activation(scT[:, :, r_], cols[:, 16 * r_:16 * r_ + 16], AF.Silu),
                 r=["cols"], w=["scT"])
            S.op("act", lambda e, r_=r_: e.activation(scT32[:, :, r_], cols[:, 16 * r_:16 * r_ + 16], AF.Silu),
                 r=["cols"], w=["scT32"])

        SC_OFF = [16, 64, 112, 160]
        SH_OFF = [0, 48, 96, 144]
        GA_OFF = [32, 80, 128]
        G_ROW = [R_N1, R_N2, R_N3, R_NF]
        for n in range(24):
            src_ = w_mod[:, n * 256:(n + 1) * 256]
            bufs4 = [wgb[0], wgb[1], wub[0], wub[1]]
            keys4 = [("wgb", 0), ("wgb", 1), ("wub", 0), ("wub", 1)]
            buf, bk = bufs4[n % 4], keys4[n % 4]
            S.dma("pool", lambda e, buf=buf, src_=src_: e.dma_start(
                out=buf[:], in_=src_.rearrange("(k p) c -> p k c", p=128)), w=[bk])
            for cc in range(2):
                j = n * 2 + cc
                for k in range(16):
                    S.op("pe", lambda e, buf=buf, k=k, cc=cc, j=j: e.matmul(
                        ps[:, 6, 2 * j:2 * j + 2], buf[:, k, cc * 128:(cc + 1) * 128], scT[:, k, :],
                        start=(k == 0), stop=(k == 15)), r=[bk, "scT"], w=[("ps", 6)])
        S.op("dve", lambda e: e.tensor_tensor(
            modc[:, 0:48, :], ps[:, 6, 0:96].rearrange("p (j r) -> p j r", r=2),
            cols[:, R_BM:R_BM + 48].unsqueeze(2).to_broadcast([128, 48, 2]), ALU.add),
            r=[("ps", 6), "cols"], w=["modc"])

        def make_effs(n):
            for r_ in range(2):
                S.op("dve", lambda e, n=n, r_=r_: e.scalar_tensor_tensor(
                    effs[:, n, :, r_], modc[:, SC_OFF[n]:SC_OFF[n] + 16, r_], 1.0,
                    cols[:, G_ROW[n]:G_ROW[n] + 16], ALU.add, ALU.mult), r=["modc", "cols"], w=["effs"])
        make_effs(0)

        bg = []
        bg_slot = [0]
        stg = xS[:].bitcast(BF16).rearrange("p (k c) -> p k c", k=16)

        def mod_group_task(n):
            def task():
                src_ = w_mod[:, n * 256:(n + 1) * 256] if n < 72 else w_fin[:, (n - 72) * 256:(n - 71) * 256]
                S.dma("pool", lambda e: e.dma_start(out=stg, in_=src_.rearrange("(k p) c -> p k c", p=128)), w=[("x", NTB)])
                for cc in range(2):
                    j = n * 2 + cc
                    for k in range(16):
                        S.op("pe", lambda e, k=k, cc=cc, j=j: e.matmul(
                            ps[:, 6, 2 * j:2 * j + 2], stg[:, k, cc * 128:(cc + 1) * 128], scT[:, k, :],
                            start=(k == 0), stop=(k == 15)), r=[("x", NTB), "scT"], w=[("ps", 6)])
            return task
        for n in range(24, 88):
            bg.append(mod_group_task(n))

        def finish_mod():
            while bg:
                bg.pop(0)()
            S.op("dve", lambda e: e.tensor_tensor(
                modc[:, 48:176, :], ps[:, 6, 96:352].rearrange("p (j r) -> p j r", r=2),
                cols[:, R_BM + 48:R_BM + 176].unsqueeze(2).to_broadcast([128, 128, 2]), ALU.add),
                r=[("ps", 6), "cols"], w=["modc"])
            for n in (1, 2, 3):
                make_effs(n)

        def make_gate(m, mult):
            for r_, gt, gk in ((0, gP, "gP"), (1, gS, "gS")):
                for k in range(16):
                    dg = diag[k % 2]
                    S.op("dve", lambda e, dg=dg, k=k, r_=r_: e.tensor_scalar(
                        dg[:], ident[:], modc[:, GA_OFF[m] + k, r_:r_ + 1], mult, ALU.mult, ALU.mult),
                        r=["ident", "modc"], w=[("diag", k % 2)])
                    S.op("pe", lambda e, dg=dg, k=k: e.matmul(
                        ps[:, 4 + k // 4, (k % 4) * 128:(k % 4 + 1) * 128], ones_f[:], dg[:], start=True, stop=True),
                        r=[("diag", k % 2), "ones_f"], w=[("ps", 4 + k // 4)])
                for b in range(4):
                    S.op("act", lambda e, b=b, gt=gt: e.activation(gt[:, b * 512:(b + 1) * 512], ps[:, 4 + b, :], AF.Copy),
                         r=[("ps", 4 + b)], w=[gk])

        def tile_cols(with_sample):
            t = [(0, 512), (512, 512)]
            if with_sample:
                t.append((1024, 8))
            return t

        def tb_list(with_sample):
            l = [(tb, 128) for tb in range(NTB)]
            if with_sample:
                l.append((NTB, 8))
            return l

        def xrow(tb, n):
            return x_tok[:, tb, :] if tb < NTB else xS[0:n, :]

        def norm_to_hT(nidx, with_sample):
            for tb, n in tb_list(with_sample):
                xin = xrow(tb, n)
                r_ = 0 if tb < NTB else 1
                S.op("act", lambda e, xin=xin, n=n, tb=tb: e.activation(
                    xn[0:n, :], xin, AF.Square, accum_out=ssq[0:n, tb:tb + 1]), r=[("x", tb)], w=["xn", ("ssq", tb)])
                S.op("act", lambda e, n=n, tb=tb: e.activation(
                    rstd[0:n, tb:tb + 1], ssq[0:n, tb:tb + 1], AF.Sqrt, bias=EPS, scale=1.0 / D),
                    r=[("ssq", tb)], w=[("rstd", tb)])
                S.op("dve", lambda e, n=n, tb=tb: e.reciprocal(rstd[0:n, tb:tb + 1], rstd[0:n, tb:tb + 1]),
                     r=[("rstd", tb)], w=[("rstd", tb)])
                S.op("dve", lambda e, xin=xin, n=n, tb=tb: e.tensor_scalar(
                    xn[0:n, :], xin, rstd[0:n, tb:tb + 1], None, ALU.mult), r=[("x", tb), ("rstd", tb)], w=["xn"])
                for kq in range(4):
                    bank = kq % 2
                    for kk in range(4):
                        k = kq * 4 + kk
                        S.op("pe", lambda e, k=k, kk=kk, n=n, bank=bank: e.transpose(
                            ps[:, bank, kk * 128:kk * 128 + n], xn[0:n, k * 128:(k + 1) * 128], ident[0:n, 0:n]),
                            r=["xn", "ident"], w=[("ps", bank)])
                    for kk in range(4):
                        k = kq * 4 + kk
                        S.op("act", lambda e, k=k, kk=kk, n=n, tb=tb, bank=bank, r_=r_: e.activation(
                            hT[:, k, tb * 128:tb * 128 + n], ps[:, bank, kk * 128:kk * 128 + n], AF.Identity,
                            bias=modc[:, SH_OFF[nidx] + k, r_:r_ + 1], scale=effs[:, nidx, k, r_:r_ + 1]),
                            r=[("ps", bank), "modc", "effs"], w=[("hT", tb)])

        def ffn(wg, wu, wd, with_sample):
            tiles = tile_cols(with_sample)
            tbs = tb_list(with_sample)
            pair_i = 0
            dq = 0
            for g0 in range(0, NFF, GC):
                nch = min(GC, NFF - g0)
                for cp in range(0, nch, 2):
                    c0 = g0 + cp
                    bi = pair_i % 2
                    pair_i += 1
                    S.dma("pool", lambda e, bi=bi, c0=c0: e.dma_start(
                        out=wgb[bi][:], in_=wg[:, c0 * 128:c0 * 128 + 256].rearrange("(k p) c -> p k c", p=128)),
                        w=[("wgb", bi)])
                    S.dma("pool", lambda e, bi=bi, c0=c0: e.dma_start(
                        out=wub[bi][:], in_=wu[:, c0 * 128:c0 * 128 + 256].rearrange("(k p) c -> p k c", p=128)),
                        w=[("wub", bi)])
                    for cc in range(2):
                        ci = cp + cc
                        for ti, (t0, tn) in enumerate(tiles):
                            hkeys = [("hT", tb) for tb in range(t0 // 128, t0 // 128 + max(1, tn // 128))]
                            bg_slot[0] += 1
                            if bg and bg_slot[0] % 2 == 0:
                                bg.pop(0)()
                            bg_, bu = (0, 1) if (ci * 3 + ti) % 2 == 0 else (2, 3)
                            for k in range(16):
                                S.op("pe", lambda e, k=k, cc=cc, t0=t0, tn=tn, bg_=bg_, bi=bi: e.matmul(
                                    ps[:, bg_, 0:tn], wgb[bi][:, k, cc * 128:(cc + 1) * 128], hT[:, k, t0:t0 + tn],
                                    start=(k == 0), stop=(k == 15)), r=[("wgb", bi)] + hkeys, w=[("ps", bg_)])
                            for k in range(16):
                                S.op("pe", lambda e, k=k, cc=cc, t0=t0, tn=tn, bu=bu, bi=bi: e.matmul(
                                    ps[:, bu, 0:tn], wub[bi][:, k, cc * 128:(cc + 1) * 128], hT[:, k, t0:t0 + tn],
                                    start=(k == 0), stop=(k == 15)), r=[("wub", bi)] + hkeys, w=[("ps", bu)])
                            sg = sgt[0]
                            sgk = ("sgt", 0)
                            S.op("act", lambda e, sg=sg, tn=tn, bg_=bg_: e.activation(sg[:, 0:tn], ps[:, bg_, 0:tn], AF.Silu),
                                 r=[("ps", bg_)], w=[sgk])
                            S.op("dve", lambda e, sg=sg, tn=tn, bu=bu, ci=ci, t0=t0: e.tensor_tensor(
                                aT[:, ci, t0:t0 + tn], sg[:, 0:tn], ps[:, bu, 0:tn], ALU.mult),
                                r=[sgk, ("ps", bu)], w=[("aT", ci, ti)])
                for cg in range(4):
                    wi = dq % 2
                    dq += 1
                    S.dma("pool", lambda e, wi=wi, g0=g0, nch=nch, cg=cg: e.dma_start(
                        out=wdb[wi][:, 0:nch, :],
                        in_=wd[g0 * 128:(g0 + nch) * 128, cg * 512:(cg + 1) * 512].rearrange("(k p) c -> p k c", p=128)),
                        w=[("wdb", wi)])
                    for bi_, (tb, n) in enumerate(tbs):
                        bank = 4 + (cg * len(tbs) + bi_) % 2
                        ti = min(tb // 4, 2)
                        for ci in range(nch):
                            S.op("pe", lambda e, ci=ci, tb=tb, n=n, bank=bank, wi=wi: e.matmul(
                                ps[0:n, bank, :], aT[:, ci, tb * 128:tb * 128 + n], wdb[wi][:, ci, :],
                                start=(ci == 0), stop=(ci == nch - 1)),
                                r=[("aT", ci, ti), ("wdb", wi)], w=[("ps", bank)])
                        tmp = dtmp[bank - 4]
                        gt, gk = (gP, "gP") if tb < NTB else (gS, "gS")
                        S.op("dve", lambda e, tmp=tmp, n=n, bank=bank, gt=gt, cg=cg: e.tensor_tensor(
                            tmp[0:n, :], ps[0:n, bank, :], gt[0:n, cg * 512:(cg + 1) * 512], ALU.mult),
                            r=[("ps", bank), gk], w=[("dtmp", bank)])
                        xo = xrow(tb, n)[:, cg * 512:(cg + 1) * 512]
                        S.op("dve", lambda e, tmp=tmp, n=n, xo=xo: e.tensor_tensor(xo, xo, tmp[0:n, :], ALU.add),
                             r=[("dtmp", bank), ("x", tb)], w=[("x", tb)])

        S.op("act", lambda e: e.activation(lbc[:], cols[:, R_LB:R_LB + 8], AF.Sigmoid), r=["cols"], w=["lbc"])
        S.op("dve", lambda e: e.tensor_scalar(oml[:], lbc[:], -1.0, 1.0, ALU.mult, ALU.add), r=["lbc"], w=["oml"])
        S.op("dve", lambda e: e.memset(Sst[:], 0.0), w=["Sst"])
        S.dma("sp", lambda e: e.dma_start(out=Sss[:], in_=st_in.rearrange("h k v -> k h v")), w=["Sss"])

        def kv_proj(half, with_sample):
            tbs = tb_list(with_sample)
            bufs4 = [wgb[0], wgb[1], wub[0], wub[1]]
            keys4 = [("wgb", 0), ("wgb", 1), ("wub", 0), ("wub", 1)]
            for cgi in range(8):
                c0 = 5120 + cgi * 256
                buf, bk = bufs4[cgi % 4], keys4[cgi % 4]
                S.dma("pool", lambda e, buf=buf, c0=c0: e.dma_start(
                    out=buf[:], in_=w_in[:, c0:c0 + 256].rearrange("(k p) c -> p k c", p=128)), w=[bk])
                for bi_, (tb, n) in enumerate(tbs):
                    bank = 4 + (cgi * len(tbs) + bi_) % 2
                    for k in range(16):
                        S.op("pe", lambda e, k=k, tb=tb, n=n, bank=bank, buf=buf: e.matmul(
                            ps[0:n, bank, 0:256], hT[:, k, tb * 128:tb * 128 + n], buf[:, k, :],
                            start=(k == 0), stop=(k == 15)), r=[("hT", tb), bk], w=[("ps", bank)])
                    tmp = dtmp[bank - 4]
                    S.op("act", lambda e, tmp=tmp, n=n, bank=bank: e.activation(tmp[0:n, 0:256], ps[0:n, bank, 0:256], AF.Copy),
                         r=[("ps", bank)], w=[("dtmp", bank)])
                    isk = cgi < 4
                    cc0 = (cgi % 4) * 256
                    if tb < NTB:
                        dst = (k_p if isk else v_p)[tb * 128:(tb + 1) * 128, cc0:cc0 + 256]
                    else:
                        dst = (k_s if isk else v_s)[:, cc0:cc0 + 256]
                    S.dma("sp", lambda e, dst=dst, tmp=tmp, n=n: e.dma_start(out=dst, in_=tmp[0:n, 0:256]), r=[("dtmp", bank)])

        HS = 128.0 ** -0.5
        oT = U[:, 0:8256].bitcast(BF16).rearrange("p (k t) -> p k t", k=16)
        _uo = [8256]

        def ualloc_f(n):
            a_ = _uo[0]
            _uo[0] += n
            assert _uo[0] <= NTB * D
            return U[:, a_:a_ + n]

        def ualloc_b(ncols):
            n = (ncols + 1) // 2
            a_ = _uo[0]
            _uo[0] += n
            assert _uo[0] <= NTB * D
            return U[:, a_:a_ + n].bitcast(BF16)[:, 0:ncols]

        fbT, bbT, b2T, ebT, nbT, sgT_ = [ualloc_f(512) for _ in range(6)]
        qd, kiv = ualloc_b(512), ualloc_b(512)
        ke = ualloc_f(128)
        aTm = ualloc_b(128)
        osq = ualloc_b(512)
        orstd = ualloc_f(512)
        onb = ualloc_f(512)
        Sbf = ualloc_b(128)
        keB = ualloc_f(128)
        aTmB = ualloc_b(128)
        qT = ualloc_b(1032)
        kTo = ualloc_b(1032)
        kTp = ualloc_b(1024)
        Vo = ualloc_b(1024).rearrange("p (t d) -> p t d", t=8)
        Vp = ualloc_b(1024).rearrange("p (t d) -> p t d", t=8)
        e_s = [dtmp[0], dtmp[1]]
        lfl = sgt[0][:].bitcast(BF16)
        l_s = [lfl[:, 0:512], lfl[:, 512:1024]]
        aTf = aT[:].rearrange("p g t -> p (g t)")
        R32 = [aTf[:, 0:1024].bitcast(F32), aTf[:, 1024:2048].bitcast(F32)]
        Rbf = [aTf[:, 2048:2560], aTf[:, 2560:3072]]
        ATs = [aTf[:, 3072:3584], aTf[:, 3584:4096]]
        w0f = wdb[0][:].rearrange("p g t -> p (g t)")
        qpad = w0f[:, 0:512].rearrange("p (h m) -> p h m", h=8)
        kTn = w0f[:, 512:576].rearrange("p (h m) -> p h m", h=8)
        Vn = w0f[:, 576:1600]
        Mj = wdb[1]
        W7 = [wgb[0][:], wgb[1][:], wub[0][:], wub[1][:],
              gP[:].bitcast(BF16).rearrange("p (k c) -> p k c", k=16),
              gS[:].bitcast(BF16).rearrange("p (k c) -> p k c", k=16),
              xn[:].bitcast(BF16).rearrange("p (k c) -> p k c", k=16)]
        W7K = [("W7", i) for i in range(7)]
        TYPE_BASE = [0, 1024, 2048, 3072, 4096, 5120, 6144]

        def proj_fm(dst_bank, wi, cc, t0, tn):
            hkeys = [("hT", tb) for tb in range(t0 // 128, t0 // 128 + max(1, tn // 128))]
            for k in range(16):
                S.op("pe", lambda e, k=k: e.matmul(
                    ps[:, dst_bank, 0:tn], W7[wi][:, k, cc * 128:(cc + 1) * 128], hT[:, k, t0:t0 + tn],
                    start=(k == 0), stop=(k == 15)), r=[W7K[wi]] + hkeys, w=[("ps", dst_bank)])

        def proj_tm(dst_bank, wi, cc, tb, n):
            for k in range(16):
                S.op("pe", lambda e, k=k: e.matmul(
                    ps[0:n, dst_bank, 0:128], hT[:, k, tb * 128:tb * 128 + n], W7[wi][:, k, cc * 128:(cc + 1) * 128],
                    start=(k == 0), stop=(k == 15)), r=[W7K[wi], ("hT", tb)], w=[("ps", dst_bank)])

        def head_norm_out(obank, C, dst, gcol, gate, ssbank=6):
            S.op("act", lambda e: e.activation(osq[:, 0:C], ps[:, obank, 0:C], AF.Square), r=[("ps", obank)], w=["osq"])
            S.op("pe", lambda e: e.matmul(ps[:, ssbank, 0:C], ones_b[:], osq[:, 0:C], start=True, stop=True),
                 r=["ones_b", "osq"], w=[("ps", ssbank)])
            S.op("act", lambda e: e.activation(orstd[:, 0:C], ps[:, ssbank, 0:C], AF.Sqrt, bias=EPS, scale=1.0 / 128),
                 r=[("ps", ssbank)], w=["orstd"])
            S.op("dve", lambda e: e.reciprocal(orstd[:, 0:C], orstd[:, 0:C]), r=["orstd"], w=["orstd"])
            S.op("dve", lambda e: e.tensor_tensor(onb[:, 0:C], ps[:, obank, 0:C], orstd[:, 0:C], ALU.mult),
                 r=[("ps", obank), "orstd"], w=["onb"])
            if gate is None:
                S.op("dve", lambda e: e.tensor_scalar(dst, onb[:, 0:C], gcol, None, ALU.mult), r=["onb", "cols"], w=["oT"])
            else:
                S.op("dve", lambda e: e.scalar_tensor_tensor(dst, onb[:, 0:C], gcol, gate, ALU.mult, ALU.mult),
                     r=["onb", "cols", "sgT"], w=["oT"])

        ke2, kend2, va2, aTm2 = [ke, keB], [kend_tok, kend_tokB], [va_tok, va_tokB], [aTm, aTmB]

        def hgrn_head(ws, h, cc, state_only=False):
            for ti, (t0, tn) in enumerate(tile_cols(ws)):
                C = min(128, tn)
                fb, bb, b2, eb, nb, sg = fbT[:, 0:tn], bbT[:, 0:tn], b2T[:, 0:tn], ebT[:, 0:tn], nbT[:, 0:tn], sgT_[:, 0:tn]
                proj_fm(0, 1, cc, t0, tn)
                S.op("act", lambda e, fb=fb, tn=tn: e.activation(fb, ps[:, 0, 0:tn], AF.Sigmoid), r=[("ps", 0)], w=["fbT"])
                S.op("dve", lambda e, fb=fb: e.tensor_scalar(fb, fb, oml[:, h:h + 1], lbc[:, h:h + 1], ALU.mult, ALU.add),
                     r=["fbT", "oml", "lbc"], w=["fbT"])
                S.op("act", lambda e, fb=fb, bb=bb: e.activation(bb, fb, AF.Ln), r=["fbT"], w=["bbT"])
                for c in range(max(1, tn // 128)):
                    S.op("dve", lambda e, c=c, C=C, bb=bb, b2=b2: e.tensor_tensor_scan(
                        b2[:, c * C:(c + 1) * C], ones_f[:, 0:C], bb[:, c * C:(c + 1) * C], 0.0, ALU.mult, ALU.add),
                        r=["bbT", "ones_f"], w=["b2T"])
                S.op("act", lambda e, b2=b2, eb=eb: e.activation(eb, b2, AF.Exp), r=["b2T"], w=["ebT"])
                S.op("act", lambda e, b2=b2, nb=nb: e.activation(nb, b2, AF.Exp, scale=-1.0), r=["b2T"], w=["nbT"])
                S.op("dve", lambda e, fb=fb: e.tensor_scalar(fb, fb, -1.0, 1.0, ALU.mult, ALU.add), r=["fbT"], w=["fbT"])
                S.op("dve", lambda e, fb=fb, nb=nb: e.tensor_tensor(nb, fb, nb, ALU.mult), r=["fbT", "nbT"], w=["nbT"])
                if not state_only:
                    S.op("dve", lambda e, nb=nb, tn=tn: e.tensor_copy(kiv[:, 0:tn], nb), r=["nbT"], w=["kiv"])
                    proj_fm(7, 0, cc, t0, tn)
                    S.op("dve", lambda e, eb=eb, tn=tn: e.scalar_tensor_tensor(
                        qd[:, 0:tn], ps[:, 7, 0:tn], HS, eb, ALU.mult, ALU.mult), r=[("ps", 7), "ebT"], w=["qd"])
                    proj_fm(0, 3, cc, t0, tn)
                    S.op("act", lambda e, sg=sg, tn=tn: e.activation(sg, ps[:, 0, 0:tn], AF.Silu), r=[("ps", 0)], w=["sgT"])
                nchk = max(1, tn // 128)

                def P1(c, C=C, nb=nb, eb=eb, t0=t0):
                    tb = t0 // 128 + c
                    lo, hi = c * C, (c + 1) * C
                    last = hi - 1
                    rb = c % 2
                    ke_, kt_, vt_, am_ = ke2[rb], kend2[rb], va2[rb], aTm2[rb]
                    kb_, vb_ = (2, 6)[rb], (3, 7)[rb]
                    S.op("dve", lambda e: e.tensor_scalar(
                        ke_[:, 0:C], nb[:, lo:hi], eb[:, last:last + 1], None, ALU.mult), r=["nbT", "ebT"], w=[("ke", rb)])
                    S.op("pe", lambda e: e.transpose(ps[0:C, kb_, 0:128], ke_[:, 0:C], ident[:]),
                         r=[("ke", rb), "ident"], w=[("ps", kb_)])
                    S.op("act", lambda e: e.activation(kt_[0:C, :], ps[0:C, kb_, 0:128], AF.Copy),
                         r=[("ps", kb_)], w=[("kend", rb)])
                    proj_tm(vb_, 2, cc, tb, C)
                    S.op("act", lambda e: e.activation(vt_[0:C, :], ps[0:C, vb_, 0:128], AF.Copy),
                         r=[("ps", vb_)], w=[("va", rb)])
                    if not state_only:
                        S.op("pe", lambda e: e.matmul(ps[0:C, 4, 0:C], kiv[:, lo:hi], qd[:, lo:hi], start=True, stop=True),
                             r=["kiv", "qd"], w=[("ps", 4)])
                        S.op("dve", lambda e: e.tensor_tensor(am_[0:C, 0:C], ps[0:C, 4, 0:C], mle[0:C, 0:C], ALU.mult),
                             r=[("ps", 4), "mle"], w=[("aTm", rb)])

                def P2(c, C=C, nb=nb, eb=eb, t0=t0, sg=sg):
                    tb = t0 // 128 + c
                    St = Sst if tb < NTB else Sss
                    sk = "Sst" if tb < NTB else "Sss"
                    lo, hi = c * C, (c + 1) * C
                    last = hi - 1
                    rb = c % 2
                    kt_, vt_, am_ = kend2[rb], va2[rb], aTm2[rb]
                    if not state_only:
                        S.op("act", lambda e: e.activation(Sbf[:, :], St[:, h, :], AF.Copy), r=[sk], w=["Sbf"])
                        S.op("pe", lambda e: e.matmul(ps[:, 5, 0:C], vt_[0:C, :], am_[0:C, 0:C], start=True, stop=False),
                             r=[("va", rb), ("aTm", rb)], w=[("ps", 5)])
                        S.op("pe", lambda e: e.matmul(ps[:, 5, 0:C], Sbf[:, :], qd[:, lo:hi], start=False, stop=True),
                             r=["Sbf", "qd"], w=[("ps", 5)])
                    S.op("pe", lambda e: e.matmul(ps[:, 1, 0:128], kt_[0:C, :], vt_[0:C, :], start=True, stop=True),
                         r=[("kend", rb), ("va", rb)], w=[("ps", 1)])
                    S.op("dve", lambda e: e.scalar_tensor_tensor(
                        St[:, h, :], St[:, h, :], eb[:, last:last + 1], ps[:, 1, 0:128], ALU.mult, ALU.add),
                        r=[sk, "ebT", ("ps", 1)], w=[sk])
                    if not state_only:
                        head_norm_out(5, C, oT[:, h, t0 + lo:t0 + hi], cols[:, R_GA + h:R_GA + h + 1], sg[:, lo:hi], ssbank=0)

                P1(0)
                for c in range(nchk):
                    if c + 1 < nchk:
                        P1(c + 1)
                    P2(c)

        def sb_head(half, ws, h, cc):
            for ti, (t0, tn) in enumerate(tile_cols(ws)):
                if half == 1:
                    proj_fm(4, 4, cc, t0, tn)
                    S.op("act", lambda e, t0=t0, tn=tn: e.activation(qT[:, t0:t0 + tn], ps[:, 4, 0:tn], AF.Copy, scale=HS),
                         r=[("ps", 4)], w=["qT"])
                proj_fm(5, 5, cc, t0, tn)
                S.op("act", lambda e, t0=t0, tn=tn: e.activation(kTo[:, t0:t0 + tn], ps[:, 5, 0:tn], AF.Copy),
                     r=[("ps", 5)], w=["kTo"])
            for tb, n in tb_list(ws):
                proj_tm(7, 6, cc, tb, n)
                dst = Vo[:, tb, :] if tb < NTB else Vn[0:n, h * 128:(h + 1) * 128]
                S.op("act", lambda e, n=n, dst=dst: e.activation(dst, ps[0:n, 7, 0:128], AF.Copy), r=[("ps", 7)],
                     w=["Vo" if tb < NTB else "Vn"])
            if ws:
                S.op("dve", lambda e: e.tensor_copy(qpad[:, h, 8 * h:8 * h + 8], qT[:, T:T + 8]), r=["qT"], w=["qpad"])
                S.op("dve", lambda e: e.tensor_copy(kTn[:, h, :], kTo[:, T:T + 8]), r=["kTo"], w=["kTn"])
            if half == 0:
                S.dma("sp", lambda e: e.dma_start(out=kTp_d[h], in_=kTo[:, 0:T]), r=["kTo"], w=[("kTp_d", h)])
                S.dma("sp", lambda e: e.dma_start(out=Vp_d[h], in_=Vo.rearrange("p t d -> p (t d)")), r=["Vo"], w=[("Vp_d", h)])
                return
            else:
                S.dma("sp", lambda e: e.dma_start(out=kTp[:, :], in_=kTp_d[h]), r=[("kTp_d", h)], w=["kTp"])
                S.dma("sp", lambda e: e.dma_start(out=Vp.rearrange("p t d -> p (t d)"), in_=Vp_d[h]), r=[("Vp_d", h)], w=["Vp"])
            lists = []
            for r_ in range(2):
                l_ = [("p", i) for i in range(8)] if half == 1 else []
                l_ += [("o", ob) for ob in range(4 * r_ + 4)]
                lists.append(l_)
            for step in range(max(len(l_) for l_ in lists)):
                for s in range(2):
                    if step >= len(lists[s]):
                        continue
                    kind, bi_ = lists[s][-(1 + step)]
                    nsteps = len(lists[s])
                    if kind == "p":
                        kblk, vblk, kk, vk, j = kTp[:, bi_ * 128:(bi_ + 1) * 128], Vp[:, bi_, :], "kTp", "Vp", -1
                        bias = bsbp[:, h:h + 1]
                    else:
                        kblk, vblk, kk, vk, j = kTo[:, bi_ * 128:(bi_ + 1) * 128], Vo[:, bi_, :], "kTo", "Vo", bi_ - 4 * s
                        bias = bsbc[:, h:h + 1]
                    q = qT[:, 512 * s:512 * s + 512]
                    S.op("pe", lambda e, s=s, kblk=kblk, q=q: e.matmul(ps[:, s, :], kblk, q, start=True, stop=False),
                         r=[kk, "qT"], w=[("ps", s)])
                    S.op("act", lambda e, s=s, bias=bias: e.activation(e_s[s][:, :], ps[:, s, :], AF.Exp, bias=bias), r=[("ps", s), "bsbc", "bsbp"], w=[("e", s)])
                    S.op("act", lambda e, s=s: e.activation(l_s[s], e_s[s][:, :], AF.Ln, bias=1.0), r=[("e", s)], w=[("l", s)])
                    if j >= 0:
                        S.op("dve", lambda e, s=s, j=j: e.tensor_tensor(l_s[s], l_s[s], Mj[:, j, :], ALU.mult),
                             r=[("l", s), "Mj"], w=[("l", s)])
                    S.op("pe", lambda e, s=s, step=step: e.matmul(ps[:, s, :], ntri[:], l_s[s], start=False, stop=(step == 0)),
                         r=["ntri", ("l", s)], w=[("ps", s)])
                    if step > 0:
                        S.op("pe", lambda e, s=s: e.matmul(ps[:, s, :], nones[:], Rbf[s], start=False, stop=True),
                             r=["nones", ("Rbf", s)], w=[("ps", s)])
                    S.op("act", lambda e, s=s, bias=bias: e.activation(ATs[s], ps[:, s, :], AF.Exp, bias=bias), r=[("ps", s), "bsbc", "bsbp"], w=[("AT", s)])
                    if j >= 0:
                        S.op("dve", lambda e, s=s, j=j: e.tensor_tensor(ATs[s], ATs[s], Mj[:, j, :], ALU.mult),
                             r=[("AT", s), "Mj"], w=[("AT", s)])
                    S.op("pe", lambda e, s=s, vblk=vblk, step=step, nsteps=nsteps: e.matmul(
                        ps[:, 2 + s, :], vblk, ATs[s], start=(step == 0), stop=(step == nsteps - 1)),
                        r=[vk, ("AT", s)], w=[("ps", 2 + s)])
                    if step < nsteps - 1:
                        if step == 0:
                            S.op("dve", lambda e, s=s: e.tensor_copy(R32[s], l_s[s]), r=[("l", s)], w=[("R32", s)])
                        else:
                            S.op("dve", lambda e, s=s: e.tensor_tensor(R32[s], R32[s], l_s[s], ALU.add), r=[("l", s), ("R32", s)], w=[("R32", s)])
                        S.op("dve", lambda e, s=s: e.tensor_copy(Rbf[s], R32[s]), r=[("R32", s)], w=[("Rbf", s)])
            for s in range(2):
                head_norm_out(2 + s, 512, oT[:, 8 + h, 512 * s:512 * s + 512], cols[:, R_GB + h:R_GB + h + 1], None)

        def sample_attention():
            base = 8256
            Kpg = [U[:, base + i * 1024:base + (i + 1) * 1024] for i in range(2)] + [U[:, base + 6144:base + 7168]]
            Vpg = [U[:, base + 2048 + i * 2048:base + 2048 + (i + 1) * 2048].bitcast(BF16).rearrange("p (s c) -> p s c", s=4)
                   for i in range(2)]
            KT = wgb[0][:].rearrange("p k c -> p (k c)").rearrange("p (h s) -> p h s", h=8)
            f0 = wub[0][:].rearrange("p k c -> p (k c)").bitcast(F32)
            f1 = wub[1][:].rearrange("p k c -> p (k c)").bitcast(F32)
            eB, lB, lbB, cB = f0[0:64, 0:512], f0[0:64, 512:1024], f0[0:64, 1024:1536], f0[0:64, 1536:2048]
            AB, ones5 = f1[0:64, 0:512], f1[0:64, 512:1024]
            AT4 = f1[:, 1024:1152].bitcast(BF16).rearrange("p (s m) -> p s m", s=4)
            ATn = f1[:, 1152:1184].bitcast(BF16)
            sm = f1[0:64, 1184:1200]
            mnew = f1[0:64, 1200:1208]
            o64 = f1[0:64, 1280:1408]
            tmpB = gP[0:64, 0:1024]
            mbig = gP[0:64, 1024:2048]
            gSi = gS[:].bitcast(I32)
            pt_i, idx_i = gSi[:, 0:128], gSi[:, 128:256]
            pt_f = gS[:, 256:384]

            S.dma("sp", lambda e: e.dma_start(out=pt_i, in_=ptab.partition_broadcast(128)), w=["pt_i"])
            S.dma("sp", lambda e: e.dma_start(out=sm[:, 2:3], in_=b64), w=["sm_b"])
            S.dma("sp", lambda e: e.dma_start(out=mnew, in_=cst2[:, 0:8]), w=["mnew"])
            S.dma("sp", lambda e: e.dma_start(out=mbig, in_=cst2[:, 8:1032]), w=["mbig"])
            S.op("dve", lambda e: e.tensor_copy(pt_f, pt_i), r=["pt_i"], w=["pt_f"])
            S.op("dve", lambda e: e.tensor_scalar(pt_f, pt_f, 128.0, pid_f[:, 0:1], ALU.mult, ALU.add), r=["pt_f", "pid_f"], w=["pt_f"])
            S.op("dve", lambda e: e.tensor_copy(idx_i, pt_f), r=["pt_f"], w=["idx_i"])
            S.op("dve", lambda e: e.memset(ones5, 1.0), w=["ones5"])
            bcol = sm[:, 2:3]

            def ew_chain(zbank, n, mask, first):
                S.op("act", lambda e: e.activation(eB[:, 0:n], ps[0:64, zbank, 0:n], AF.Exp, bias=bcol), r=[("ps", zbank), "sm_b"], w=["eB"])
                S.op("act", lambda e: e.activation(lB[:, 0:n], eB[:, 0:n], AF.Ln, bias=1.0), r=["eB"], w=["lB"])
                if mask is not None:
                    S.op("dve", lambda e: e.tensor_tensor(lB[:, 0:n], lB[:, 0:n], mask, ALU.mult), r=["lB", "mnew"], w=["lB"])
                S.op("dve", lambda e: e.scalar_tensor_tensor(lbB[:, 0:n], ps[0:64, zbank, 0:n], bcol, lB[:, 0:n], ALU.add, ALU.subtract),
                     r=[("ps", zbank), "sm_b", "lB"], w=["lbB"])
                S.op("dve", lambda e: e.tensor_tensor_scan(cB[:, 0:n], ones5[:, 0:n], lB[:, 0:n], 0.0, ALU.mult, ALU.add),
                     r=["lB", "ones5"], w=["cB"])
                if first:
                    S.op("dve", lambda e: e.tensor_copy(sm[:, 0:1], cB[:, n - 1:n]), r=["cB"], w=["carry"])
                else:
                    S.op("dve", lambda e: e.tensor_tensor(sm[:, 0:1], sm[:, 0:1], cB[:, n - 1:n], ALU.add), r=["cB", "carry"], w=["carry"])
                S.op("dve", lambda e: e.tensor_scalar(sm[:, 1:2], sm[:, 0:1], -1.0, None, ALU.mult), r=["carry"], w=["negc"])
                S.op("dve", lambda e: e.tensor_tensor(lbB[:, 0:n], lbB[:, 0:n], cB[:, 0:n], ALU.add), r=["lbB", "cB"], w=["lbB"])
                S.op("act", lambda e: e.activation(AB[:, 0:n], lbB[:, 0:n], AF.Exp, bias=sm[:, 1:2]), r=["lbB", "negc"], w=["AB"])
                if mask is not None:
                    S.op("dve", lambda e: e.tensor_tensor(AB[:, 0:n], AB[:, 0:n], mask, ALU.mult), r=["AB", "mnew"], w=["AB"])

            for h in range(8):
                S.op("pe", lambda e, h=h: e.matmul(ps[0:64, 0, 0:8], qpad[:, h, :], kTn[:, h, :], start=(h == 0), stop=(h == 7)),
                     r=["qpad", "kTn"], w=[("ps", 0)])
            ew_chain(0, 8, mnew, True)
            S.op("pe", lambda e: e.transpose(ps[0:8, 1, 0:64], AB[:, 0:8], ident[0:64, 0:64]), r=["AB", "ident"], w=[("ps", 1)])
            S.op("act", lambda e: e.activation(ATn[0:8, :], ps[0:8, 1, 0:64], AF.Copy), r=[("ps", 1)], w=["ATn"])
            for hf in range(2):
                S.op("pe", lambda e, hf=hf: e.matmul(ps[0:64, 6 + hf, :], ATn[0:8, :], Vn[0:8, hf * 512:(hf + 1) * 512], start=True, stop=False),
                     r=["ATn", "Vn"], w=[("ps", 6 + hf)])
            NCH = NPAGES // 4
            KTs = [KT, wgb[1][:].rearrange("p k c -> p (k c)").rearrange("p (h s) -> p h s", h=8)]
            ZB = [0, 4]

            def prep(ci):
                ch = NCH - 1 - ci
                vb = Vpg[ci % 2]
                vk = ("Vpg", ci % 2)
                KTc = KTs[ci % 2]
                for sl in range(4):
                    pg = ch * 4 + sl
                    kb_ = Kpg[(ci * 4 + sl) % 3]
                    kk_ = ("Kpg", (ci * 4 + sl) % 3)
                    S.dma("pool", lambda e, kb_=kb_, pg=pg: e.indirect_dma_start(
                        out=kb_, out_offset=None, in_=cache_k[:, :],
                        in_offset=bass.IndirectOffsetOnAxis(ap=idx_i[:, pg:pg + 1], axis=0)), r=["idx_i"], w=[kk_])
                    S.dma("pool", lambda e, vb=vb, sl=sl, pg=pg: e.indirect_dma_start(
                        out=vb[:, sl, :], out_offset=None, in_=cache_v[:, :],
                        in_offset=bass.IndirectOffsetOnAxis(ap=idx_i[:, pg:pg + 1], axis=0)), r=["idx_i"], w=[(vk, sl)])
                    for hq in range(2):
                        bank = 2 + hq
                        for hh in range(4):
                            h = hq * 4 + hh
                            S.op("pe", lambda e, kb_=kb_, h=h, hh=hh, bank=bank: e.transpose(
                                ps[:, bank, hh * 128:(hh + 1) * 128], kb_[:, h * 128:(h + 1) * 128], ident[:]),
                                r=[kk_, "ident"], w=[("ps", bank)])
                        dst = KTc[:, hq * 4:(hq + 1) * 4, sl * 128:(sl + 1) * 128]
                        srcp = ps[:, bank, :].rearrange("p (h s) -> p h s", h=4)
                        if hq == 0:
                            S.op("act", lambda e, dst=dst, srcp=srcp: e.activation(dst, srcp, AF.Copy), r=[("ps", bank)], w=[("KT", ci % 2, sl)])
                        else:
                            S.op("dve", lambda e, dst=dst, srcp=srcp: e.tensor_copy(dst, srcp), r=[("ps", bank)], w=[("KT", ci % 2, sl)])
                zb = ZB[ci % 2]
                for h in range(8):
                    S.op("pe", lambda e, h=h, zb=zb, KTc=KTc: e.matmul(ps[0:64, zb, :], qpad[:, h, :], KTc[:, h, :], start=(h == 0), stop=(h == 7)),
                         r=["qpad"] + [("KT", ci % 2, sl) for sl in range(4)], w=[("ps", zb)])

            prep(0)
            for ci in range(NCH):
                if ci + 1 < NCH:
                    prep(ci + 1)
                vb = Vpg[ci % 2]
                vk = ("Vpg", ci % 2)
                ew_chain(ZB[ci % 2], 512, None, False)
                for sl in range(4):
                    S.op("pe", lambda e, sl=sl: e.transpose(ps[:, 1, sl * 64:(sl + 1) * 64], AB[:, sl * 128:(sl + 1) * 128], ident[0:64, 0:64]),
                         r=["AB", "ident"], w=[("ps", 1)])
                S.op("act", lambda e: e.activation(AT4[:, :, :], ps[:, 1, 0:256].rearrange("p (s m) -> p s m", s=4), AF.Copy),
                     r=[("ps", 1)], w=["AT4"])
                for sl in range(4):
                    for hf in range(2):
                        S.op("pe", lambda e, sl=sl, hf=hf, vb=vb, ci=ci: e.matmul(
                            ps[0:64, 6 + hf, :], AT4[:, sl, :], vb[:, sl, hf * 512:(hf + 1) * 512],
                            start=False, stop=(ci == NCH - 1 and sl == 3)), r=["AT4", (vk, sl)], w=[("ps", 6 + hf)])
            for hf in range(2):
                S.op("dve", lambda e, hf=hf: e.tensor_tensor(tmpB[:, hf * 512:(hf + 1) * 512], ps[0:64, 6 + hf, :],
                                                             mbig[:, hf * 512:(hf + 1) * 512], ALU.mult),
                     r=[("ps", 6 + hf), "mbig"], w=["tmpB"])
            S.op("dve", lambda e: e.tensor_reduce(o64, tmpB.rearrange("p (h d) -> p d h", h=8), mybir.AxisListType.X, ALU.add),
                 r=["tmpB"], w=["o64"])
            S.op("act", lambda e: e.activation(tmpB[:, 0:128], o64, AF.Square, accum_out=sm[:, 3:4]), r=["o64"], w=["tmpB", "ssq64"])
            S.op("act", lambda e: e.activation(sm[:, 4:5], sm[:, 3:4], AF.Sqrt, bias=EPS, scale=1.0 / 128), r=["ssq64"], w=["rstd64"])
            S.op("dve", lambda e: e.reciprocal(sm[:, 4:5], sm[:, 4:5]), r=["rstd64"], w=["rstd64"])
            S.op("dve", lambda e: e.tensor_scalar(o64, o64, sm[:, 4:5], None, ALU.mult), r=["o64", "rstd64"], w=["o64"])
            S.op("pe", lambda e: e.transpose(ps[:, 1, 0:64], o64, ident[0:64, 0:64]), r=["o64", "ident"], w=[("ps", 1)])
            for h in range(8):
                S.op("dve", lambda e, h=h: e.tensor_scalar(oT[:, 8 + h, T:T + 8], ps[:, 1, 8 * h:8 * h + 8],
                                                           cols[:, R_GB + h:R_GB + h + 1], None, ALU.mult),
                     r=[("ps", 1), "cols"], w=["oT"])

        def mixer(half, ws):
            if half == 0:
                for hp in range(4):
                    for ty in (1, 2, 5, 6):
                        c0 = TYPE_BASE[ty] + hp * 256
                        S.dma("pool", lambda e, ty=ty, c0=c0: e.dma_start(
                            out=W7[ty], in_=w_in[:, c0:c0 + 256].rearrange("(k p) c -> p k c", p=128)), w=[W7K[ty]])
                    for cc in range(2):
                        h = hp * 2 + cc
                        hgrn_head(False, h, cc, state_only=True)
                        sb_head(0, False, h, cc)
                S.op("dve", lambda e: e.tensor_scalar(Sst[:].rearrange("p h v -> p (h v)"), Sst[:].rearrange("p h v -> p (h v)"),
                                                      flagc[:, 0:1], None, ALU.mult), r=["Sst", "flagc"], w=["Sst"])
                return
            S.dma("pool", lambda e: e.dma_start(out=Mj[:], in_=cst[:, 256:2304].rearrange("p (j t) -> p j t", j=4)), w=["Mj"])
            if ws:
                S.op("dve", lambda e: e.memset(w0f[:, 0:576], 0.0), w=["qpad", "kTn"])
            for hp in range(4):
                for ty in range(7):
                    c0 = TYPE_BASE[ty] + hp * 256
                    S.dma("pool", lambda e, ty=ty, c0=c0: e.dma_start(
                        out=W7[ty], in_=w_in[:, c0:c0 + 256].rearrange("(k p) c -> p k c", p=128)), w=[W7K[ty]])
                for cc in range(2):
                    h = hp * 2 + cc
                    hgrn_head(ws, h, cc)
                    sb_head(half, ws, h, cc)
            if ws:
                S.barrier()
                sample_attention()

        def make_bcast(dst, dkey, colfn, mult):
            for k in range(16):
                dg = diag[k % 2]
                S.op("dve", lambda e, dg=dg, k=k: e.tensor_scalar(dg[:], ident[:], colfn(k), mult, ALU.mult, ALU.mult),
                     r=["ident", "modc", "effs"], w=[("diag", k % 2)])
                S.op("pe", lambda e, dg=dg, k=k: e.matmul(
                    ps[:, 4 + k // 4, (k % 4) * 128:(k % 4 + 1) * 128], ones_f[:], dg[:], start=True, stop=True),
                    r=[("diag", k % 2), "ones_f"], w=[("ps", 4 + k // 4)])
            for b_ in range(4):
                S.op("act", lambda e, b_=b_: e.activation(dst[:, b_ * 512:(b_ + 1) * 512], ps[:, 4 + b_, :], AF.Copy),
                     r=[("ps", 4 + b_)], w=[dkey])

        def wout_residual(ws):
            tbs = tb_list(ws)
            for cg in range(4):
                pair = (wgb, "wgb") if cg % 2 == 0 else (wub, "wub")
                for hf in range(2):
                    S.dma("pool", lambda e, pair=pair, hf=hf, cg=cg: e.dma_start(
                        out=pair[0][hf][:].rearrange("p k c -> p (k c)").rearrange("p (k c) -> p k c", k=8),
                        in_=w_out[hf * 1024:(hf + 1) * 1024, cg * 512:(cg + 1) * 512].rearrange("(k p) c -> p k c", p=128)),
                        w=[(pair[1], hf)])
                for bi_, (tb, n) in enumerate(tbs):
                    bank = 4 + (cg * len(tbs) + bi_) % 2
                    for k in range(16):
                        wv_ = pair[0][k // 8][:].rearrange("p k c -> p (k c)").rearrange("p (k c) -> p k c", k=8)
                        S.op("pe", lambda e, k=k, tb=tb, n=n, bank=bank, wv_=wv_: e.matmul(
                            ps[0:n, bank, :], hT[:, k, tb * 128:tb * 128 + n], wv_[:, k % 8, :],
                            start=(k == 0), stop=(k == 15)), r=[("hT", tb), (pair[1], k // 8)], w=[("ps", bank)])
                    tmp = dtmp[bank - 4]
                    gt, gk = (gP, "gP") if tb < NTB else (gS, "gS")
                    S.op("dve", lambda e, tmp=tmp, n=n, bank=bank, gt=gt, cg=cg: e.tensor_tensor(
                        tmp[0:n, :], ps[0:n, bank, :], gt[0:n, cg * 512:(cg + 1) * 512], ALU.mult),
                        r=[("ps", bank), gk], w=[("dtmp", bank)])
                    xo = xrow(tb, n)[:, cg * 512:(cg + 1) * 512]
                    S.op("dve", lambda e, tmp=tmp, n=n, xo=xo: e.tensor_tensor(xo, xo, tmp[0:n, :], ALU.add),
                         r=[("dtmp", bank), ("x", tb)], w=[("x", tb)])

        def final_out(half, ws):
            for r_ in ((0, 1) if ws else (0,)):
                make_bcast(gP, "gP", lambda k, r_=r_: effs[:, 3, k, r_:r_ + 1], 1.0)
                make_bcast(gS, "gS", lambda k, r_=r_: modc[:, SH_OFF[3] + k, r_:r_ + 1], 1.0)
                for tb, n in (tb_list(False) if r_ == 0 else [(NTB, 8)]):
                    xin = xrow(tb, n)
                    S.op("act", lambda e, xin=xin, n=n, tb=tb: e.activation(
                        xn[0:n, :], xin, AF.Square, accum_out=ssq[0:n, tb:tb + 1]), r=[("x", tb)], w=["xn", ("ssq", tb)])
                    S.op("act", lambda e, n=n, tb=tb: e.activation(
                        rstd[0:n, tb:tb + 1], ssq[0:n, tb:tb + 1], AF.Sqrt, bias=EPS, scale=1.0 / D),
                        r=[("ssq", tb)], w=[("rstd", tb)])
                    S.op("dve", lambda e, n=n, tb=tb: e.reciprocal(rstd[0:n, tb:tb + 1], rstd[0:n, tb:tb + 1]),
                         r=[("rstd", tb)], w=[("rstd", tb)])
                    S.op("dve", lambda e, xin=xin, n=n, tb=tb: e.scalar_tensor_tensor(
                        xn[0:n, :], xin, rstd[0:n, tb:tb + 1], gP[0:n, :], ALU.mult, ALU.mult),
                        r=[("x", tb), ("rstd", tb), "gP"], w=["xn"])
                    S.op("dve", lambda e, n=n: e.tensor_tensor(xn[0:n, :], xn[0:n, :], gS[0:n, :], ALU.add), r=["xn", "gS"], w=["xn"])
                    dst = y_p[tb * 128:(tb + 1) * 128, :] if tb < NTB else y_s
                    S.dma("sp", lambda e, dst=dst, n=n: e.dma_start(out=dst, in_=xn[0:n, :]), r=["xn"])

        for half in range(2):
            ws = (half == 1)
            for tb in range(NTB):
                S.dma("sp", lambda e, tb=tb, half=half: e.dma_start(
                    out=x_tok[:, tb, :], in_=x_all[half * T + tb * 128:half * T + (tb + 1) * 128, :]), w=[("x", tb)])
            if ws:
                S.dma("sp", lambda e: e.dma_start(out=xS[0:8, :], in_=x_smp), w=[("x", NTB)])
            make_gate(0, 0.5)
            norm_to_hT(0, ws)
            ffn(w1g, w1u, w1d, ws)
            if half == 0:
                finish_mod()
            norm_to_hT(1, ws)
            if half == 0:
                S.barrier()
                mixer(0, False)
                S.barrier()
                continue
            kv_proj(half, ws)
            for tb in range(NTB):
                S.dma("sp", lambda e, tb=tb: e.dma_start(out=xsp[tb * 128:(tb + 1) * 128, :], in_=x_tok[:, tb, :]), r=[("x", tb)], w=[("xsp", tb)])
            S.barrier()
            mixer(half, ws)
            S.barrier()
            for k in range(16):
                S.op("dve" if k % 2 == 0 else "act",
                     (lambda e, k=k: e.tensor_copy(hT[:, k, :], oT[:, k, :])) if k % 2 == 0 else
                     (lambda e, k=k: e.activation(hT[:, k, :], oT[:, k, :], AF.Copy)), r=["oT"], w=["hTall"])
            S.barrier()
            for tb in range(NTB):
                S.dma("sp", lambda e, tb=tb: e.dma_start(out=x_tok[:, tb, :], in_=xsp[tb * 128:(tb + 1) * 128, :]), w=[("x", tb)])
            make_gate(1, 1.0)
            wout_residual(ws)
            make_gate(2, 0.5)
            norm_to_hT(2, ws)
            ffn(w2g, w2u, w2d, ws)
            final_out(half, ws)
            S.barrier()
        S.dma("sp", lambda e: e.dma_start(out=s_p.rearrange("h k v -> k h v"), in_=Sst[:]), r=["Sst"])
        S.dma("sp", lambda e: e.dma_start(out=s_s.rearrange("h k v -> k h v"), in_=Sss[:]), r=["Sss"])
        S.final_waits("sp", S.all_dma_tokens())

        with nc.Block() as block:
            S.emit(block)
    return nc


def _prep_vecs(inp, b, i):
    v = np.zeros((384, 128), np.float32)
    v[R_CP:R_CP + 16] = inp["c_prompt"][b].reshape(16, 128)
    v[R_CS:R_CS + 16] = inp["c_sample"][i].reshape(16, 128)
    v[R_N1:R_N1 + 16] = inp["norm_ffn1"][0].reshape(16, 128)
    v[R_N2:R_N2 + 16] = inp["norm_mix"][0].reshape(16, 128)
    v[R_N3:R_N3 + 16] = inp["norm_ffn2"][0].reshape(16, 128)
    v[R_NF:R_NF + 16] = inp["norm_final"].reshape(16, 128)
    v[R_LB:R_LB + 8] = inp["lb_logits"][0].reshape(8, 128)
    v[R_GA:R_GA + 8] = inp["g_out_a"][0].reshape(8, 128)
    v[R_GB:R_GB + 8] = inp["g_out_b"][0].reshape(8, 128)
    v[R_BM:R_BM + 144] = inp["b_mod"][0].reshape(144, 128)
    v[R_BM + 144:R_BM + 176] = inp["b_final_mod"].reshape(32, 128)
    return v


def _consts():
    c = np.zeros((128, 2304), np.float32)
    p = np.arange(128)[:, None]
    f = np.arange(128)[None, :]
    c[:, 0:128] = -((p >= f).astype(np.float32))
    c[:, 128:256] = (p <= f)
    t = np.arange(512)[None, :]
    for j in range(4):
        c[:, 256 + j * 512:256 + (j + 1) * 512] = (t > 128 * j + p)
    return c


def _consts2():
    c = np.zeros((64, 1032), np.float32)
    p = np.arange(64)[:, None]
    c[:, 0:8] = (np.arange(8)[None, :] < (p % 8))
    c[:, 8:1032] = ((np.arange(1024)[None, :] // 128) == (p // 8))
    return c


_NC_CACHE = {}


def kernel(**inp):
    inp = {k: np.asarray(v) for k, v in inp.items()}
    lim = int(os.environ.get("MK_STAGE", "99"))
    if lim not in _NC_CACHE:
        _NC_CACHE[lim] = build_program(lim)
    nc = _NC_CACHE[lim]
    in_maps = []
    for i in range(NCORES):
        b = i // 2
        m = {
            "x_all": np.ascontiguousarray(np.concatenate([inp["x_prompt"][b, 0:T], inp["x_prompt"][b, (i % 2) * T:(i % 2 + 1) * T]], axis=0)),
            "flag": np.full((1, 1), float(i % 2), np.float32),
            "x_smp": np.ascontiguousarray(inp["x_sample"][i]),
            "vecs": _prep_vecs(inp, b, i),
            "w_mod": inp["w_mod"][0], "w_fin": inp["w_final_mod"],
            "w1g": inp["w_ffn1_gate"][0], "w1u": inp["w_ffn1_up"][0], "w1d": inp["w_ffn1_down"][0],
            "w2g": inp["w_ffn2_gate"][0], "w2u": inp["w_ffn2_up"][0], "w2d": inp["w_ffn2_down"][0],
            "w_in": inp["w_in"][0], "w_out": inp["w_out"][0],
            "st_in": np.ascontiguousarray(inp["state_hgrn"][0, i]),
            "cst": _consts(), "b_sb": np.ascontiguousarray(inp["b_sb"].reshape(1, 8)),
            "cache_k": inp["cache_k"].reshape(NPHYS * 128, 1024), "cache_v": inp["cache_v"].reshape(NPHYS * 128, 1024),
            "ptab": np.ascontiguousarray(inp["page_table"][i].reshape(1, NPAGES)).astype(np.int32),
            "b64": np.ascontiguousarray(np.repeat(inp["b_sb"].reshape(8), 8).reshape(64, 1)),
            "cst2": _consts2(),
        }
        in_maps.append(m)
    res = run_bass_kernel_spmd(nc, in_maps, core_ids=list(range(NCORES))).results
    f32 = np.float32
    y_prompt = np.stack([np.concatenate([res[2 * b]["y_p"], res[2 * b + 1]["y_p"]], axis=0) for b in range(4)]).astype(f32)
    y_sample = np.stack([res[i]["y_s"] for i in range(8)]).astype(f32)
    k_prompt = np.stack([np.concatenate([res[2 * b]["k_p"], res[2 * b + 1]["k_p"]], axis=0).reshape(2 * T, 8, 128)
                         for b in range(4)])[None].astype(f32)
    v_prompt = np.stack([np.concatenate([res[2 * b]["v_p"], res[2 * b + 1]["v_p"]], axis=0).reshape(2 * T, 8, 128)
                         for b in range(4)])[None].astype(f32)
    k_sample = np.stack([res[i]["k_s"].reshape(8, 8, 128) for i in range(8)])[None].astype(f32)
    v_sample = np.stack([res[i]["v_s"].reshape(8, 8, 128) for i in range(8)])[None].astype(f32)
    s_prompt = np.stack([res[2 * b + 1]["s_p"] for b in range(4)])[None].astype(f32)
    s_sample = np.stack([res[i]["s_s"] for i in range(8)])[None].astype(f32)
    return (y_prompt, y_sample, k_prompt, v_prompt, k_sample, v_sample, s_prompt, s_sample)
```
